# Optimizing a Trainium2 kernel written in Bass

```python
import math
import jax, jax.numpy as jnp
from jax import lax
import numpy as np

D_MODEL = 2048
BATCH = 4
SEQ = 2048
DEPTH = 1
DEC_BATCH = 128
DEC_SEQ = 1
PAST_LEN = 16384
PAGE_SIZE = 128

D_MIX = D_MODEL
RG_WIDTH = D_MIX // 2
RG_BLOCKS = 8
RG_BLOCK = RG_WIDTH // RG_BLOCKS
CONV_W = 4
RG_C = 8.0
ML_HEADS = 4
ML_WIDTH = D_MIX - RG_WIDTH
ML_DV = ML_WIDTH // ML_HEADS
ML_DK = ML_DV // 2
ML_CHUNK = 64
N_MEM = 256
XA_HEADS = 4
XA_DH = D_MODEL // XA_HEADS
D_FF = -(-8 * D_MODEL // (3 * 256)) * 256
EPS = 1e-6
NEG = -1e30

OFF_RGX = 0
OFF_RGG = OFF_RGX + RG_WIDTH
OFF_Q = OFF_RGG + RG_WIDTH
OFF_K = OFF_Q + ML_HEADS * ML_DK
OFF_V = OFF_K + ML_HEADS * ML_DK
OFF_O = OFF_V + ML_WIDTH
OFF_I = OFF_O + ML_WIDTH
OFF_F = OFF_I + ML_HEADS
IN_W = OFF_F + ML_HEADS

kernel_name = "hymba_rglru_mlstm_memxattn_decode_step"


def rmsnorm(x, g):
    xf = x.astype(jnp.float32)
    y = xf * lax.rsqrt(jnp.mean(xf * xf, axis=-1, keepdims=True) + EPS)
    return (y * g.astype(jnp.float32)).astype(x.dtype)


def causal_conv(x, buf, w, b):
    L = x.shape[1]
    xp = jnp.concatenate([buf.astype(x.dtype), x], axis=1)
    y = b + sum(xp[:, j:j + L] * w[j] for j in range(CONV_W))
    return y, xp[:, L:]


def block_diag(x, w, b):
    B, L, _ = x.shape
    xb = x.reshape(B, L, RG_BLOCKS, RG_BLOCK)
    return jnp.einsum('blnc,ncd->blnd', xb, w).reshape(B, L, RG_WIDTH) + b


def rglru(xr, r, i, lam, h0):
    log_a = -RG_C * r * jax.nn.softplus(-lam)
    a = jnp.exp(log_a)
    mult = jnp.sqrt(jnp.maximum(1.0 - jnp.exp(2.0 * log_a), 0.0))
    bx = mult * (i * xr)
    bx = bx.at[:, 0].add(a[:, 0] * h0)

    def comb(c1, c2):
        a1, b1 = c1
        a2, b2 = c2
        return a1 * a2, a2 * b1 + b2

    _, h = lax.associative_scan(comb, (a, bx), axis=1)
    return h, h[:, -1]


def mlstm(q, k, v, logi, logf, C0, n0, m0):
    B, L = q.shape[0], q.shape[1]
    cs = min(ML_CHUNK, L)
    pad = (-L) % cs
    nc = (L + pad) // cs

    def prep(t, fill):
        t = jnp.moveaxis(t.astype(jnp.float32), 2, 1)
        widths = [(0, 0), (0, 0), (0, pad)] + [(0, 0)] * (t.ndim - 3)
        t = jnp.pad(t, widths, constant_values=fill)
        t = t.reshape(t.shape[:2] + (nc, cs) + t.shape[3:])
        return jnp.moveaxis(t, 2, 0)

    qs, ks, vs = prep(q, 0.0), prep(k, 0.0), prep(v, 0.0)
    lis, lfs = prep(logi, NEG), prep(logf, 0.0)
    causal = jnp.tril(jnp.ones((cs, cs), dtype=bool))

    def step(carry, inp):
        C, n, m = carry
        qc, kc, vc, li, lf = inp
        bcum = jnp.cumsum(lf, axis=-1)
        logD = bcum[..., :, None] - bcum[..., None, :] + li[..., None, :]
        logD = jnp.where(causal, logD, NEG)
        inter = bcum + m[..., None]
        m_t = jnp.maximum(inter, jnp.max(logD, axis=-1))
        D = jnp.exp(logD - m_t[..., None])
        sc = jnp.exp(inter - m_t)
        qk = jnp.einsum('bhtd,bhsd->bhts', qc, kc) * D
        num = sc[..., None] * jnp.einsum('bhtd,bhdv->bhtv', qc, C) + jnp.einsum('bhts,bhsv->bhtv', qk, vc)
        den = sc * jnp.einsum('bhtd,bhd->bht', qc, n) + jnp.sum(qk, axis=-1)
        den = jnp.maximum(jnp.abs(den), jnp.exp(-m_t))
        h = num / den[..., None]
        m_new = m_t[..., -1]
        w_end = jnp.exp(bcum[..., -1:] - bcum + li - m_new[..., None])
        dec = jnp.exp(bcum[..., -1] + m - m_new)
        C_new = dec[..., None, None] * C + jnp.einsum('bhs,bhsd,bhsv->bhdv', w_end, kc, vc)
        n_new = dec[..., None] * n + jnp.einsum('bhs,bhsd->bhd', w_end, kc)
        return (C_new, n_new, m_new), h

    carry0 = (C0.astype(jnp.float32), n0.astype(jnp.float32), m0.astype(jnp.float32))
    (C, n, m), hs = lax.scan(step, carry0, (qs, ks, vs, lis, lfs))
    hs = jnp.moveaxis(hs, 0, 2).reshape(B, ML_HEADS, nc * cs, ML_DV)[:, :, :L]
    return jnp.moveaxis(hs, 1, 2), C, n, m


def mem_kv(mem, g_mem, w_k, w_v):
    B = mem.shape[0]
    mn = rmsnorm(mem, g_mem)
    mk = (mn @ w_k).reshape(B, N_MEM, XA_HEADS, XA_DH)
    mv = (mn @ w_v).reshape(B, N_MEM, XA_HEADS, XA_DH)
    return mk, mv


def layer(x, conv_buf, h0, C0, n0, m0, mk, mv,
          g_mix, w_in, conv_w, conv_b, w_rg_a, b_rg_a, w_rg_x, b_rg_x, rg_lambda,
          b_ml_i, b_ml_f, g_rg_out, g_ml_out, w_out,
          g_xa, w_xa_q, w_xa_o, g_ffn, w_ffn_gate, w_ffn_up, w_ffn_down):
    B, L, _ = x.shape
    dt = x.dtype
    xn = rmsnorm(x, g_mix)
    z = xn @ w_in
    xr, conv_new = causal_conv(z[..., OFF_RGX:OFF_RGG], conv_buf, conv_w, conv_b)
    r = jax.nn.sigmoid(block_diag(xr, w_rg_a, b_rg_a).astype(jnp.float32))
    ig = jax.nn.sigmoid(block_diag(xr, w_rg_x, b_rg_x).astype(jnp.float32))
    h_rg, h_last = rglru(xr.astype(jnp.float32), r, ig, rg_lambda.astype(jnp.float32),
                         h0.astype(jnp.float32))
    y_rg = rmsnorm((h_rg.astype(dt) * jax.nn.gelu(z[..., OFF_RGG:OFF_Q])), g_rg_out)
    q = z[..., OFF_Q:OFF_K].reshape(B, L, ML_HEADS, ML_DK)
    k = z[..., OFF_K:OFF_V].reshape(B, L, ML_HEADS, ML_DK) * (ML_DK ** -0.5)
    v = z[..., OFF_V:OFF_O].reshape(B, L, ML_HEADS, ML_DV)
    logi = (z[..., OFF_I:OFF_F] + b_ml_i).astype(jnp.float32)
    logf = jax.nn.log_sigmoid((z[..., OFF_F:IN_W] + b_ml_f).astype(jnp.float32))
    h_ml, C_new, n_new, m_new = mlstm(q, k, v, logi, logf, C0, n0, m0)
    h_ml = rmsnorm(h_ml.astype(dt), g_ml_out.reshape(ML_HEADS, ML_DV)).reshape(B, L, ML_WIDTH)
    y_ml = h_ml * jax.nn.sigmoid(z[..., OFF_O:OFF_I])
    x = x + jnp.concatenate([y_rg, y_ml], axis=-1) @ w_out
    xq = (rmsnorm(x, g_xa) @ w_xa_q).reshape(B, L, XA_HEADS, XA_DH)
    s = jnp.einsum('blhd,bmhd->bhlm', xq, mk.astype(dt)).astype(jnp.float32) * (XA_DH ** -0.5)
    p = jax.nn.softmax(s, axis=-1).astype(dt)
    o = jnp.einsum('bhlm,bmhd->blhd', p, mv.astype(dt)).reshape(B, L, D_MODEL)
    x = x + o @ w_xa_o
    xf = rmsnorm(x, g_ffn)
    x = x + (jax.nn.silu(xf @ w_ffn_gate) * (xf @ w_ffn_up)) @ w_ffn_down
    return x, (h_last.astype(dt), conv_new.astype(dt), C_new.astype(dt), n_new.astype(dt), m_new.astype(dt))


def setup_inputs(seed: int = 0) -> dict:
    key = jax.random.key(seed)
    ks = jax.random.split(key, 40)
    f32 = jnp.float32
    nrm = lambda i, shape, s: jax.random.normal(ks[i], shape, f32) * s
    gain = lambda i, shape: 1.0 + 0.05 * jax.random.normal(ks[i], shape, f32)
    a0 = jax.random.uniform(ks[10], (DEPTH, RG_WIDTH), f32, 0.9, 0.999)
    return {
        "x_prompt": nrm(0, (BATCH, SEQ, D_MODEL), 1.0),
        "x_sample": nrm(1, (DEC_BATCH, DEC_SEQ, D_MODEL), 1.0),
        "mem_prompt": nrm(2, (BATCH, N_MEM, D_MODEL), 1.0),
        "state_rg_h": nrm(3, (DEPTH, DEC_BATCH, RG_WIDTH), 0.5),
        "state_rg_conv": nrm(4, (DEPTH, DEC_BATCH, CONV_W - 1, RG_WIDTH), 1.0),
        "state_ml_C": nrm(5, (DEPTH, DEC_BATCH, ML_HEADS, ML_DK, ML_DV), 0.1),
        "state_ml_n": nrm(6, (DEPTH, DEC_BATCH, ML_HEADS, ML_DK), 0.1),
        "state_ml_m": nrm(7, (DEPTH, DEC_BATCH, ML_HEADS), 1.0),
        "cache_mem_k": nrm(8, (DEPTH, DEC_BATCH, N_MEM, XA_HEADS, XA_DH), 1.0),
        "cache_mem_v": nrm(9, (DEPTH, DEC_BATCH, N_MEM, XA_HEADS, XA_DH), 1.0),
        "g_mix": gain(11, (DEPTH, D_MODEL)),
        "w_in": nrm(12, (DEPTH, D_MODEL, IN_W), D_MODEL ** -0.5),
        "conv_w": nrm(13, (DEPTH, CONV_W, RG_WIDTH), CONV_W ** -0.5),
        "conv_b": nrm(14, (DEPTH, RG_WIDTH), 0.02),
        "w_rg_a": nrm(15, (DEPTH, RG_BLOCKS, RG_BLOCK, RG_BLOCK), RG_BLOCK ** -0.5),
        "b_rg_a": nrm(16, (DEPTH, RG_WIDTH), 0.02),
        "w_rg_x": nrm(17, (DEPTH, RG_BLOCKS, RG_BLOCK, RG_BLOCK), RG_BLOCK ** -0.5),
        "b_rg_x": nrm(18, (DEPTH, RG_WIDTH), 0.02),
        "rg_lambda": jnp.log(a0) - jnp.log1p(-a0),
        "b_ml_i": nrm(19, (DEPTH, ML_HEADS), 0.1),
        "b_ml_f": 3.0 + jax.random.uniform(ks[20], (DEPTH, ML_HEADS), f32, 0.0, 3.0),
        "g_rg_out": gain(21, (DEPTH, RG_WIDTH)),
        "g_ml_out": gain(22, (DEPTH, ML_WIDTH)),
        "w_out": nrm(23, (DEPTH, D_MIX, D_MODEL), D_MIX ** -0.5),
        "g_xa": gain(24, (DEPTH, D_MODEL)),
        "g_mem": gain(25, (DEPTH, D_MODEL)),
        "w_xa_q": nrm(26, (DEPTH, D_MODEL, D_MODEL), D_MODEL ** -0.5),
        "w_xa_k": nrm(27, (DEPTH, D_MODEL, D_MODEL), D_MODEL ** -0.5),
        "w_xa_v": nrm(28, (DEPTH, D_MODEL, D_MODEL), D_MODEL ** -0.5),
        "w_xa_o": nrm(29, (DEPTH, D_MODEL, D_MODEL), D_MODEL ** -0.5),
        "g_ffn": gain(30, (DEPTH, D_MODEL)),
        "w_ffn_gate": nrm(31, (DEPTH, D_MODEL, D_FF), D_MODEL ** -0.5),
        "w_ffn_up": nrm(32, (DEPTH, D_MODEL, D_FF), D_MODEL ** -0.5),
        "w_ffn_down": nrm(33, (DEPTH, D_FF, D_MODEL), D_FF ** -0.5),
        "g_final": gain(34, (D_MODEL,)),
    }


def reference(x_prompt, x_sample, mem_prompt, state_rg_h, state_rg_conv, state_ml_C, state_ml_n,
              state_ml_m, cache_mem_k, cache_mem_v,
              g_mix, w_in, conv_w, conv_b, w_rg_a, b_rg_a, w_rg_x, b_rg_x, rg_lambda,
              b_ml_i, b_ml_f, g_rg_out, g_ml_out, w_out,
              g_xa, g_mem, w_xa_q, w_xa_k, w_xa_v, w_xa_o,
              g_ffn, w_ffn_gate, w_ffn_up, w_ffn_down, g_final):
    dt = x_prompt.dtype
    xp, xs = x_prompt, x_sample
    Bp = xp.shape[0]
    p_st, s_st, p_mk, p_mv = [], [], [], []
    for l in range(DEPTH):
        w = (g_mix[l], w_in[l], conv_w[l], conv_b[l], w_rg_a[l], b_rg_a[l], w_rg_x[l], b_rg_x[l],
             rg_lambda[l], b_ml_i[l], b_ml_f[l], g_rg_out[l], g_ml_out[l], w_out[l],
             g_xa[l], w_xa_q[l], w_xa_o[l], g_ffn[l], w_ffn_gate[l], w_ffn_up[l], w_ffn_down[l])
        mk, mv = mem_kv(mem_prompt, g_mem[l], w_xa_k[l], w_xa_v[l])
        xp, st_p = layer(xp,
                         jnp.zeros((Bp, CONV_W - 1, RG_WIDTH), dt),
                         jnp.zeros((Bp, RG_WIDTH), dt),
                         jnp.zeros((Bp, ML_HEADS, ML_DK, ML_DV), dt),
                         jnp.zeros((Bp, ML_HEADS, ML_DK), dt),
                         jnp.zeros((Bp, ML_HEADS), dt),
                         mk, mv, *w)
        p_st.append(st_p)
        p_mk.append(mk)
        p_mv.append(mv)
        xs, st_s = layer(xs, state_rg_conv[l], state_rg_h[l], state_ml_C[l], state_ml_n[l],
                         state_ml_m[l], cache_mem_k[l], cache_mem_v[l], *w)
        s_st.append(st_s)
    y_prompt = rmsnorm(xp, g_final)
    y_sample = rmsnorm(xs, g_final)
    stk = lambda lst, j: jnp.stack([s[j] for s in lst], axis=0)
    return (y_prompt, y_sample,
            stk(p_st, 0), stk(p_st, 1), stk(p_st, 2), stk(p_st, 3), stk(p_st, 4),
            jnp.stack(p_mk, axis=0), jnp.stack(p_mv, axis=0),
            stk(s_st, 0), stk(s_st, 1), stk(s_st, 2), stk(s_st, 3), stk(s_st, 4))
```

```python
import os
import numpy as np
import concourse.bass as bass
import concourse.mybir as mybir
from concourse.bass_utils import run_bass_kernel_spmd
from contextlib import ExitStack

F32 = mybir.dt.float32
BF16 = mybir.dt.bfloat16
AF = mybir.ActivationFunctionType
ALU = mybir.AluOpType
AX = mybir.AxisListType

COMPUTE = ("pe", "act", "dve", "pool")
ALL_ENG = ("pe", "act", "dve", "pool", "sp")

D = 2048
NT = 1024
NS = 16
IN_W = 5128
DFF = 5632
EPS = 1e-6
OFF_Q, OFF_K, OFF_V, OFF_O, OFF_I = 2048, 2560, 3072, 4096, 5120


class _Stop(Exception):
    pass


def _on(tag):
    return tag not in os.environ.get("KS_OFF", "")


def _stop(tag):
    if os.environ.get("KSTOP", "ALL") == tag:
        raise _Stop()


class Op:
    __slots__ = ("eng", "fn", "deps", "sig", "dkey", "dval", "waits", "has_dep", "idx")


class Prog:
    def __init__(self, nc, stack):
        self.nc = nc
        self.stack = stack
        self.ops = []
        self.lw = {}
        self.rd = {}
        self.dma_cum = {}
        self.dma_sems = {}
        self.nbar = 0

    def add(self, eng, fn, reads=(), writes=(), dkey=None):
        i = len(self.ops)
        psr = [r for r in reads if isinstance(r, str) and r.startswith("ps") and r[2:].isdigit()]
        if psr:
            reads = [r for r in reads if r not in psr]
            writes = list(writes) + psr
        deps = set()
        for r in reads:
            w = self.lw.get(r)
            if w is not None:
                deps.add(w)
        for r in writes:
            w = self.lw.get(r)
            if w is not None:
                deps.add(w)
            rr = self.rd.get(r)
            if rr:
                deps.update(rr.values())
        op = Op()
        op.eng, op.fn, op.deps, op.sig, op.dkey, op.dval, op.has_dep, op.idx = eng, fn, deps, 0, dkey, 0, False, i
        if dkey is not None:
            self.dma_cum[dkey] = self.dma_cum.get(dkey, 0) + 16
            op.dval = self.dma_cum[dkey]
        rkey = eng if dkey is None else ("dma", dkey)
        for r in reads:
            self.rd.setdefault(r, {})[rkey] = i
        for r in writes:
            self.lw[r] = i
            self.rd[r] = {}
        self.ops.append(op)
        return i

    def pe(self, fn, reads=(), writes=()):
        return self.add("pe", fn, reads, writes)

    def act(self, fn, reads=(), writes=()):
        return self.add("act", fn, reads, writes)

    def dve(self, fn, reads=(), writes=()):
        return self.add("dve", fn, reads, writes)

    def dma(self, q, key, out, in_, reads=(), writes=()):
        return self.add(q, lambda e: e.dma_start(out=out, in_=in_), reads, writes, dkey=key)

    def barrier(self, bar_tiles):
        n = self.nbar
        self.nbar += 1
        engs = ("pe", "act", "dve", "sp")
        def tiny(e):
            if e == "dve":
                t = bar_tiles[e]
                return lambda en, t=t: en.memset(t, 0.0)
            if e == "act":
                t, t2 = bar_tiles[e], bar_tiles["act_src"]
                return lambda en, t=t, t2=t2: en.copy(out=t, in_=t2)
            if e == "pe":
                pt_, o32 = bar_tiles["pe_ps"], bar_tiles["ones32"]
                return lambda en: en.matmul(pt_, lhsT=o32, rhs=o32, start=True, stop=True)
            return lambda en: en.nop()
        pkey = bar_tiles["pe_key"]
        extra = {"pe": ([], [pkey]), "act": (["bart_src"], ["bart_act"]), "dve": ([], ["bart_dve"]), "sp": ([], [])}
        for e in engs:
            self.add(e, tiny(e), reads=extra[e][0], writes=["bar%d_%s" % (n, e)] + extra[e][1])
        for e in engs:
            self.add(e, tiny(e), reads=["bar%d_%s" % (n, x) for x in engs] + extra[e][0], writes=["bar%d_%s_b" % (n, e)] + extra[e][1])

    def emit(self):
        nc = self.nc
        ops = self.ops
        for op in ops:
            nd = set()
            for d in op.deps:
                p = ops[d]
                if p.dkey is None and op.dkey is None and p.eng == op.eng and p.eng == "pe":
                    continue
                nd.add(d)
                p.has_dep = True
            op.deps = nd
        cnt = {e: 0 for e in ALL_ENG}
        for op in ops:
            if op.dkey is None and op.has_dep:
                cnt[op.eng] += 1
                op.sig = cnt[op.eng]
        sems = {e: self.stack.enter_context(nc.semaphore("s_" + e)) for e in ALL_ENG}
        for k in self.dma_cum:
            self.dma_sems[k] = self.stack.enter_context(nc.semaphore("d_" + str(k)))
        known = {e: {} for e in ALL_ENG}
        for op in ops:
            need = {}
            for d in op.deps:
                p = ops[d]
                if p.dkey is not None:
                    sk, v = ("d", p.dkey), p.dval
                else:
                    sk, v = ("c", p.eng), p.sig
                if v > need.get(sk, 0):
                    need[sk] = v
            kn = known[op.eng]
            waits = []
            for sk, v in need.items():
                if kn.get(sk, 0) >= v:
                    continue
                kn[sk] = v
                waits.append((sk, v))
            op.waits = waits
        self.n_waits = sum(len(o.waits) for o in ops)

        def emit_engine(ename, eobj):
            for op in ops:
                if op.eng != ename:
                    continue
                for sk, v in op.waits:
                    s = self.dma_sems[sk[1]] if sk[0] == "d" else sems[sk[1]]
                    eobj.wait_ge(s, v)
                ins = op.fn(eobj)
                if op.dkey is not None:
                    ins.then_inc(self.dma_sems[op.dkey], 16)
                elif op.sig:
                    ins.then_inc(sems[op.eng], 1)
            if ename == "sp":
                for k, v in self.dma_cum.items():
                    eobj.wait_ge(self.dma_sems[k], v)

        with nc.Block() as block:
            @block.tensor
            def _(e):
                emit_engine("pe", e)

            @block.scalar
            def _(e):
                emit_engine("act", e)

            @block.vector
            def _(e):
                emit_engine("dve", e)

            @block.gpsimd
            def _(e):
                emit_engine("pool", e)

            @block.sync
            def _(e):
                emit_engine("sp", e)


class Reg:
    def __init__(self, t, nwords):
        self.t, self.n, self.off = t, nwords, 0

    def reset(self):
        self.off = 0

    def _take(self, w):
        assert self.off + w <= self.n, ("region overflow", self.off, w, self.n)
        ap = self.t[:, self.off:self.off + w]
        self.off += w
        return ap

    @staticmethod
    def _shape(ap, shape):
        if len(shape) == 1:
            return ap
        if len(shape) == 2:
            return ap.rearrange("p (a b) -> p a b", a=shape[0])
        if len(shape) == 3:
            return ap.rearrange("p (a b c) -> p a b c", a=shape[0], b=shape[1])
        raise ValueError(shape)

    def f32(self, *shape):
        n = int(np.prod(shape))
        return self._shape(self._take(n), shape)

    def bf16(self, *shape):
        n = int(np.prod(shape))
        w = (n + 1) // 2
        return self._shape(self._take(w).bitcast(BF16)[:, 0:n], shape)


def build(do_sample=True):
    nc = bass.Bass("TRN2", target_bir_lowering=False)

    def din(name, shape):
        return nc.dram_tensor(name, list(shape), F32, kind="ExternalInput").ap()

    def dout(name, shape):
        return nc.dram_tensor(name, list(shape), F32, kind="ExternalOutput").ap()

    xo = din("xo", [NT, D]); xp = din("xp", [NT, D]); xs_in = din("xs", [NS, D])
    mask_in = din("mask", [128, 1]); mem = din("mem", [256, D])
    st_h = din("st_h", [NS, 1024]); st_conv = din("st_conv", [NS, 3072])
    st_C = din("st_C", [NS, 4, 128, 256]); st_n = din("st_n", [NS, 512]); st_m = din("st_m", [NS, 4])
    ck = din("ck", [NS, 256, D]); cv = din("cv", [NS, 256, D])
    g_mix = din("g_mix", [D]); w_in = din("w_in", [D, IN_W]); conv_w = din("conv_w", [4, 1024])
    conv_b = din("conv_b", [1024]); w_rg_a = din("w_rg_a", [8, 128, 128]); b_rg_a = din("b_rg_a", [1024])
    w_rg_x = din("w_rg_x", [8, 128, 128]); b_rg_x = din("b_rg_x", [1024]); rg_lambda = din("rg_lambda", [1024])
    b_ml_i = din("b_ml_i", [4]); b_ml_f = din("b_ml_f", [4]); g_rg_out = din("g_rg_out", [1024])
    g_ml_out = din("g_ml_out", [1024]); w_out = din("w_out", [D, D]); g_xa = din("g_xa", [D]); g_mem = din("g_mem", [D])
    w_xa_q = din("w_xa_q", [D, D]); w_xa_k = din("w_xa_k", [D, D]); w_xa_v = din("w_xa_v", [D, D]); w_xa_o = din("w_xa_o", [D, D])
    g_ffn = din("g_ffn", [D]); w_g = din("w_ffn_gate", [D, DFF]); w_u = din("w_ffn_up", [D, DFF]); w_d = din("w_ffn_down", [DFF, D])
    g_final = din("g_final", [D])
    ident_in = din("ident", [128, 128]); causal_in = din("causal", [128, 128]); eyebc_in = din("eyebc", [128, 256])

    yo = dout("yo", [NT, D]); ys = dout("ys", [NS, D])
    o_ph = dout("o_ph", [128, 8]); o_pconv = dout("o_pconv", [128, 24]); o_pC = dout("o_pC", [128, 4 * 257]); o_pm = dout("o_pm", [4, 1])
    o_mk = dout("o_mk", [256, D]); o_mv = dout("o_mv", [256, D])
    o_sh = dout("o_sh", [NS, 1024]); o_sconv = dout("o_sconv", [NS, 3072]); o_sC = dout("o_sC", [NS, 4, 128, 256])
    o_sn = dout("o_sn", [NS, 512]); o_sm = dout("o_sm", [NS, 4])

    st = ExitStack()
    with st:
        P = Prog(nc, st)
        SB = lambda name, shape, dt=F32: st.enter_context(nc.sbuf_tensor("sb_" + name, list(shape), dt))
        A = SB("A", [128, 16, NT], BF16)
        B = SB("B", [128, 16, NT], BF16)
        RXt = SB("RX", [128, 16384], F32)
        RINGt = SB("RING", [128, 3, 8192], BF16)
        NSCR = 5132
        SCRt = SB("SCR", [128, NSCR], F32)
        ident32 = SB("ident32", [128, 128]); identb = SB("identb", [128, 128], BF16)
        causal = SB("causal", [128, 128]); eyebc = SB("eyebc", [128, 256])
        ones32 = SB("ones32", [128, 128]); onesb = SB("onesb", [128, 128], BF16)
        prm = SB("prm", [128, 72]); cl = SB("cl", [128, 8]); gfm = SB("gfm", [128, 64])
        maskt = SB("maskt", [128, 1])
        wgt = SB("wgt", [128, 8, 2, 128], BF16)
        wif = SB("wif", [128, 16, 8], BF16)
        bart = SB("bart", [128, 4])
        ph_sb = SB("ph_sb", [128, 8]); pconv_sb = SB("pconv_sb", [128, 8, 3]); pm_sb = SB("pm_sb", [4, 1])
        small = SB("small", [128, 64])
        PS = [st.enter_context(nc.psum_tensor("ps%d" % i, [128, 512], F32)) for i in range(8)]
        bar_tiles = {"act": bart[:, 0:1], "dve": bart[:, 1:2], "act_src": bart[:, 2:3]}
        P.dve(lambda e: e.memset(bart[:], 0.0), writes=["bart_act", "bart_dve", "bart_src", "bart3"])
        bar_tiles["pe_ps"] = PS[7][0:1, 511:512]
        bar_tiles["ones32"] = ones32[0:1, 0:1]
        bar_tiles["pe_key"] = "ps7"

        RX = Reg(RXt, 16384)
        NSMP = 776
        SMPp = Reg(SCRt[:, 0:NSMP], NSMP)
        SCR = Reg(SCRt[:, NSMP:NSCR], NSCR - NSMP)
        xrs = SMPp.f32(16, 16)
        Xs = SMPp.bf16(16, 16)
        Ys = SMPp.bf16(16, 16)
        zsT = SMPp.f32(8, 16); zgT = SMPp.f32(8, 16)
        lif_s = SMPp.f32(8)
        xrs2 = xrs.rearrange("p a b -> p (a b)")
        srs = small[:, 32:48]

        def norm_fm(gc0, sq3):
            sq2 = sq3.rearrange("p a b -> p (a b)")
            P.dve(lambda e: e.tensor_tensor(out=sq2, in0=xrs2, in1=xrs2, op=ALU.mult), reads=["xrs"], writes=["sq3"])
            pt, pk = newps()
            for kc in range(16):
                P.pe(lambda e, kc=kc, pt=pt: e.matmul(pt[:, 0:16], lhsT=ones32[:], rhs=sq3[:, kc, :], start=(kc == 0), stop=(kc == 15)), reads=["ones32", "sq3"], writes=[pk])
            P.act(lambda e, pt=pt: e.activation(out=srs, in_=pt[:, 0:16], func=AF.Sqrt, bias=EPS, scale=1.0 / D), reads=[pk], writes=["srs"])
            P.dve(lambda e: e.reciprocal(out=srs, in_=srs), reads=["srs"], writes=["srs"])
            P.dve(lambda e: e.tensor_tensor(out=sq3, in0=xrs, in1=srs.unsqueeze(1).to_broadcast([128, 16, 16]), op=ALU.mult), reads=["xrs", "srs", "sq3"], writes=["sq3"])
            P.dve(lambda e: e.tensor_tensor(out=Xs, in0=sq3, in1=gfm[:, gc0:gc0 + 16].unsqueeze(2).to_broadcast([128, 16, 16]), op=ALU.mult), reads=["sq3", "gfm"], writes=["Xs"])
        xres = RXt[:, :].rearrange("p (t n) -> p t n", t=8)

        ps_state = {"pool": list(range(8)), "i": 0}

        def newps():
            pool = ps_state["pool"]
            b = pool[ps_state["i"] % len(pool)]
            ps_state["i"] += 1
            return PS[b], "ps%d" % b

        ring_state = {"n": 0, "slots": [0, 1, 2]}

        def ring_unit(parts):
            sl_ = ring_state["slots"]
            s = sl_[ring_state["n"] % len(sl_)]
            ring_state["n"] += 1
            slot = RINGt[:, s, :]
            key = "ring%d" % s
            for dst_fn, src in parts:
                P.dma("pool", key, dst_fn(slot), src, writes=[key])
            return slot, key

        P.dma("sp", "c_ident", ident32[:], ident_in[:, :], writes=["ident32"])
        P.dma("sp", "c_causal", causal[:], causal_in[:, :], writes=["causal"])
        P.dma("sp", "c_eyebc", eyebc[:], eyebc_in[:, :], writes=["eyebc"])
        P.dma("sp", "c_mask", maskt[:], mask_in[:, :], writes=["mask"])
        P.dve(lambda e: e.tensor_copy(out=identb[:], in_=ident32[:]), reads=["ident32"], writes=["identb"])
        P.dve(lambda e: e.memset(ones32[:], 1.0), writes=["ones32"])
        P.dve(lambda e: e.memset(onesb[:], 1.0), writes=["onesb"])
        P.dma("pool", "c_wga", wgt[:, :, 0, :], w_rg_a.rearrange("c k d -> k c d"), writes=["wgt"])
        P.dma("pool", "c_wgx", wgt[:, :, 1, :], w_rg_x.rearrange("c k d -> k c d"), writes=["wgt"])
        P.dma("pool", "c_wif", wif[:], w_in[:, OFF_I:OFF_I + 8].rearrange("(k p) n -> p k n", p=128), writes=["wif"])
        SCR.reset()
        prow = RXt[:, 16128:16256]
        P.dma("sp", "c_pr0", prow[0:32, :], conv_w.rearrange("j (c p) -> (j c) p", p=128), writes=["prow"])
        for i, v in enumerate([conv_b, b_rg_a, b_rg_x, rg_lambda, g_rg_out]):
            P.dma("sp", "c_pr%d" % (i + 1), prow[32 + 8 * i:40 + 8 * i, :], v.rearrange("(c p) -> c p", p=128), writes=["prow"])
        pst, psk = newps()
        P.pe(lambda e: e.transpose(out=pst[:, 0:72], in_=prow[0:72, :], identity=ident32[0:72, 0:72]), reads=["prow", "ident32"], writes=[psk])
        P.dve(lambda e: e.tensor_copy(out=prm[:], in_=pst[:, 0:72]), reads=[psk], writes=["prm"])
        grow = RXt[:, 16256:16384]
        for i, v in enumerate([g_mix, g_xa, g_ffn, g_final]):
            P.dma("sp", "c_gr%d" % i, grow[16 * i:16 * i + 16, :], v.rearrange("(c p) -> c p", p=128), writes=["grow"])
        pst2, psk2 = newps()
        P.pe(lambda e: e.transpose(out=pst2[:, 0:64], in_=grow[0:64, :], identity=ident32[0:64, 0:64]), reads=["grow", "ident32"], writes=[psk2])
        P.dve(lambda e: e.tensor_copy(out=gfm[:], in_=pst2[:, 0:64]), reads=[psk2], writes=["gfm"])
        P.act(lambda e: e.activation(out=cl[:], in_=prm[:, 56:64], func=AF.Exp, scale=-1.0), reads=["prm"], writes=["cl"])
        P.act(lambda e: e.activation(out=cl[:], in_=cl[:], func=AF.Ln, bias=1.0, scale=1.0), reads=["cl"], writes=["cl"])
        P.dve(lambda e: e.tensor_scalar(out=cl[:], in0=cl[:], scalar1=-8.0, scalar2=None, op0=ALU.mult), reads=["cl"], writes=["cl"])

        def rstd_from_ss(ss, n, key):
            P.act(lambda e: e.activation(out=ss, in_=ss, func=AF.Sqrt, bias=EPS, scale=1.0 / n), reads=[key], writes=[key])
            P.dve(lambda e: e.reciprocal(out=ss, in_=ss), reads=[key], writes=[key])

        def norm_tile(src, srckey, gbc, gkey, dst3, dstkey, junk, xn_tm, ssap, sskey, rows=128):
            P.act(lambda e: e.activation(out=junk[0:rows, :], in_=src, func=AF.Square, accum_out=ssap[0:rows, :]), reads=[srckey], writes=["xn_tm", sskey])
            rstd_from_ss(ssap[0:rows, :], D, sskey)
            P.dve(lambda e: e.scalar_tensor_tensor(out=xn_tm[0:rows, :], in0=src, scalar=ssap[0:rows, :], in1=gbc[0:rows, :], op0=ALU.mult, op1=ALU.mult),
                  reads=[srckey, sskey, gkey], writes=["xn_tm"])
            for hf in range(2):
                pt, pk = newps()
                ptb = pt[:].bitcast(BF16)
                for k in range(8):
                    kk = hf * 8 + k
                    P.pe(lambda e, k=k, kk=kk, ptb=ptb: e.transpose(out=ptb[:, k * 128:k * 128 + rows], in_=xn_tm[0:rows, kk * 128:(kk + 1) * 128], identity=identb[0:rows, 0:rows]),
                         reads=["xn_tm", "identb"], writes=[pk])
                srcv = ptb[:, 0:1024].rearrange("p (k t) -> p k t", k=8)[:, :, 0:rows]
                if hf == 0:
                    P.act(lambda e, srcv=srcv: e.copy(out=dst3[:, 0:8, :], in_=srcv), reads=[pk], writes=[dstkey])
                else:
                    P.dve(lambda e, srcv=srcv: e.tensor_copy(out=dst3[:, 8:16, :], in_=srcv), reads=[pk], writes=[dstkey])

        def load_gbc(gvec, gbc, key):
            P.dma("sp", "gbc", gbc, gvec.partition_broadcast(128), writes=[key])

        try:
            RX.reset()
            PRE = RX.bf16(16, NT)
            SCR.reset()
            gbc = SCR.f32(2048)
            xn_tm = SCR.bf16(2048)
            junk = xn_tm
            xstage = [RX.f32(2048), RX.f32(2048)]
            load_gbc(g_mix, gbc, "gbc")
            for i in range(16):
                src_d = xp if i < 8 else xo
                t = i % 8
                stg = xstage[i % 2]
                skey = "xstage%d" % (i % 2)
                P.dma("sp", skey, stg, src_d[t * 128:(t + 1) * 128, :], writes=[skey])
                dstbuf, dkey = (PRE, "PRE") if i < 8 else (A, "A")
                norm_tile(stg, skey, gbc, "gbc", dstbuf[:, :, t * 128:(t + 1) * 128], dkey, junk, xn_tm, small[:, 0:1], "ss0")

            if do_sample:
                xs_tm = xstage[0][0:16, :]
                P.dma("sp", "xstage0", xs_tm, xs_in[:, :], writes=["xstage0"])
                pt, pk = newps()
                for kc in range(16):
                    P.pe(lambda e, kc=kc, pt=pt: e.transpose(out=pt[:, kc * 16:(kc + 1) * 16], in_=xs_tm[:, kc * 128:(kc + 1) * 128], identity=ident32[0:16, 0:16]), reads=["xstage0", "ident32"], writes=[pk])
                P.dve(lambda e, pt=pt: e.tensor_copy(out=xrs2, in_=pt[:, 0:256]), reads=[pk], writes=["xrs"])
                sq3_a = SCR.f32(16, 16)
                norm_fm(0, sq3_a)
            _stop("A")
            P.barrier(bar_tiles)
            rgw = Reg(RXt[:, 8192:16384], 8192)
            zbs = [rgw.f32(1027), rgw.f32(1027)]; xr = rgw.f32(1024); xrb = rgw.bf16(1024)
            ssacc = gbc[:, 0:1024]
            rr = rgw.f32(1024); ig = rgw.f32(1024); tmp = rgw.f32(1024); hh = rgw.f32(1024)
            av = rr
            hl = small[:, 1:2]
            ps_state["pool"] = list(range(8)); ps_state["i"] = 0
            w2048 = w_in[:, 0:2048].rearrange("(k p) (g n) -> p k g n", p=128, g=2)
            glb = SCR.bf16(1024)
            hgsq = SCR.bf16(1024)
            glbs = [glb, gbc[:, 1024:1536].bitcast(BF16)]
            units = {}

            def rg_unit(u):
                if u not in units:
                    slot, rk = ring_unit([
                        (lambda s: s.rearrange("p (k g n) -> p k g n", k=16, g=2)[:, :, 0, :], w2048[:, :, 0, u * 256:(u + 1) * 256]),
                        (lambda s: s.rearrange("p (k g n) -> p k g n", k=16, g=2)[:, :, 1, :], w2048[:, :, 1, u * 256:(u + 1) * 256])])
                    units[u] = (slot.rearrange("p (k g n) -> p k g n", k=16, g=2), rk)
                return units[u]

            def H1a(c, pas):
                u, j = c // 2, c % 2
                wu, rk = rg_unit(u)
                X, xkey = (PRE, "PRE") if pas == 0 else (A, "A")
                zb = zbs[pas]; zk = "zb%d" % pas
                if pas == 0:
                    P.dve(lambda e, zb=zb: e.memset(zb[:, 0:3], 0.0), writes=[zk])
                else:
                    P.dve(lambda e, zb=zb: e.tensor_scalar(out=zb[:, 0:3], in0=zbs[0][:, 1024:1027], scalar1=maskt[:, 0:1], scalar2=None, op0=ALU.mult),
                          reads=["zb0", "mask"], writes=[zk])
                for tg in range(2):
                    pt, pk = newps()
                    for kc in range(16):
                        P.pe(lambda e, kc=kc, pt=pt, X=X, tg=tg, j=j, wu=wu: e.matmul(pt[:], lhsT=wu[:, kc, 0, j * 128:(j + 1) * 128], rhs=X[:, kc, tg * 512:(tg + 1) * 512], start=(kc == 0), stop=(kc == 15)),
                             reads=[rk, xkey], writes=[pk])
                    P.act(lambda e, pt=pt, tg=tg, zb=zb: e.copy(out=zb[:, 3 + tg * 512:3 + (tg + 1) * 512], in_=pt[:]), reads=[pk], writes=[zk])
                if pas == 1:
                    for tg in range(2):
                        pt, pk = newps()
                        for kc in range(16):
                            P.pe(lambda e, kc=kc, pt=pt, tg=tg, j=j, wu=wu: e.matmul(pt[:], lhsT=wu[:, kc, 1, j * 128:(j + 1) * 128], rhs=A[:, kc, tg * 512:(tg + 1) * 512], start=(kc == 0), stop=(kc == 15)),
                                 reads=[rk, "A"], writes=[pk])
                        P.act(lambda e, pt=pt, tg=tg, c=c: e.activation(out=glbs[c % 2][:, tg * 512:(tg + 1) * 512], in_=pt[:], func=AF.Gelu), reads=[pk], writes=["glb%d" % (c % 2)])
                    if do_sample:
                        pt, pk = newps()
                        for g_ in range(2):
                            for kc in range(16):
                                P.pe(lambda e, kc=kc, pt=pt, g_=g_, j=j, wu=wu: e.matmul(pt[:, g_ * 16:(g_ + 1) * 16], lhsT=wu[:, kc, g_, j * 128:(j + 1) * 128], rhs=Xs[:, kc, :], start=(kc == 0), stop=(kc == 15)),
                                     reads=[rk, "Xs"], writes=[pk])
                        P.dve(lambda e, pt=pt, c=c: e.tensor_copy(out=zsT[:, c, :], in_=pt[:, 0:16]), reads=[pk], writes=["zsT"])
                        P.dve(lambda e, pt=pt, c=c: e.tensor_copy(out=zgT[:, c, :], in_=pt[:, 16:32]), reads=[pk], writes=["zgT"])

            def H1b(c, pas):
                zb = zbs[pas]; zk = "zb%d" % pas
                P.dve(lambda e, c=c, zb=zb: e.tensor_scalar(out=xr, in0=zb[:, 3:1027], scalar1=prm[:, 24 + c:25 + c], scalar2=prm[:, 32 + c:33 + c], op0=ALU.mult, op1=ALU.add),
                      reads=[zk, "prm"], writes=["xr"])
                for jj in range(3):
                    P.dve(lambda e, c=c, jj=jj, zb=zb: e.scalar_tensor_tensor(out=xr, in0=zb[:, jj:jj + 1024], scalar=prm[:, jj * 8 + c:jj * 8 + c + 1], in1=xr, op0=ALU.mult, op1=ALU.add),
                          reads=[zk, "prm", "xr"], writes=["xr"])
                P.act(lambda e: e.copy(out=xrb, in_=xr), reads=["xr"], writes=["xrb"])
                if pas == 1:
                    P.dve(lambda e, c=c, zb=zb: e.tensor_copy(out=pconv_sb[:, c, :], in_=zb[:, 1024:1027]), reads=[zk], writes=["pconv_sb"])
                gps = []
                for gi in range(2):
                    for tg in range(2):
                        pt, pk = newps()
                        P.pe(lambda e, pt=pt, gi=gi, c=c, tg=tg: e.matmul(pt[:], lhsT=wgt[:, c, gi, :], rhs=xrb[:, tg * 512:(tg + 1) * 512], start=True, stop=True),
                             reads=["wgt", "xrb"], writes=[pk])
                        gps.append((pt, pk))
                return gps

            def H2a(c, pas, gps):
                i_ = 0
                for gi, (dst, dk_, bo) in enumerate([(rr, "rr", 40), (ig, "ig", 48)]):
                    for tg in range(2):
                        pt, pk = gps[i_]; i_ += 1
                        P.act(lambda e, pt=pt, dst=dst, tg=tg, bo=bo, c=c: e.activation(out=dst[:, tg * 512:(tg + 1) * 512], in_=pt[:], func=AF.Sigmoid, bias=prm[:, bo + c:bo + c + 1], scale=1.0),
                              reads=[pk, "prm"], writes=[dk_])
                P.dve(lambda e: e.tensor_tensor(out=ig, in0=ig, in1=xr, op=ALU.mult), reads=["ig", "xr"], writes=["ig"])

            def H2b(c, pas):
                P.act(lambda e, c=c: e.activation(out=av, in_=rr, func=AF.Exp, scale=cl[:, c:c + 1]), reads=["rr", "cl"], writes=["rr"])
                P.act(lambda e: e.activation(out=tmp, in_=av, func=AF.Square), reads=["rr"], writes=["tmp"])
                P.act(lambda e: e.activation(out=tmp, in_=tmp, func=AF.Sqrt, bias=1.0, scale=-1.0), reads=["tmp"], writes=["tmp"])
                P.dve(lambda e: e.tensor_tensor(out=ig, in0=ig, in1=tmp, op=ALU.mult), reads=["ig", "tmp"], writes=["ig"])
                if pas == 0:
                    P.dve(lambda e: e.tensor_tensor_scan(out=hh, data0=av, data1=ig, initial=0.0, op0=ALU.mult, op1=ALU.add), reads=["rr", "ig"], writes=["hh"])
                    P.dve(lambda e: e.tensor_scalar(out=hl, in0=hh[:, 1023:1024], scalar1=maskt[:, 0:1], scalar2=None, op0=ALU.mult), reads=["hh", "mask"], writes=["hl"])
                else:
                    P.dve(lambda e: e.tensor_tensor_scan(out=hh, data0=av, data1=ig, initial=hl, op0=ALU.mult, op1=ALU.add), reads=["rr", "ig", "hl"], writes=["hh"])
                    P.dve(lambda e, c=c: e.tensor_copy(out=ph_sb[:, c:c + 1], in_=hh[:, 1023:1024]), reads=["hh"], writes=["ph_sb"])
                    P.dve(lambda e, c=c: e.tensor_tensor(out=hh, in0=hh, in1=glbs[c % 2], op=ALU.mult), reads=["hh", "glb%d" % (c % 2)], writes=["hh"])
                    P.act(lambda e, c=c: e.copy(out=B[:, c, :], in_=hh), reads=["hh"], writes=["B"])
                    P.act(lambda e: e.activation(out=hgsq, in_=hh, func=AF.Square), reads=["hh"], writes=["hgsq"])
                    for tg in range(2):
                        pt, pk = newps()
                        P.pe(lambda e, tg=tg, pt=pt: e.matmul(pt[:], lhsT=onesb[:], rhs=hgsq[:, tg * 512:(tg + 1) * 512], start=True, stop=True),
                             reads=["onesb", "hgsq"], writes=[pk])
                        if c == 0:
                            P.dve(lambda e, tg=tg, pt=pt: e.tensor_copy(out=ssacc[:, tg * 512:(tg + 1) * 512], in_=pt[:]), reads=[pk], writes=["ssacc"])
                        else:
                            P.dve(lambda e, tg=tg, pt=pt: e.tensor_tensor(out=ssacc[:, tg * 512:(tg + 1) * 512], in0=ssacc[:, tg * 512:(tg + 1) * 512], in1=pt[:], op=ALU.add), reads=[pk, "ssacc"], writes=["ssacc"])

            steps = [(c, pas) for c in range(8) for pas in range(2)]
            H1a(*steps[0])
            for k, (c, pas) in enumerate(steps):
                if k + 1 < len(steps):
                    H1a(*steps[k + 1])
                gps = H1b(c, pas)
                if k > 0:
                    H2b(*steps[k - 1])
                H2a(c, pas, gps)
            H2b(*steps[-1])
            _stop("B3")
            P.act(lambda e: e.activation(out=tmp, in_=ssacc, func=AF.Sqrt, bias=EPS, scale=1.0 / 1024), reads=["ssacc"], writes=["tmp"])
            P.dve(lambda e: e.reciprocal(out=tmp, in_=tmp), reads=["tmp"], writes=["tmp"])
            for c in range(8):
                P.dve(lambda e, c=c: e.scalar_tensor_tensor(out=B[:, c, :], in0=B[:, c, :], scalar=prm[:, 64 + c:65 + c], in1=tmp, op0=ALU.mult, op1=ALU.mult),
                      reads=["B", "prm", "tmp"], writes=["B"])
            P.dma("sp", "o_ph", o_ph[:, :], ph_sb[:], reads=["ph_sb"])
            P.dma("sp", "o_pconv", o_pconv[:, :], pconv_sb[:].rearrange("p c r -> p (c r)"), reads=["pconv_sb"])
            ps_state["pool"] = list(range(8)); ps_state["i"] = 0

            _stop("B")
            P.barrier(bar_tiles)
            mw = Reg(RXt[:, 8192:16384], 8192)
            gli = mw.f32(2048); glf = mw.f32(2048); gB = mw.f32(2048)
            SCR.reset()
            gml_bc = SCR.f32(1024)
            P.dma("sp", "gml", gml_bc, g_ml_out.partition_broadcast(128), writes=["gml_bc"])
            bif = small[0:4, 2:4]
            P.dma("sp", "c_bi", bif[:, 0:1], b_ml_i.rearrange("(p o) -> p o", o=1), writes=["bif"])
            P.dma("sp", "c_bf", bif[:, 1:2], b_ml_f.rearrange("(p o) -> p o", o=1), writes=["bif"])
            P.dve(lambda e: e.tensor_scalar(out=bif[:, 1:2], in0=bif[:, 1:2], scalar1=-1.0, scalar2=None, op0=ALU.mult), reads=["bif"], writes=["bif"])
            for pas in range(2):
                X, xkey = (PRE, "PRE") if pas == 0 else (A, "A")
                for tg in range(2):
                    col = pas * 1024 + tg * 512
                    for gi in range(2):
                        pt, pk = newps()
                        for kc in range(16):
                            P.pe(lambda e, kc=kc, pt=pt, X=X, tg=tg, gi=gi: e.matmul(pt[0:4, :], lhsT=wif[:, kc, gi * 4:gi * 4 + 4], rhs=X[:, kc, tg * 512:(tg + 1) * 512], start=(kc == 0), stop=(kc == 15)),
                                 reads=["wif", xkey], writes=[pk])
                        if gi == 0:
                            P.act(lambda e, pt=pt, col=col: e.activation(out=gli[0:4, col:col + 512], in_=pt[0:4, :], func=AF.Identity, bias=bif[:, 0:1], scale=1.0), reads=[pk, "bif"], writes=["gli"])
                        else:
                            P.act(lambda e, pt=pt, col=col: e.activation(out=glf[0:4, col:col + 512], in_=pt[0:4, :], func=AF.Exp, bias=bif[:, 1:2], scale=-1.0), reads=[pk, "bif"], writes=["glf"])
            P.act(lambda e: e.activation(out=glf[0:4, :], in_=glf[0:4, :], func=AF.Ln, bias=1.0, scale=1.0), reads=["glf"], writes=["glf"])
            P.dve(lambda e: e.tensor_scalar(out=glf[0:4, :], in0=glf[0:4, :], scalar1=-1.0, scalar2=None, op0=ALU.mult), reads=["glf"], writes=["glf"])
            _stop("C0")
            gsm = SB("gsm", [4, 192])
            Gpe = gsm[:, 0:16]; Gend = gsm[:, 16:32]; dec = gsm[:, 32:48]; m0own = gsm[:, 48:49]; decd = gsm[:, 64:128]
            gones = mw.f32(1024)
            P.dve(lambda e: e.memset(gones[0:4, :], 1.0), writes=["gones"])
            for pas in range(2):
                sl = slice(pas * 1024, (pas + 1) * 1024)
                P.dve(lambda e, sl=sl: e.tensor_tensor_scan(out=gB[0:4, sl], data0=gones[0:4, :], data1=glf[0:4, sl], initial=0.0, op0=ALU.mult, op1=ALU.add),
                      reads=["gones", "glf"], writes=["gB"])
                P.dve(lambda e, sl=sl: e.tensor_tensor(out=gli[0:4, sl], in0=gli[0:4, sl], in1=gB[0:4, sl], op=ALU.subtract), reads=["gli", "gB"], writes=["gli"])
                if pas == 0:
                    P.dve(lambda e, sl=sl: e.tensor_tensor_scan(out=glf[0:4, sl], data0=gli[0:4, sl], data1=gli[0:4, sl], initial=0.0, op0=ALU.max, op1=ALU.max),
                          reads=["gli", "glf"], writes=["glf"])
                    P.dve(lambda e: e.memset(Gpe[:, 0:1], 0.0), writes=["Gpe"])
                else:
                    P.dve(lambda e, sl=sl: e.tensor_tensor_scan(out=glf[0:4, sl], data0=gli[0:4, sl], data1=gli[0:4, sl], initial=m0own, op0=ALU.max, op1=ALU.max),
                          reads=["gli", "glf", "gsm_m0"], writes=["glf"])
                    P.dve(lambda e: e.tensor_copy(out=Gpe[:, 8:9], in_=m0own), reads=["gsm_m0"], writes=["Gpe"])
                Gv = glf[0:4, sl].rearrange("p (c t) -> p c t", t=128)
                P.dve(lambda e, Gv=Gv, pas=pas: e.tensor_copy(out=Gend[:, pas * 8:pas * 8 + 8], in_=Gv[:, :, 127]), reads=["glf"], writes=["Gend"])
                P.dve(lambda e, pas=pas: e.tensor_copy(out=Gpe[:, pas * 8 + 1:pas * 8 + 8], in_=Gend[:, pas * 8:pas * 8 + 7]), reads=["Gend"], writes=["Gpe"])
                if pas == 0:
                    P.dve(lambda e: e.tensor_tensor(out=m0own, in0=gB[0:4, 1023:1024], in1=glf[0:4, 1023:1024], op=ALU.add), reads=["gB", "glf"], writes=["gsm_m0"])
                    P.dve(lambda e: e.tensor_scalar(out=m0own, in0=m0own, scalar1=maskt[0:4, 0:1], scalar2=None, op0=ALU.mult), reads=["gsm_m0", "mask"], writes=["gsm_m0"])
                else:
                    P.dve(lambda e: e.tensor_tensor(out=pm_sb[:], in0=gB[0:4, 2047:2048], in1=glf[0:4, 2047:2048], op=ALU.add), reads=["gB", "glf"], writes=["pm_sb"])
                    P.dma("sp", "o_pm", o_pm[:, :], pm_sb[:], reads=["pm_sb"])
            if do_sample:
                pt, pk = newps()
                for kc in range(16):
                    P.pe(lambda e, kc=kc, pt=pt: e.matmul(pt[0:16, 0:8], lhsT=Xs[:, kc, :], rhs=wif[:, kc, :], start=(kc == 0), stop=(kc == 15)), reads=["Xs", "wif"], writes=[pk])
                P.dve(lambda e, pt=pt: e.tensor_copy(out=lif_s[0:16, :], in_=pt[0:16, 0:8]), reads=[pk], writes=["lif_s"])
            P.dve(lambda e: e.tensor_tensor(out=dec, in0=Gpe, in1=Gend, op=ALU.subtract), reads=["Gpe", "Gend"], writes=["dec"])
            P.act(lambda e: e.activation(out=dec, in_=dec, func=AF.Exp), reads=["dec"], writes=["dec"])
            Gpe_b = Gpe.unsqueeze(2).to_broadcast([4, 16, 128])
            uv = gli[0:4, :].rearrange("p (c t) -> p c t", t=128)
            Bv = gB[0:4, :].rearrange("p (c t) -> p c t", t=128)
            P.dve(lambda e: e.tensor_tensor(out=uv, in0=uv, in1=Gpe_b, op=ALU.subtract), reads=["gli", "Gpe"], writes=["gli"])
            P.act(lambda e: e.activation(out=gli[0:4, :], in_=gli[0:4, :], func=AF.Exp), reads=["gli"], writes=["gli"])
            P.dve(lambda e: e.tensor_tensor(out=Bv, in0=Bv, in1=Gpe_b, op=ALU.add), reads=["gB", "Gpe"], writes=["gB"])
            P.act(lambda e: e.activation(out=gB[0:4, :], in_=gB[0:4, :], func=AF.Exp, scale=-1.0), reads=["gB"], writes=["gB"])
            ekT = SB("ekT", [128, 16, 4]); clT = SB("clT", [128, 8, 4]); decbc = SB("decbc", [128, 16, 4])
            pt, pk = newps()
            for tt in range(16):
                P.pe(lambda e, tt=tt, pt=pt: e.transpose(out=pt[:, tt * 4:tt * 4 + 4], in_=gli[0:4, tt * 128:(tt + 1) * 128], identity=ident32[0:4, 0:4]), reads=["gli", "ident32"], writes=[pk])
            for c in range(8):
                P.pe(lambda e, c=c, pt=pt: e.transpose(out=pt[:, 64 + c * 4:64 + c * 4 + 4], in_=gB[0:4, 1024 + c * 128:1024 + (c + 1) * 128], identity=ident32[0:4, 0:4]), reads=["gB", "ident32"], writes=[pk])
            P.dve(lambda e, pt=pt: e.tensor_copy(out=ekT[:].rearrange("p a b -> p (a b)"), in_=pt[:, 0:64]), reads=[pk], writes=["ekT"])
            P.dve(lambda e, pt=pt: e.tensor_copy(out=clT[:].rearrange("p a b -> p (a b)"), in_=pt[:, 64:96]), reads=[pk], writes=["clT"])
            ekTs = SB("ekTs", [128, 16, 4])
            P.dve(lambda e: e.tensor_scalar(out=ekTs[:], in0=ekT[:], scalar1=128 ** -0.5, scalar2=None, op0=ALU.mult), reads=["ekT"], writes=["ekTs"])
            eye4 = ident32[0:4, 0:4]
            P.dve(lambda e: e.tensor_tensor(out=decd.rearrange("p (a b) -> p a b", b=4), in0=dec.unsqueeze(2).to_broadcast([4, 16, 4]), in1=eye4.unsqueeze(1).to_broadcast([4, 16, 4]), op=ALU.mult),
                  reads=["dec", "ident32"], writes=["decd"])
            pt, pk = newps()
            P.pe(lambda e, pt=pt: e.matmul(pt[:, 0:64], lhsT=ones32[0:4, :], rhs=decd, start=True, stop=True), reads=["ones32", "decd"], writes=[pk])
            P.dve(lambda e, pt=pt: e.tensor_copy(out=decbc[:].rearrange("p a b -> p (a b)"), in_=pt[:, 0:64]), reads=[pk], writes=["decbc"])

            _stop("C1")
            P.barrier(bar_tiles)
            mw = Reg(RXt[:, 8192:16384], 8192)
            qT = mw.bf16(1024); kT = mw.bf16(1024)
            kw = mw.bf16(16, 128); vaug = mw.bf16(16, 264)[:, :, 0:257]; og = mw.bf16(8, 256)
            Cbf = mw.bf16(8, 264)[:, :, 0:257]
            pC_sb = mw.f32(4, 257)
            Rst = SCR.f32(257); C0own = SCR.f32(257)
            Ptil = SCR.bf16(128); t1 = SCR.f32(256); ymlb = SCR.bf16(256)
            den = small[:, 8:9]; ssm = small[:, 9:10]; tot = small[:, 10:11]
            P.dve(lambda e: e.memset(vaug[:, :, 256:257], 1.0), writes=["vaug"])
            DK_S = 128 ** -0.5
            uO = None
            for h in range(4):
                def fillA(s, h=h):
                    return s
                slotA, rkA = ring_unit([
                    (lambda s: s.rearrange("p (k n) -> p k n", k=16)[:, :, 0:128], w_in[:, OFF_Q + h * 128:OFF_Q + (h + 1) * 128].rearrange("(k p) n -> p k n", p=128)),
                    (lambda s: s.rearrange("p (k n) -> p k n", k=16)[:, :, 128:256], w_in[:, OFF_K + h * 128:OFF_K + (h + 1) * 128].rearrange("(k p) n -> p k n", p=128)),
                    (lambda s: s.rearrange("p (k n) -> p k n", k=16)[:, :, 256:512], w_in[:, OFF_V + h * 256:OFF_V + (h + 1) * 256].rearrange("(k p) n -> p k n", p=128)),
                ])
                uA = slotA.rearrange("p (k n) -> p k n", k=16)
                if h % 2 == 0:
                    slotO, rkO = ring_unit([(lambda s: s.rearrange("p (k n) -> p k n", k=16), w_in[:, OFF_O + h * 256:OFF_O + (h + 2) * 256].rearrange("(k p) n -> p k n", p=128))])
                    uO = slotO.rearrange("p (k n) -> p k n", k=16)
                for which, dstT, dkey_, sc_ in ((0, qT, "qT", 1.0), (1, kT, "kT", DK_S)):
                    for tg in range(2):
                        pt, pk = newps()
                        for kc in range(16):
                            P.pe(lambda e, kc=kc, pt=pt, tg=tg, which=which, uA=uA: e.matmul(pt[:], lhsT=uA[:, kc, which * 128:(which + 1) * 128], rhs=A[:, kc, tg * 512:(tg + 1) * 512], start=(kc == 0), stop=(kc == 15)),
                                 reads=[rkA, "A"], writes=[pk])
                        P.act(lambda e, pt=pt, tg=tg, dstT=dstT, sc_=sc_: e.activation(out=dstT[:, tg * 512:(tg + 1) * 512], in_=pt[:], func=AF.Identity, scale=sc_), reads=[pk], writes=[dkey_])
                _stop("C2a")
                for tt in range(16):
                    X, xkey = (PRE, "PRE") if tt < 8 else (A, "A")
                    t = tt % 8
                    pt, pk = newps()
                    for kc in range(16):
                        P.pe(lambda e, kc=kc, pt=pt, X=X, t=t, uA=uA: e.matmul(pt[:, 0:384], lhsT=X[:, kc, t * 128:(t + 1) * 128], rhs=uA[:, kc, 128:512], start=(kc == 0), stop=(kc == 15)),
                             reads=[rkA, xkey], writes=[pk])
                    P.dve(lambda e, pt=pt, tt=tt, h=h: e.tensor_scalar(out=kw[:, tt, :], in0=pt[:, 0:128], scalar1=ekTs[:, tt, h:h + 1], scalar2=None, op0=ALU.mult),
                          reads=[pk, "ekTs"], writes=["kw"])
                    P.act(lambda e, pt=pt, tt=tt: e.copy(out=vaug[:, tt, 0:256], in_=pt[:, 128:384]), reads=[pk], writes=["vaug"])
                _stop("C2b")
                for t in range(8):
                    pt, pk = newps()
                    for kc in range(16):
                        P.pe(lambda e, kc=kc, pt=pt, t=t, uO=uO, h=h: e.matmul(pt[:, 0:256], lhsT=A[:, kc, t * 128:(t + 1) * 128], rhs=uO[:, kc, (h % 2) * 256:(h % 2 + 1) * 256], start=(kc == 0), stop=(kc == 15)),
                             reads=[rkO, "A"], writes=[pk])
                    P.act(lambda e, pt=pt, t=t: e.activation(out=og[:, t, :], in_=pt[:, 0:256], func=AF.Sigmoid), reads=[pk], writes=["og"])
                _stop("C2")
                for tt in range(16):
                    c = tt % 8
                    pt, pk = newps()
                    P.pe(lambda e, pt=pt, tt=tt: e.matmul(pt[:, 0:257], lhsT=kw[:, tt, :], rhs=vaug[:, tt, :], start=True, stop=True), reads=["kw", "vaug"], writes=[pk])
                    if tt == 0:
                        P.dve(lambda e, pt=pt: e.tensor_copy(out=Rst, in_=pt[:, 0:257]), reads=[pk], writes=["Rst"])
                    elif tt == 8:
                        P.dve(lambda e, h=h: e.tensor_scalar(out=C0own, in0=Rst, scalar1=decbc[:, 7, h:h + 1], scalar2=maskt[:, 0:1], op0=ALU.mult, op1=ALU.mult),
                              reads=["Rst", "decbc", "mask"], writes=["C0own"])
                        P.act(lambda e: e.copy(out=Cbf[:, 0, :], in_=C0own), reads=["C0own"], writes=["Cbf"])
                        P.dve(lambda e, pt=pt: e.tensor_tensor(out=Rst, in0=C0own, in1=pt[:, 0:257], op=ALU.add), reads=["C0own", pk, "Rst"], writes=["Rst"])
                    else:
                        if tt > 8:
                            P.act(lambda e, c=c, tt=tt, h=h: e.activation(out=Cbf[:, c, :], in_=Rst, func=AF.Identity, scale=decbc[:, tt - 1, h:h + 1]), reads=["Rst", "decbc"], writes=["Cbf"])
                        P.dve(lambda e, pt=pt, tt=tt, h=h: e.scalar_tensor_tensor(out=Rst, in0=Rst, scalar=decbc[:, tt - 1, h:h + 1], in1=pt[:, 0:257], op0=ALU.mult, op1=ALU.add),
                              reads=["Rst", "decbc", pk], writes=["Rst"])
                P.dve(lambda e, h=h: e.tensor_scalar(out=pC_sb[:, h, :], in0=Rst, scalar1=decbc[:, 15, h:h + 1], scalar2=None, op0=ALU.mult), reads=["Rst", "decbc"], writes=["pC_sb"])
                _stop("C3")
                for c in range(8):
                    cs = slice(c * 128, (c + 1) * 128)
                    pt, pk = newps()
                    P.pe(lambda e, pt=pt, cs=cs: e.matmul(pt[:, 0:128], lhsT=kT[:, cs], rhs=qT[:, cs], start=True, stop=True), reads=["kT", "qT"], writes=[pk])
                    P.dve(lambda e, pt=pt, c=c, h=h: e.scalar_tensor_tensor(out=Ptil, in0=pt[:, 0:128], scalar=ekT[:, 8 + c, h:h + 1], in1=causal[:], op0=ALU.mult, op1=ALU.mult),
                          reads=[pk, "ekT", "causal"], writes=["Ptil"])
                    pa, pak = newps()
                    P.pe(lambda e, pa=pa, c=c: e.matmul(pa[:, 0:257], lhsT=Ptil, rhs=vaug[:, 8 + c, :], start=True, stop=False), reads=["Ptil", "vaug"], writes=[pak])
                    P.pe(lambda e, pa=pa, c=c, cs=cs: e.matmul(pa[:, 0:257], lhsT=qT[:, cs], rhs=Cbf[:, c, :], start=False, stop=True), reads=["qT", "Cbf"], writes=[pak])
                    P.act(lambda e, pa=pa: e.activation(out=den, in_=pa[:, 256:257], func=AF.Abs), reads=[pak], writes=["den"])
                    P.dve(lambda e, c=c, h=h: e.tensor_scalar(out=den, in0=den, scalar1=clT[:, c, h:h + 1], scalar2=None, op0=ALU.max), reads=["den", "clT"], writes=["den"])
                    P.dve(lambda e: e.reciprocal(out=den, in_=den), reads=["den"], writes=["den"])
                    P.act(lambda e, pa=pa: e.activation(out=t1, in_=pa[:, 0:256], func=AF.Square, scale=den, accum_out=ssm), reads=[pak, "den"], writes=["t1", "ssm"])
                    rstd_from_ss(ssm, 256, "ssm")
                    P.dve(lambda e: e.tensor_tensor(out=tot, in0=ssm, in1=den, op=ALU.mult), reads=["ssm", "den"], writes=["tot"])
                    P.dve(lambda e, pa=pa, h=h: e.scalar_tensor_tensor(out=t1, in0=pa[:, 0:256], scalar=tot, in1=gml_bc[:, h * 256:(h + 1) * 256], op0=ALU.mult, op1=ALU.mult),
                          reads=[pak, "tot", "gml_bc", "t1"], writes=["t1"])
                    P.dve(lambda e, c=c: e.tensor_tensor(out=ymlb, in0=t1, in1=og[:, c, :], op=ALU.mult), reads=["t1", "og"], writes=["ymlb"])
                    ptr, ptk = newps()
                    ptrb = ptr[:].bitcast(BF16)
                    for jj in range(2):
                        P.pe(lambda e, jj=jj, ptrb=ptrb: e.transpose(out=ptrb[:, jj * 128:(jj + 1) * 128], in_=ymlb[:, jj * 128:(jj + 1) * 128], identity=identb[:]), reads=["ymlb", "identb"], writes=[ptk])
                    P.act(lambda e, ptrb=ptrb, h=h, cs=cs: e.copy(out=B[:, 8 + 2 * h:10 + 2 * h, cs], in_=ptrb[:, 0:256].rearrange("p (j t) -> p j t", j=2)), reads=[ptk], writes=["B"])
            P.dma("sp", "o_pC", o_pC[:, :], pC_sb[:].rearrange("p h n -> p (h n)"), reads=["pC_sb"])

            if do_sample:
                P.barrier(bar_tiles)
                P.dve(lambda e: e.memset(bart[:, 3:4], 0.0), writes=["pC_sb", "bart3"])
                P.barrier(bar_tiles)
                SR = Reg(RXt, 16384)
                qkv_s = SR.f32(4, 512); og_s = SR.f32(1024)
                eye16 = ident32[0:16, 0:16]
                for h in range(4):
                    slotA, rkA = ring_unit([
                        (lambda s: s.rearrange("p (k n) -> p k n", k=16)[:, :, 0:128], w_in[:, OFF_Q + h * 128:OFF_Q + (h + 1) * 128].rearrange("(k p) n -> p k n", p=128)),
                        (lambda s: s.rearrange("p (k n) -> p k n", k=16)[:, :, 128:256], w_in[:, OFF_K + h * 128:OFF_K + (h + 1) * 128].rearrange("(k p) n -> p k n", p=128)),
                        (lambda s: s.rearrange("p (k n) -> p k n", k=16)[:, :, 256:512], w_in[:, OFF_V + h * 256:OFF_V + (h + 1) * 256].rearrange("(k p) n -> p k n", p=128)),
                    ])
                    uA = slotA.rearrange("p (k n) -> p k n", k=16)
                    pt, pk = newps()
                    for kc in range(16):
                        P.pe(lambda e, kc=kc, pt=pt, uA=uA: e.matmul(pt[0:16, :], lhsT=Xs[:, kc, :], rhs=uA[:, kc, :], start=(kc == 0), stop=(kc == 15)), reads=[rkA, "Xs"], writes=[pk])
                    P.act(lambda e, pt=pt, h=h: e.copy(out=qkv_s[0:16, h, :], in_=pt[0:16, :]), reads=[pk], writes=["qkv_s"])
                for hp in range(2):
                    slotO, rkO = ring_unit([(lambda s: s.rearrange("p (k n) -> p k n", k=16), w_in[:, OFF_O + hp * 512:OFF_O + (hp + 1) * 512].rearrange("(k p) n -> p k n", p=128))])
                    uO = slotO.rearrange("p (k n) -> p k n", k=16)
                    pt, pk = newps()
                    for kc in range(16):
                        P.pe(lambda e, kc=kc, pt=pt, uO=uO: e.matmul(pt[0:16, :], lhsT=Xs[:, kc, :], rhs=uO[:, kc, :], start=(kc == 0), stop=(kc == 15)), reads=[rkO, "Xs"], writes=[pk])
                    P.act(lambda e, pt=pt, hp=hp: e.activation(out=og_s[0:16, hp * 512:(hp + 1) * 512], in_=pt[0:16, :], func=AF.Sigmoid), reads=[pk], writes=["og_s"])
                sr_mark = SR.off
                cb_tm = SR.f32(3072); h0_tm = SR.f32(1024); fmw = SR.f32(512)
                xr3 = SR.f32(8, 16); tmp3 = SR.f32(8, 16); xrb3 = SR.bf16(8, 16); rg3 = SR.f32(16, 16)
                a3 = SR.f32(8, 16); m3 = SR.f32(8, 16); hn3 = SR.f32(8, 16); hg3 = SR.f32(8, 16)
                SR.off = 14336
                sh_tm = SR.f32(1024); sz_tm = SR.f32(1024)
                P.dma("sp", "cb_tm", cb_tm[0:16, :], st_conv[:, :], writes=["cb_tm"])
                P.dma("sp", "h0_tm", h0_tm[0:16, :], st_h[:, :], writes=["h0_tm"])
                P.dma("sp", "o_sconv01", o_sconv[:, 0:2048], cb_tm[0:16, 1024:3072], reads=["cb_tm"])
                pt, pk = newps()
                for blk in range(24):
                    P.pe(lambda e, blk=blk, pt=pt: e.transpose(out=pt[:, blk * 16:(blk + 1) * 16], in_=cb_tm[0:16, blk * 128:(blk + 1) * 128], identity=eye16), reads=["cb_tm", "ident32"], writes=[pk])
                for c in range(8):
                    P.pe(lambda e, c=c, pt=pt: e.transpose(out=pt[:, (24 + c) * 16:(25 + c) * 16], in_=h0_tm[0:16, c * 128:(c + 1) * 128], identity=eye16), reads=["h0_tm", "ident32"], writes=[pk])
                P.dve(lambda e, pt=pt: e.tensor_copy(out=fmw, in_=pt[:, :]), reads=[pk], writes=["fmw"])
                cbT = fmw[:, 0:384].rearrange("p (r c t) -> p r c t", r=3, c=8)
                h0T = fmw[:, 384:512].rearrange("p (c t) -> p c t", c=8)

                def pb(c0):
                    return prm[:, c0:c0 + 8].unsqueeze(2).to_broadcast([128, 8, 16])

                P.dve(lambda e: e.tensor_tensor(out=xr3, in0=zsT, in1=pb(24), op=ALU.mult), reads=["zsT", "prm"], writes=["xr3"])
                P.dve(lambda e: e.tensor_tensor(out=xr3, in0=xr3, in1=pb(32), op=ALU.add), reads=["xr3", "prm"], writes=["xr3"])
                for jj in range(3):
                    P.dve(lambda e, jj=jj: e.tensor_tensor(out=tmp3, in0=cbT[:, jj], in1=pb(jj * 8), op=ALU.mult), reads=["fmw", "prm"], writes=["tmp3"])
                    P.dve(lambda e: e.tensor_tensor(out=xr3, in0=xr3, in1=tmp3, op=ALU.add), reads=["xr3", "tmp3"], writes=["xr3"])
                P.act(lambda e: e.copy(out=xrb3, in_=xr3), reads=["xr3"], writes=["xrb3"])
                pt, pk = newps()
                for gi in range(2):
                    for c in range(8):
                        P.pe(lambda e, gi=gi, c=c, pt=pt: e.matmul(pt[:, (gi * 8 + c) * 16:(gi * 8 + c + 1) * 16], lhsT=wgt[:, c, gi, :], rhs=xrb3[:, c, :], start=True, stop=True), reads=["wgt", "xrb3"], writes=[pk])
                P.dve(lambda e, pt=pt: e.tensor_tensor(out=rg3, in0=pt[:, 0:256].rearrange("p (a b) -> p a b", a=16), in1=prm[:, 40:56].unsqueeze(2).to_broadcast([128, 16, 16]), op=ALU.add), reads=[pk, "prm"], writes=["rg3"])
                P.act(lambda e: e.activation(out=rg3, in_=rg3, func=AF.Sigmoid), reads=["rg3"], writes=["rg3"])
                r3 = rg3[:, 0:8, :]; ig3 = rg3[:, 8:16, :]
                P.dve(lambda e: e.tensor_tensor(out=a3, in0=r3, in1=cl[:, 0:8].unsqueeze(2).to_broadcast([128, 8, 16]), op=ALU.mult), reads=["rg3", "cl"], writes=["a3"])
                P.act(lambda e: e.activation(out=a3, in_=a3, func=AF.Exp), reads=["a3"], writes=["a3"])
                P.dve(lambda e: e.tensor_tensor(out=m3, in0=a3, in1=a3, op=ALU.mult), reads=["a3"], writes=["m3"])
                P.act(lambda e: e.activation(out=m3, in_=m3, func=AF.Sqrt, bias=1.0, scale=-1.0), reads=["m3"], writes=["m3"])
                P.dve(lambda e: e.tensor_tensor(out=ig3, in0=ig3, in1=xr3, op=ALU.mult), reads=["rg3", "xr3"], writes=["rg3"])
                P.dve(lambda e: e.tensor_tensor(out=ig3, in0=ig3, in1=m3, op=ALU.mult), reads=["rg3", "m3"], writes=["rg3"])
                P.dve(lambda e: e.tensor_tensor(out=hn3, in0=a3, in1=h0T, op=ALU.mult), reads=["a3", "fmw"], writes=["hn3"])
                P.dve(lambda e: e.tensor_tensor(out=hn3, in0=hn3, in1=ig3, op=ALU.add), reads=["hn3", "rg3"], writes=["hn3"])
                P.act(lambda e: e.activation(out=tmp3, in_=zgT, func=AF.Gelu), reads=["zgT", "tmp3"], writes=["tmp3"])
                P.dve(lambda e: e.tensor_tensor(out=hg3, in0=hn3, in1=tmp3, op=ALU.mult), reads=["hn3", "tmp3"], writes=["hg3"])
                P.dve(lambda e: e.tensor_tensor(out=m3, in0=hg3, in1=hg3, op=ALU.mult), reads=["hg3"], writes=["m3"])
                pt, pk = newps()
                for c in range(8):
                    P.pe(lambda e, c=c, pt=pt: e.matmul(pt[:, 0:16], lhsT=ones32[:], rhs=m3[:, c, :], start=(c == 0), stop=(c == 7)), reads=["ones32", "m3"], writes=[pk])
                P.act(lambda e, pt=pt: e.activation(out=srs, in_=pt[:, 0:16], func=AF.Sqrt, bias=EPS, scale=1.0 / 1024), reads=[pk], writes=["srs"])
                P.dve(lambda e: e.reciprocal(out=srs, in_=srs), reads=["srs"], writes=["srs"])
                P.dve(lambda e: e.tensor_tensor(out=hg3, in0=hg3, in1=srs.unsqueeze(1).to_broadcast([128, 8, 16]), op=ALU.mult), reads=["hg3", "srs"], writes=["hg3"])
                P.dve(lambda e: e.tensor_tensor(out=Ys[:, 0:8, :], in0=hg3, in1=pb(64), op=ALU.mult), reads=["hg3", "prm"], writes=["Ys"])
                for (src3, skey, dst, okey, odram) in ((hn3, "hn3", sh_tm, "sh_tm", o_sh[:, :]), (zsT, "zsT", sz_tm, "sz_tm", o_sconv[:, 2048:3072])):
                    for half in range(2):
                        pt, pk = newps()
                        for cc in range(4):
                            c = half * 4 + cc
                            P.pe(lambda e, cc=cc, c=c, pt=pt, src3=src3: e.transpose(out=pt[0:16, cc * 128:(cc + 1) * 128], in_=src3[:, c, :], identity=ident32[:]), reads=[skey, "ident32"], writes=[pk])
                        P.dve(lambda e, pt=pt, dst=dst, half=half: e.tensor_copy(out=dst[0:16, half * 512:(half + 1) * 512], in_=pt[0:16, :]), reads=[pk], writes=[okey])
                    P.dma("sp", okey, odram, dst[0:16, :], reads=[okey])
                _stop("S1")
                P.barrier(bar_tiles)
                P.dve(lambda e: e.memset(bart[:, 3:4], 0.0), writes=["cb_tm", "bart3"])
                P.barrier(bar_tiles)
                SR.off = sr_mark
                bi_bc = SR.f32(8); msm = SR.f32(64); n_in = SR.f32(512); prod = SR.f32(512); nn = SR.f32(512)
                kv = SR.f32(4 * 384); qTs = SR.f32(64); Qm = SR.f32(4 * 256); sdecd = SR.f32(64); decb = SR.f32(64)
                Cin = [SR.f32(1024), SR.f32(1024)]; Cout = [SR.f32(1024), SR.f32(1024)]
                rowb = SR.f32(384); num = SR.f32(1024); yml = SR.f32(1024); junk16 = SR.f32(256)
                assert SR.off <= 14336
                P.dma("sp", "bi_bc", bi_bc[0:16, 0:4], b_ml_i.partition_broadcast(16), writes=["bi_bc"])
                P.dma("sp", "bf_bc", bi_bc[0:16, 4:8], b_ml_f.partition_broadcast(16), writes=["bi_bc"])
                m_in = msm[0:16, 0:4]; li_ = msm[0:16, 4:8]; lf_ = msm[0:16, 8:12]; mnew = msm[0:16, 12:16]; Dg = msm[0:16, 16:20]
                scg = msm[0:16, 20:24]; qk_ = msm[0:16, 24:28]; qn_ = msm[0:16, 28:32]; den_ = msm[0:16, 32:36]; tmp4 = msm[0:16, 36:40]
                dqk = msm[0:16, 40:44]; tot4 = msm[0:16, 44:48]; ss4 = msm[0:16, 48:52]
                P.dma("sp", "m_in", m_in, st_m[:, :], writes=["msm"])
                P.dma("sp", "n_in", n_in[0:16, :], st_n[:, :], writes=["n_in"])
                MS = ["msm"]
                P.dve(lambda e: e.tensor_tensor(out=msm[0:16, 4:12], in0=lif_s[0:16, :], in1=bi_bc[0:16, :], op=ALU.add), reads=["lif_s", "bi_bc"], writes=MS)
                P.act(lambda e: e.activation(out=lf_, in_=lf_, func=AF.Exp, scale=-1.0), reads=MS, writes=MS)
                P.act(lambda e: e.activation(out=lf_, in_=lf_, func=AF.Ln, bias=1.0, scale=1.0), reads=MS, writes=MS)
                P.dve(lambda e: e.tensor_scalar(out=lf_, in0=lf_, scalar1=-1.0, scalar2=None, op0=ALU.mult), reads=MS, writes=MS)
                P.dve(lambda e: e.tensor_tensor(out=tmp4, in0=lf_, in1=m_in, op=ALU.add), reads=MS, writes=MS)
                P.dve(lambda e: e.tensor_tensor(out=mnew, in0=tmp4, in1=li_, op=ALU.max), reads=MS, writes=MS)
                P.dve(lambda e: e.tensor_tensor(out=Dg, in0=li_, in1=mnew, op=ALU.subtract), reads=MS, writes=MS)
                P.act(lambda e: e.activation(out=Dg, in_=Dg, func=AF.Exp), reads=MS, writes=MS)
                P.dve(lambda e: e.tensor_tensor(out=scg, in0=tmp4, in1=mnew, op=ALU.subtract), reads=MS, writes=MS)
                P.act(lambda e: e.activation(out=scg, in_=scg, func=AF.Exp), reads=MS, writes=MS)
                osm = SR.f32(4)
                P.dve(lambda e: e.tensor_copy(out=osm[0:16, :], in_=mnew), reads=MS, writes=["osm"])
                P.dma("sp", "o_sm", o_sm[:, :], osm[0:16, :], reads=["osm"])
                q3 = qkv_s[0:16, :, 0:128]; k3 = qkv_s[0:16, :, 128:256]; v3 = qkv_s[0:16, :, 256:512]
                n3 = n_in[0:16, :].rearrange("p (h d) -> p h d", h=4)
                prod3 = prod[0:16, :].rearrange("p (h d) -> p h d", h=4)
                nn3 = nn[0:16, :].rearrange("p (h d) -> p h d", h=4)
                kv3 = kv[0:16, :].rearrange("p (h d) -> p h d", h=4)
                P.dve(lambda e: e.tensor_scalar(out=k3, in0=k3, scalar1=DK_S, scalar2=None, op0=ALU.mult), reads=["qkv_s"], writes=["qkv_s"])
                P.dve(lambda e: e.tensor_tensor(out=prod3, in0=q3, in1=k3, op=ALU.mult), reads=["qkv_s"], writes=["prod"])
                P.dve(lambda e: e.reduce_sum(out=qk_, in_=prod3, axis=AX.X), reads=["prod"] + MS, writes=MS)
                P.dve(lambda e: e.tensor_tensor(out=prod3, in0=q3, in1=n3, op=ALU.mult), reads=["qkv_s", "n_in", "prod"], writes=["prod"])
                P.dve(lambda e: e.reduce_sum(out=qn_, in_=prod3, axis=AX.X), reads=["prod"] + MS, writes=MS)
                P.dve(lambda e: e.tensor_tensor(out=dqk, in0=Dg, in1=qk_, op=ALU.mult), reads=MS, writes=MS)
                P.dve(lambda e: e.tensor_tensor(out=den_, in0=scg, in1=qn_, op=ALU.mult), reads=MS, writes=MS)
                P.dve(lambda e: e.tensor_tensor(out=den_, in0=den_, in1=dqk, op=ALU.add), reads=MS, writes=MS)
                P.act(lambda e: e.activation(out=tmp4, in_=mnew, func=AF.Exp, scale=-1.0), reads=MS, writes=MS)
                P.act(lambda e: e.activation(out=den_, in_=den_, func=AF.Abs), reads=MS, writes=MS)
                P.dve(lambda e: e.tensor_tensor(out=den_, in0=den_, in1=tmp4, op=ALU.max), reads=MS, writes=MS)
                P.dve(lambda e: e.reciprocal(out=den_, in_=den_), reads=MS, writes=MS)
                P.dve(lambda e: e.tensor_tensor(out=nn3, in0=n3, in1=scg.unsqueeze(2).to_broadcast([16, 4, 128]), op=ALU.mult), reads=["n_in"] + MS, writes=["nn"])
                P.dve(lambda e: e.tensor_tensor(out=kv3[:, :, 0:128], in0=k3, in1=Dg.unsqueeze(2).to_broadcast([16, 4, 128]), op=ALU.mult), reads=["qkv_s"] + MS, writes=["kv"])
                P.dve(lambda e: e.tensor_copy(out=kv3[:, :, 128:384], in_=v3), reads=["qkv_s"], writes=["kv"])
                P.dve(lambda e: e.tensor_tensor(out=nn3, in0=nn3, in1=kv3[:, :, 0:128], op=ALU.add), reads=["nn", "kv"], writes=["nn"])
                P.dma("sp", "o_sn", o_sn[:, :], nn[0:16, :], reads=["nn"])
                pt, pk = newps()
                for h in range(4):
                    P.pe(lambda e, h=h, pt=pt: e.transpose(out=pt[:, h * 16:(h + 1) * 16], in_=qkv_s[0:16, h, 0:128], identity=eye16), reads=["qkv_s", "ident32"], writes=[pk])
                P.dve(lambda e, pt=pt: e.tensor_copy(out=qTs, in_=pt[:, 0:64]), reads=[pk], writes=["qTs"])
                qTs3 = qTs.rearrange("p (h s) -> p h s", h=4)
                Qm4 = Qm.rearrange("p (h s c) -> p h s c", h=4, s=16)
                eyebc3 = eyebc[:].rearrange("p (s c) -> p s c", s=16)
                for h in range(4):
                    P.dve(lambda e, h=h: e.tensor_tensor(out=Qm4[:, h], in0=qTs3[:, h, :].unsqueeze(2).to_broadcast([128, 16, 16]), in1=eyebc3, op=ALU.mult), reads=["qTs", "eyebc"], writes=["Qm"])
                ssdecd3 = sdecd[0:16, :].rearrange("p (s h) -> p s h", s=16)
                P.dve(lambda e: e.tensor_tensor(out=ssdecd3, in0=eye16.unsqueeze(2).to_broadcast([16, 16, 4]), in1=scg.unsqueeze(1).to_broadcast([16, 16, 4]), op=ALU.mult), reads=["ident32"] + MS, writes=["sdecd"])
                pt, pk = newps()
                P.pe(lambda e, pt=pt: e.matmul(pt[:, 0:64], lhsT=ones32[0:16, :], rhs=sdecd[0:16, :], start=True, stop=True), reads=["ones32", "sdecd"], writes=[pk])
                P.dve(lambda e, pt=pt: e.tensor_copy(out=decb, in_=pt[:, 0:64]), reads=[pk], writes=["decb"])
                ps_state["pool"] = [0, 1, 2, 3]; ps_state["i"] = 0
                for s_ in range(16):
                    ci = Cin[s_ % 2].rearrange("p (h v) -> p h v", h=4); cik = "Cin%d" % (s_ % 2)
                    co = Cout[s_ % 2].rearrange("p (h v) -> p h v", h=4); cok = "Cout%d" % (s_ % 2)
                    P.dma("sp", cik, ci, st_C[s_].rearrange("h d v -> d h v"), writes=[cik])
                    kwm = prod[0:16, :].rearrange("p (h d) -> p h d", h=4)
                    P.dve(lambda e, s_=s_, kwm=kwm: e.tensor_scalar(out=kwm, in0=kv3[:, :, 0:128], scalar1=ident32[0:16, s_:s_ + 1], scalar2=None, op0=ALU.mult), reads=["kv", "ident32"], writes=["prod"])
                    for h in range(4):
                        P.pe(lambda e, h=h, s_=s_, ci=ci: e.matmul(PS[4 + h][0:16, 0:256], lhsT=Qm4[:, h, s_, :], rhs=ci[:, h, :], start=(s_ == 0), stop=(s_ == 15)), reads=["Qm", cik], writes=["ps%d" % (4 + h)])
                        pt2, pk2 = newps()
                        P.pe(lambda e, pt2=pt2, h=h, kwm=kwm: e.matmul(pt2[:, 0:256], lhsT=kwm[:, h, :], rhs=kv3[:, h, 128:384], start=True, stop=True), reads=["prod", "kv"], writes=[pk2])
                        P.dve(lambda e, h=h, s_=s_, ci=ci, co=co, pt2=pt2: e.scalar_tensor_tensor(out=co[:, h, :], in0=ci[:, h, :], scalar=decb[:, s_ * 4 + h:s_ * 4 + h + 1], in1=pt2[:, 0:256], op0=ALU.mult, op1=ALU.add),
                              reads=[cik, "decb", pk2], writes=[cok])
                    P.dma("sp", cok, o_sC[s_].rearrange("h d v -> d h v"), co, reads=[cok])
                num3 = num[0:16, :].rearrange("p (h v) -> p h v", h=4)
                for h in range(4):
                    P.dve(lambda e, h=h: e.tensor_scalar(out=num3[:, h, :], in0=PS[4 + h][0:16, 0:256], scalar1=scg[:, h:h + 1], scalar2=None, op0=ALU.mult), reads=["ps%d" % (4 + h)] + MS, writes=["num"])
                    P.dve(lambda e, h=h: e.scalar_tensor_tensor(out=num3[:, h, :], in0=v3[:, h, :], scalar=dqk[:, h:h + 1], in1=num3[:, h, :], op0=ALU.mult, op1=ALU.add), reads=["qkv_s", "num"] + MS, writes=["num"])
                    P.act(lambda e, h=h: e.activation(out=junk16[0:16, :], in_=num3[:, h, :], func=AF.Square, scale=den_[:, h:h + 1], accum_out=ss4[:, h:h + 1]), reads=["num"] + MS, writes=["junk16"] + MS)
                P.act(lambda e: e.activation(out=ss4, in_=ss4, func=AF.Sqrt, bias=EPS, scale=1.0 / 256), reads=MS, writes=MS)
                P.dve(lambda e: e.reciprocal(out=ss4, in_=ss4), reads=MS, writes=MS)
                P.dve(lambda e: e.tensor_tensor(out=tot4, in0=ss4, in1=den_, op=ALU.mult), reads=MS, writes=MS)
                for h in range(4):
                    P.dve(lambda e, h=h: e.scalar_tensor_tensor(out=yml[0:16, h * 256:(h + 1) * 256], in0=num3[:, h, :], scalar=tot4[:, h:h + 1], in1=gml_bc[0:16, h * 256:(h + 1) * 256], op0=ALU.mult, op1=ALU.mult),
                          reads=["num", "gml_bc"] + MS, writes=["yml"])
                P.dve(lambda e: e.tensor_tensor(out=yml[0:16, :], in0=yml[0:16, :], in1=og_s[0:16, :], op=ALU.mult), reads=["yml", "og_s"], writes=["yml"])
                ps_state["pool"] = list(range(8)); ps_state["i"] = 0
                pt, pk = newps()
                for c in range(8):
                    P.pe(lambda e, c=c, pt=pt: e.transpose(out=pt[:, c * 16:(c + 1) * 16], in_=yml[0:16, c * 128:(c + 1) * 128], identity=eye16), reads=["yml", "ident32"], writes=[pk])
                P.dve(lambda e, pt=pt: e.tensor_copy(out=Ys[:, 8:16, :], in_=pt[:, 0:128].rearrange("p (c t) -> p c t", c=8)), reads=[pk], writes=["Ys"])
                P.barrier(bar_tiles)
                P.dve(lambda e: e.memset(bart[:, 3:4], 0.0), writes=["sh_tm", "sz_tm", "nn", "osm", "Cout0", "Cout1", "bart3"])
            _stop("S")

            _stop("C")
            P.barrier(bar_tiles)
            for t in range(8):
                P.dma("sp", "xres%d" % t, xres[:, t, :], xo[t * 128:(t + 1) * 128, :], writes=["xres%d" % t, "pC_sb"])

            def proj_residual(wmat, Xk, xkey):
                for ng in range(4):
                    slot, rk = ring_unit([(lambda s: s.rearrange("p (k n) -> p k n", k=16), wmat[:, ng * 512:(ng + 1) * 512].rearrange("(k p) n -> p k n", p=128))])
                    wv = slot.rearrange("p (k n) -> p k n", k=16)
                    for t in range(8):
                        pt, pk = newps()
                        for kc in range(16):
                            P.pe(lambda e, kc=kc, pt=pt, t=t, wv=wv: e.matmul(pt[:], lhsT=Xk[:, kc, t * 128:(t + 1) * 128], rhs=wv[:, kc, :], start=(kc == 0), stop=(kc == 15)),
                                 reads=[rk, xkey], writes=[pk])
                        P.dve(lambda e, pt=pt, t=t, ng=ng: e.tensor_tensor(out=xres[:, t, ng * 512:(ng + 1) * 512], in0=xres[:, t, ng * 512:(ng + 1) * 512], in1=pt[:], op=ALU.add),
                              reads=[pk, "xres%d" % t], writes=["xres%d" % t])
                    if do_sample and _on("D"):
                        pt, pk = newps()
                        for j in range(4):
                            for kc in range(16):
                                P.pe(lambda e, kc=kc, pt=pt, j=j, wv=wv: e.matmul(pt[:, j * 16:(j + 1) * 16], lhsT=wv[:, kc, j * 128:(j + 1) * 128], rhs=Ys[:, kc, :], start=(kc == 0), stop=(kc == 15)),
                                     reads=[rk, "Ys"], writes=[pk])
                        P.dve(lambda e, pt=pt, ng=ng: e.tensor_tensor(out=xrs[:, ng * 4:(ng + 1) * 4, :], in0=xrs[:, ng * 4:(ng + 1) * 4, :], in1=pt[:, 0:64].rearrange("p (a b) -> p a b", a=4), op=ALU.add),
                              reads=[pk, "xrs"], writes=["xrs"])

            proj_residual(w_out, B, "B")

            _stop("D")
            SCR.reset()
            gbc = SCR.f32(2048)
            xn_tm = SCR.bf16(2048)
            junk = xn_tm
            _w32 = wgt[:].rearrange("p a b c -> p (a b c)").bitcast(F32)
            stg_o = [_w32[:, 0:512], _w32[:, 512:1024]]
            load_gbc(g_mem, gbc, "gbc")
            memnT = A[:, :, 0:256]
            mstage = B[:, 0:4, :].rearrange("p a b -> p (a b)").bitcast(F32)
            for mt in range(2):
                P.dma("sp", "mstage", mstage, mem[mt * 128:(mt + 1) * 128, :], reads=[], writes=["B"])
                norm_tile(mstage, "B", gbc, "gbc", memnT[:, :, mt * 128:(mt + 1) * 128], "A", junk, xn_tm, small[:, 0:1], "ss0")
            ring_state["slots"] = [0, 1]; ring_state["n"] = 0
            mkT = RINGt[:, 2, 0:4096].rearrange("p (a b) -> p a b", a=16)
            mvb = RINGt[:, 2, 4096:8192].rearrange("p (a b) -> p a b", a=2)
            ostage_i = [0]

            def out_stage(pt, pk, dst_dram):
                i = ostage_i[0] % 2
                ostage_i[0] += 1
                sg = stg_o[i]
                key = "stg_o%d" % i
                P.act(lambda e, pt=pt, sg=sg: e.copy(out=sg, in_=pt[:]), reads=[pk], writes=[key])
                P.dma("sp", key, dst_dram, sg, reads=[key])

            for which, wmat in ((0, w_xa_k), (1, w_xa_v)):
                for ng in range(4):
                    slot, rk = ring_unit([(lambda s: s.rearrange("p (k n) -> p k n", k=16), wmat[:, ng * 512:(ng + 1) * 512].rearrange("(k p) n -> p k n", p=128))])
                    wv = slot.rearrange("p (k n) -> p k n", k=16)
                    if which == 0:
                        for dc in range(4):
                            pt, pk = newps()
                            for kc in range(16):
                                P.pe(lambda e, kc=kc, pt=pt, dc=dc, wv=wv: e.matmul(pt[:, 0:256], lhsT=wv[:, kc, dc * 128:(dc + 1) * 128], rhs=memnT[:, kc, :], start=(kc == 0), stop=(kc == 15)),
                                     reads=[rk, "A"], writes=[pk])
                            P.act(lambda e, pt=pt, ng=ng, dc=dc: e.copy(out=mkT[:, ng * 4 + dc, :], in_=pt[:, 0:256]), reads=[pk], writes=["ring2"])
                    for mt in range(2):
                        pt, pk = newps()
                        for kc in range(16):
                            P.pe(lambda e, kc=kc, pt=pt, mt=mt, wv=wv: e.matmul(pt[:], lhsT=memnT[:, kc, mt * 128:(mt + 1) * 128], rhs=wv[:, kc, :], start=(kc == 0), stop=(kc == 15)),
                                 reads=[rk, "A"], writes=[pk])
                        if which == 1:
                            P.dve(lambda e, pt=pt, mt=mt, ng=ng: e.tensor_copy(out=mvb[:, mt, ng * 512:(ng + 1) * 512], in_=pt[:]), reads=[pk], writes=["ring2"])
                        out_stage(pt, pk, (o_mk if which == 0 else o_mv)[mt * 128:(mt + 1) * 128, ng * 512:(ng + 1) * 512])
            load_gbc(g_xa, gbc, "gbc")
            for t in range(8):
                norm_tile(xres[:, t, :], "xres%d" % t, gbc, "gbc", A[:, :, t * 128:(t + 1) * 128], "A", junk, xn_tm, small[:, 0:1], "ss0")
            XS = 512 ** -0.5
            if do_sample:
                sq3_e = SCR.f32(16, 16)
                xq_s = SCR.bf16(2048)
                norm_fm(16, sq3_e)
            for ng in range(4):
                slot, rk = ring_unit([(lambda s: s.rearrange("p (k n) -> p k n", k=16), w_xa_q[:, ng * 512:(ng + 1) * 512].rearrange("(k p) n -> p k n", p=128))])
                wv = slot.rearrange("p (k n) -> p k n", k=16)
                for dc in range(4):
                    for tg in range(2):
                        pt, pk = newps()
                        for kc in range(16):
                            P.pe(lambda e, kc=kc, pt=pt, dc=dc, tg=tg, wv=wv: e.matmul(pt[:], lhsT=wv[:, kc, dc * 128:(dc + 1) * 128], rhs=A[:, kc, tg * 512:(tg + 1) * 512], start=(kc == 0), stop=(kc == 15)),
                                 reads=[rk, "A"], writes=[pk])
                        if tg == 0:
                            P.act(lambda e, pt=pt, ng=ng, dc=dc, tg=tg: e.activation(out=B[:, ng * 4 + dc, tg * 512:(tg + 1) * 512], in_=pt[:], func=AF.Identity, scale=XS), reads=[pk], writes=["B"])
                        else:
                            P.dve(lambda e, pt=pt, ng=ng, dc=dc, tg=tg: e.tensor_scalar(out=B[:, ng * 4 + dc, tg * 512:(tg + 1) * 512], in0=pt[:], scalar1=XS, scalar2=None, op0=ALU.mult), reads=[pk], writes=["B"])
                if do_sample and _on("Q"):
                    pt, pk = newps()
                    for kc in range(16):
                        P.pe(lambda e, kc=kc, pt=pt, wv=wv: e.matmul(pt[0:16, :], lhsT=Xs[:, kc, :], rhs=wv[:, kc, :], start=(kc == 0), stop=(kc == 15)), reads=[rk, "Xs"], writes=[pk])
                    P.act(lambda e, pt=pt, ng=ng: e.activation(out=xq_s[0:16, ng * 512:(ng + 1) * 512], in_=pt[0:16, :], func=AF.Identity, scale=XS), reads=[pk], writes=["xq_s"])
            _stop("E1")
            pfp = stg_o[0][:, 0:256]; pbf = stg_o[0][:, 256:384].bitcast(BF16)
            pT = stg_o[1].bitcast(BF16).rearrange("p (a b) -> p a b", a=2)
            mx = small[:, 12:13]; rs = small[:, 13:14]
            for tg in range(2):
                for hd in range(4):
                    for t4 in range(4):
                        t = tg * 4 + t4
                        ts_ = slice(t * 128, (t + 1) * 128)
                        pt, pk = newps()
                        for dc in range(4):
                            P.pe(lambda e, pt=pt, dc=dc, hd=hd, ts_=ts_: e.matmul(pt[:, 0:256], lhsT=B[:, hd * 4 + dc, ts_], rhs=mkT[:, hd * 4 + dc, :], start=(dc == 0), stop=(dc == 3)),
                                 reads=["B", "ring2"], writes=[pk])
                        P.dve(lambda e, pt=pt: e.reduce_max(out=mx, in_=pt[:, 0:256], axis=AX.X), reads=[pk], writes=["mx"])
                        P.dve(lambda e: e.tensor_scalar(out=mx, in0=mx, scalar1=-1.0, scalar2=None, op0=ALU.mult), reads=["mx"], writes=["mx"])
                        P.act(lambda e, pt=pt: e.activation(out=pfp, in_=pt[:, 0:256], func=AF.Exp, bias=mx, scale=1.0, accum_out=rs), reads=[pk, "mx"], writes=["stg_o0", "rs"])
                        P.dve(lambda e: e.reciprocal(out=rs, in_=rs), reads=["rs"], writes=["rs"])
                        P.dve(lambda e: e.tensor_scalar(out=pbf, in0=pfp, scalar1=rs, scalar2=None, op0=ALU.mult), reads=["stg_o0", "rs"], writes=["stg_o0"])
                        ptr, ptk = newps()
                        ptrb = ptr[:].bitcast(BF16)
                        for mt in range(2):
                            P.pe(lambda e, mt=mt, ptrb=ptrb: e.transpose(out=ptrb[:, mt * 128:(mt + 1) * 128], in_=pbf[:, mt * 128:(mt + 1) * 128], identity=identb[:]), reads=["stg_o0", "identb"], writes=[ptk])
                        P.act(lambda e, ptrb=ptrb, t4=t4: e.copy(out=pT[:, :, t4 * 128:(t4 + 1) * 128], in_=ptrb[:, 0:256].rearrange("p (m t) -> p m t", m=2)), reads=[ptk], writes=["stg_o1"])
                    for dc in range(4):
                        pt, pk = newps()
                        for mt in range(2):
                            P.pe(lambda e, pt=pt, mt=mt, hd=hd, dc=dc: e.matmul(pt[:], lhsT=mvb[:, mt, hd * 512 + dc * 128:hd * 512 + (dc + 1) * 128], rhs=pT[:, mt, :], start=(mt == 0), stop=(mt == 1)),
                                 reads=["ring2", "stg_o1"], writes=[pk])
                        if dc % 2 == 0:
                            P.act(lambda e, pt=pt, hd=hd, dc=dc, tg=tg: e.copy(out=A[:, hd * 4 + dc, tg * 512:(tg + 1) * 512], in_=pt[:]), reads=[pk], writes=["A"])
                        else:
                            P.dve(lambda e, pt=pt, hd=hd, dc=dc, tg=tg: e.tensor_copy(out=A[:, hd * 4 + dc, tg * 512:(tg + 1) * 512], in_=pt[:]), reads=[pk], writes=["A"])
            if do_sample and _on("T"):
                P.barrier(bar_tiles)
                Bf = B[:].rearrange("p a b -> p (a b)")
                Kb = [Bf[:, 0:4096].bitcast(F32), Bf[:, 4096:8192].bitcast(F32)]
                Vb = [Bf[:, 8192:10240], Bf[:, 10240:12288]]
                xm = Bf[:, 12288:14336]
                sc_all = Bf[:, 14336:14592].bitcast(F32)
                junkK = Bf[:, 14592:15616].bitcast(F32)
                scT = Bf[:, 15616:16128].bitcast(F32)
                pTs = Bf[:, 16128:16384].bitcast(F32)
                Pm = A[:, 0:2, :].rearrange("p a b -> p (a b)")
                for s_ in range(16):
                    P.dve(lambda e, s_=s_: e.tensor_scalar(out=xm[0:16, :], in0=xq_s[0:16, :], scalar1=ident32[0:16, s_:s_ + 1], scalar2=None, op0=ALU.mult), reads=["xq_s", "ident32"], writes=["xm"])
                    qb = []
                    for h in range(4):
                        pt, pk = newps()
                        P.pe(lambda e, pt=pt, h=h: e.matmul(pt[:], lhsT=onesb[0:16, :], rhs=xm[0:16, h * 512:(h + 1) * 512], start=True, stop=True), reads=["onesb", "xm"], writes=[pk])
                        qb.append((pt, pk))
                    for mt in range(2):
                        i_ = (s_ * 2 + mt) % 2
                        P.dma("sp", "Kb%d" % i_, Kb[i_], ck[s_, mt * 128:(mt + 1) * 128, :], writes=["Kb%d" % i_])
                        for h in range(4):
                            col = mt * 64 + s_ * 4 + h
                            P.dve(lambda e, i_=i_, h=h, col=col, pt=qb[h][0]: e.scalar_tensor_tensor(out=junkK, in0=Kb[i_][:, h * 512:(h + 1) * 512], scalar=1.0, in1=pt[:], op0=ALU.mult, op1=ALU.mult, accum_out=sc_all[:, col:col + 1]),
                                  reads=["Kb%d" % i_, qb[h][1]], writes=["junkK", "sc_all"])
                pt, pk = newps()
                for mt in range(2):
                    P.pe(lambda e, mt=mt, pt=pt: e.transpose(out=pt[0:64, mt * 128:(mt + 1) * 128], in_=sc_all[:, mt * 64:(mt + 1) * 64], identity=ident32[:]), reads=["sc_all", "ident32"], writes=[pk])
                mxs = small[0:64, 20:21]; rss = small[0:64, 21:22]
                P.dve(lambda e, pt=pt: e.reduce_max(out=mxs, in_=pt[0:64, 0:256], axis=AX.X), reads=[pk], writes=["mxs"])
                P.dve(lambda e: e.tensor_scalar(out=mxs, in0=mxs, scalar1=-1.0, scalar2=None, op0=ALU.mult), reads=["mxs"], writes=["mxs"])
                P.act(lambda e, pt=pt: e.activation(out=scT[0:64, :], in_=pt[0:64, 0:256], func=AF.Exp, bias=mxs, scale=1.0, accum_out=rss), reads=[pk, "mxs"], writes=["scT", "rss"])
                P.dve(lambda e: e.reciprocal(out=rss, in_=rss), reads=["rss"], writes=["rss"])
                P.dve(lambda e: e.tensor_scalar(out=scT[0:64, :], in0=scT[0:64, :], scalar1=rss, scalar2=None, op0=ALU.mult), reads=["scT", "rss"], writes=["scT"])
                pt, pk = newps()
                for mt in range(2):
                    P.pe(lambda e, mt=mt, pt=pt: e.transpose(out=pt[:, mt * 64:(mt + 1) * 64], in_=scT[0:64, mt * 128:(mt + 1) * 128], identity=ident32[0:64, 0:64]), reads=["scT", "ident32"], writes=[pk])
                P.dve(lambda e, pt=pt: e.tensor_copy(out=pTs, in_=pt[:, 0:128]), reads=[pk], writes=["pTs"])
                Pm5 = Kb[0].bitcast(BF16)[:, 0:2048].rearrange("p (mt h s c) -> p mt h s c", mt=2, h=4, s=16)
                pTs4 = pTs.rearrange("p (mt s h) -> p mt s h", mt=2, s=16)
                eyebc3b = eyebc[:].rearrange("p (s c) -> p s c", s=16)
                for mt in range(2):
                    for h in range(4):
                        P.dve(lambda e, mt=mt, h=h: e.tensor_tensor(out=Pm5[:, mt, h], in0=pTs4[:, mt, :, h].unsqueeze(2).to_broadcast([128, 16, 16]), in1=eyebc3b, op=ALU.mult),
                              reads=["pTs", "eyebc", "Kb0"], writes=["Kb0"])
                ps_state["pool"] = [0, 1, 2, 3]; ps_state["i"] = 0
                for s_ in range(16):
                    for mt in range(2):
                        i_ = (s_ * 2 + mt) % 2
                        P.dma("pool", "Vb%d" % i_, Vb[i_], cv[s_, mt * 128:(mt + 1) * 128, :], writes=["Vb%d" % i_] + (["B"] if s_ == 0 else []))
                        for h in range(4):
                            P.pe(lambda e, s_=s_, mt=mt, h=h, i_=i_: e.matmul(PS[4 + h][0:16, :], lhsT=Pm5[:, mt, h, s_, :], rhs=Vb[i_][:, h * 512:(h + 1) * 512], start=(s_ == 0 and mt == 0), stop=(s_ == 15 and mt == 1)),
                                 reads=["Kb0", "Vb%d" % i_], writes=["ps%d" % (4 + h)])
                os_tm = Kb[1].bitcast(BF16)[:, 0:2048]
                for h in range(4):
                    P.act(lambda e, h=h: e.copy(out=os_tm[0:16, h * 512:(h + 1) * 512], in_=PS[4 + h][0:16, :]), reads=["ps%d" % (4 + h)], writes=["Kb1"])
                ps_state["pool"] = list(range(8)); ps_state["i"] = 0
                pt, pk = newps()
                ptb = pt[:].bitcast(BF16)
                for kc in range(16):
                    P.pe(lambda e, kc=kc, ptb=ptb: e.transpose(out=ptb[:, kc * 16:(kc + 1) * 16], in_=os_tm[0:16, kc * 128:(kc + 1) * 128], identity=identb[0:16, 0:16]), reads=["Kb1", "identb"], writes=[pk])
                P.dve(lambda e, ptb=ptb: e.tensor_copy(out=Ys[:].rearrange("p a b -> p (a b)"), in_=ptb[:, 0:256]), reads=[pk], writes=["Ys"])
                P.barrier(bar_tiles)
            proj_residual(w_xa_o, A, "A")

            _stop("E")
            ring_state["slots"] = [0, 1, 2]; ring_state["n"] = 2
            load_gbc(g_ffn, gbc, "gbc")
            for t in range(8):
                norm_tile(xres[:, t, :], "xres%d" % t, gbc, "gbc", B[:, :, t * 128:(t + 1) * 128], "B", junk, xn_tm, small[:, 0:1], "ss0")
            if do_sample:
                SCR.reset()
                gbc = SCR.f32(2048); xn_tm = SCR.bf16(2048); junk = xn_tm
                sq3_f = SCR.f32(16, 16)
                sgs = SCR.f32(64); hs = SCR.bf16(4, 16)
                norm_fm(32, sq3_f)
            hT = [A[:, 0:4, :], A[:, 4:8, :]]
            sg = A[:, 8:10, :].rearrange("p a b -> p (a b)").bitcast(F32)
            for grp in range(11):
                cs0 = grp * 512
                slg, rkg = ring_unit([(lambda s: s.rearrange("p (k n) -> p k n", k=16), w_g[:, cs0:cs0 + 512].rearrange("(k p) n -> p k n", p=128))])
                slu, rku = ring_unit([(lambda s: s.rearrange("p (k n) -> p k n", k=16), w_u[:, cs0:cs0 + 512].rearrange("(k p) n -> p k n", p=128))])
                sld, rkd = ring_unit([(lambda s: s.rearrange("p (c n) -> p c n", c=4), w_d[cs0:cs0 + 512, :].rearrange("(c p) n -> p c n", p=128))])
                wgv = slg.rearrange("p (k n) -> p k n", k=16); wuv = slu.rearrange("p (k n) -> p k n", k=16); wdv = sld.rearrange("p (c n) -> p c n", c=4)
                hb = hT[grp % 2]; hk = "hT%d" % (grp % 2)
                for dc in range(4):
                    for tg in range(2):
                        pg, pgk = newps()
                        for kc in range(16):
                            P.pe(lambda e, kc=kc, pg=pg, dc=dc, tg=tg, wgv=wgv: e.matmul(pg[:], lhsT=wgv[:, kc, dc * 128:(dc + 1) * 128], rhs=B[:, kc, tg * 512:(tg + 1) * 512], start=(kc == 0), stop=(kc == 15)),
                                 reads=[rkg, "B"], writes=[pgk])
                        pu, puk = newps()
                        for kc in range(16):
                            P.pe(lambda e, kc=kc, pu=pu, dc=dc, tg=tg, wuv=wuv: e.matmul(pu[:], lhsT=wuv[:, kc, dc * 128:(dc + 1) * 128], rhs=B[:, kc, tg * 512:(tg + 1) * 512], start=(kc == 0), stop=(kc == 15)),
                                 reads=[rku, "B"], writes=[puk])
                        sgi = (dc * 2 + tg) % 2
                        sgv = sg[:, sgi * 512:(sgi + 1) * 512]
                        P.act(lambda e, pg=pg, sgv=sgv: e.activation(out=sgv, in_=pg[:], func=AF.Silu), reads=[pgk], writes=["sg%d" % sgi] + (["A"] if grp == 0 and dc == 0 else []))
                        P.dve(lambda e, pu=pu, sgv=sgv, hb=hb, dc=dc, tg=tg: e.tensor_tensor(out=hb[:, dc, tg * 512:(tg + 1) * 512], in0=sgv, in1=pu[:], op=ALU.mult), reads=["sg%d" % sgi, puk], writes=[hk] + (["A"] if grp < 2 and dc == 0 and tg == 0 else []))
                if do_sample and _on("F"):
                    pt, pk = newps()
                    for gu, wv_, rk_ in ((0, wgv, rkg), (1, wuv, rku)):
                        for dc in range(4):
                            for kc in range(16):
                                P.pe(lambda e, kc=kc, pt=pt, gu=gu, dc=dc, wv_=wv_: e.matmul(pt[:, (gu * 4 + dc) * 16:(gu * 4 + dc + 1) * 16], lhsT=wv_[:, kc, dc * 128:(dc + 1) * 128], rhs=Xs[:, kc, :], start=(kc == 0), stop=(kc == 15)),
                                     reads=[rk_, "Xs"], writes=[pk])
                    P.act(lambda e, pt=pt: e.activation(out=sgs, in_=pt[:, 0:64], func=AF.Silu), reads=[pk], writes=["sgs"])
                    P.dve(lambda e, pt=pt: e.tensor_tensor(out=hs[:].rearrange("p a b -> p (a b)"), in0=sgs, in1=pt[:, 64:128], op=ALU.mult), reads=["sgs", pk], writes=["hs"])
                    pt2, pk2 = newps()
                    for f in range(16):
                        for dc in range(4):
                            P.pe(lambda e, pt2=pt2, f=f, dc=dc, wdv=wdv: e.matmul(pt2[:, f * 16:(f + 1) * 16], lhsT=wdv[:, dc, f * 128:(f + 1) * 128], rhs=hs[:, dc, :], start=(dc == 0), stop=(dc == 3)),
                                 reads=[rkd, "hs"], writes=[pk2])
                    P.dve(lambda e, pt2=pt2: e.tensor_tensor(out=xrs2, in0=xrs2, in1=pt2[:, 0:256], op=ALU.add), reads=[pk2, "xrs"], writes=["xrs"])
                for t in range(8):
                    for ng in range(4):
                        pt, pk = newps()
                        for dc in range(4):
                            P.pe(lambda e, pt=pt, dc=dc, t=t, ng=ng, hb=hb, wdv=wdv: e.matmul(pt[:], lhsT=hb[:, dc, t * 128:(t + 1) * 128], rhs=wdv[:, dc, ng * 512:(ng + 1) * 512], start=(dc == 0), stop=(dc == 3)),
                                 reads=[hk, rkd], writes=[pk])
                        P.dve(lambda e, pt=pt, t=t, ng=ng: e.tensor_tensor(out=xres[:, t, ng * 512:(ng + 1) * 512], in0=xres[:, t, ng * 512:(ng + 1) * 512], in1=pt[:], op=ALU.add),
                              reads=[pk, "xres%d" % t], writes=["xres%d" % t])

            _stop("F")
            load_gbc(g_final, gbc, "gbc")
            ystg = [B[:, 0:4, :].rearrange("p a b -> p (a b)").bitcast(F32), B[:, 4:8, :].rearrange("p a b -> p (a b)").bitcast(F32)]
            for t in range(8):
                ssx = small[:, 16 + (t % 2):17 + (t % 2)]
                sk = "ssx%d" % (t % 2)
                yb = ystg[t % 2]
                yk = "ystg%d" % (t % 2)
                P.act(lambda e, t=t, ssx=ssx: e.activation(out=junk, in_=xres[:, t, :], func=AF.Square, accum_out=ssx), reads=["xres%d" % t], writes=["xn_tm", sk])
                rstd_from_ss(ssx, D, sk)
                P.dve(lambda e, t=t, ssx=ssx, yb=yb: e.scalar_tensor_tensor(out=yb, in0=xres[:, t, :], scalar=ssx, in1=gbc, op0=ALU.mult, op1=ALU.mult),
                      reads=["xres%d" % t, sk, "gbc"], writes=[yk, "B"])
                P.dma("sp", yk, yo[t * 128:(t + 1) * 128, :], yb, reads=[yk])


            if do_sample and _on("G"):
                sq3_g = SCR.f32(16, 16)
                yfm = SCR.f32(16, 16)
                norm_fm(48, sq3_g)
                P.dve(lambda e: e.tensor_tensor(out=yfm, in0=sq3_g, in1=gfm[:, 48:64].unsqueeze(2).to_broadcast([128, 16, 16]), op=ALU.mult), reads=["sq3", "gfm", "Xs"], writes=["yfm"])
                ys_tm = B[:, 8:12, :].rearrange("p a b -> p (a b)").bitcast(F32)
                for q4 in range(4):
                    pt, pk = newps()
                    for cc in range(4):
                        kc = q4 * 4 + cc
                        P.pe(lambda e, pt=pt, cc=cc, kc=kc: e.transpose(out=pt[0:16, cc * 128:(cc + 1) * 128], in_=yfm[:, kc, :], identity=ident32[:]), reads=["yfm", "ident32"], writes=[pk])
                    P.act(lambda e, pt=pt, q4=q4: e.copy(out=ys_tm[0:16, q4 * 512:(q4 + 1) * 512], in_=pt[0:16, :]), reads=[pk], writes=["ys_tm", "B"])
                P.dma("sp", "ys_tm", ys[:, :], ys_tm[0:16, :], reads=["ys_tm"])
        except _Stop:
            pass
        P.emit()
        print("ops", len(P.ops), "waits", P.n_waits, flush=True)
    return nc


_CONST = {}


def _consts():
    if not _CONST:
        _CONST["ident"] = np.eye(128, dtype=np.float32)
        _CONST["causal"] = np.triu(np.ones((128, 128), dtype=np.float32))
        _CONST["eyebc"] = np.ascontiguousarray(np.broadcast_to(np.eye(16, dtype=np.float32).reshape(1, 256), (128, 256)))
    return _CONST


def make_in_maps(inp):
    f = lambda a: np.ascontiguousarray(np.asarray(a, dtype=np.float32))
    cst = _consts()
    shared = {}
    for k in ["g_mix", "w_in", "conv_w", "conv_b", "w_rg_a", "b_rg_a", "w_rg_x", "b_rg_x", "rg_lambda", "b_ml_i", "b_ml_f",
              "g_rg_out", "g_ml_out", "w_out", "g_xa", "g_mem", "w_xa_q", "w_xa_k", "w_xa_v", "w_xa_o", "g_ffn",
              "w_ffn_gate", "w_ffn_up", "w_ffn_down"]:
        shared[k] = f(np.asarray(inp[k])[0])
    shared["g_final"] = f(inp["g_final"])
    shared.update(cst)
    xpr = np.asarray(inp["x_prompt"]); xsm = np.asarray(inp["x_sample"]); memp = np.asarray(inp["mem_prompt"])
    maps = []
    for c in range(8):
        j, h = c // 2, c % 2
        m = dict(shared)
        m["xo"] = f(xpr[j, h * NT:(h + 1) * NT])
        m["xp"] = f(xpr[j, 0:NT])
        m["xs"] = f(xsm[c * NS:(c + 1) * NS, 0])
        m["mask"] = np.full((128, 1), float(h), dtype=np.float32)
        m["mem"] = f(memp[j])
        sl = slice(c * NS, (c + 1) * NS)
        m["st_h"] = f(np.asarray(inp["state_rg_h"])[0, sl])
        m["st_conv"] = f(np.asarray(inp["state_rg_conv"])[0, sl].reshape(NS, 3072))
        m["st_C"] = f(np.asarray(inp["state_ml_C"])[0, sl])
        m["st_n"] = f(np.asarray(inp["state_ml_n"])[0, sl].reshape(NS, 512))
        m["st_m"] = f(np.asarray(inp["state_ml_m"])[0, sl])
        m["ck"] = f(np.asarray(inp["cache_mem_k"])[0, sl].reshape(NS, 256, D))
        m["cv"] = f(np.asarray(inp["cache_mem_v"])[0, sl].reshape(NS, 256, D))
        maps.append(m)
    return maps


_NC = {}


def kernel(**inputs):
    if "nc" not in _NC:
        _NC["nc"] = build(do_sample=os.environ.get("KNOSAMPLE") is None)
    nc = _NC["nc"]
    maps = make_in_maps(inputs)
    res = run_bass_kernel_spmd(nc, maps, core_ids=list(range(8)))
    R = res.results
    y_prompt = np.zeros((4, 2048, D), np.float32)
    for c in range(8):
        j, h = c // 2, c % 2
        y_prompt[j, h * NT:(h + 1) * NT] = R[c]["yo"]
    y_sample = np.concatenate([R[c]["ys"] for c in range(8)], axis=0).reshape(128, 1, D)
    odd = [R[2 * j + 1] for j in range(4)]
    p_rg_h = np.stack([r["o_ph"].T.reshape(1024) for r in odd])[None]
    p_rg_conv = np.stack([r["o_pconv"].reshape(128, 8, 3).transpose(2, 1, 0).reshape(3, 1024) for r in odd])[None]
    pC = np.stack([r["o_pC"].reshape(128, 4, 257).transpose(1, 0, 2) for r in odd])
    p_ml_C = np.ascontiguousarray(pC[..., 0:256])[None]
    p_ml_n = np.ascontiguousarray(pC[..., 256])[None]
    p_ml_m = np.stack([r["o_pm"].reshape(4) for r in odd])[None]
    p_mem_k = np.stack([R[2 * j]["o_mk"].reshape(256, 4, 512) for j in range(4)])[None]
    p_mem_v = np.stack([R[2 * j]["o_mv"].reshape(256, 4, 512) for j in range(4)])[None]
    cat = lambda k, shp: np.concatenate([R[c][k] for c in range(8)], axis=0).reshape(shp)
    s_rg_h = cat("o_sh", (1, 128, 1024))
    s_rg_conv = cat("o_sconv", (1, 128, 3, 1024))
    s_ml_C = cat("o_sC", (1, 128, 4, 128, 256))
    s_ml_n = cat("o_sn", (1, 128, 4, 128))
    s_ml_m = cat("o_sm", (1, 128, 4))
    return (y_prompt, y_sample, p_rg_h, p_rg_conv, p_ml_C, p_ml_n, p_ml_m, p_mem_k, p_mem_v,
            s_rg_h, s_rg_conv, s_ml_C, s_ml_n, s_ml_m)
```

```python
import os
import numpy as np
import concourse.bass as bass
import concourse.mybir as mybir
from concourse.bass_utils import run_bass_kernel_spmd
from contextlib import ExitStack

F32 = mybir.dt.float32
BF16 = mybir.dt.bfloat16
AF = mybir.ActivationFunctionType
ALU = mybir.AluOpType
AX = mybir.AxisListType

COMPUTE = ("pe", "act", "dve", "pool")
ALL_ENG = ("pe", "act", "dve", "pool", "sp")

D = 2048
NT = 1024
NS = 16
IN_W = 5128
DFF = 5632
EPS = 1e-6
OFF_Q, OFF_K, OFF_V, OFF_O, OFF_I = 2048, 2560, 3072, 4096, 5120


class _Stop(Exception):
    pass


def _on(tag):
    return tag not in os.environ.get("KS_OFF", "")


def _stop(tag):
    if os.environ.get("KSTOP", "ALL") == tag:
        raise _Stop()


class Op:
    __slots__ = ("eng", "fn", "deps", "sig", "dkey", "dval", "waits", "has_dep", "idx")


class Prog:
    def __init__(self, nc, stack):
        self.nc = nc
        self.stack = stack
        self.ops = []
        self.lw = {}
        self.rd = {}
        self.dma_cum = {}
        self.dma_sems = {}
        self.nbar = 0

    def add(self, eng, fn, reads=(), writes=(), dkey=None):
        i = len(self.ops)
        psr = [r for r in reads if isinstance(r, str) and r.startswith("ps") and r[2:].isdigit()]
        if psr:
            reads = [r for r in reads if r not in psr]
            writes = list(writes) + psr
        deps = set()
        for r in reads:
            w = self.lw.get(r)
            if w is not None:
                deps.add(w)
        for r in writes:
            w = self.lw.get(r)
            if w is not None:
                deps.add(w)
            rr = self.rd.get(r)
            if rr:
                deps.update(rr.values())
        op = Op()
        op.eng, op.fn, op.deps, op.sig, op.dkey, op.dval, op.has_dep, op.idx = eng, fn, deps, 0, dkey, 0, False, i
        if dkey is not None:
            self.dma_cum[dkey] = self.dma_cum.get(dkey, 0) + 16
            op.dval = self.dma_cum[dkey]
        rkey = eng if dkey is None else ("dma", dkey)
        for r in reads:
            self.rd.setdefault(r, {})[rkey] = i
        for r in writes:
            self.lw[r] = i
            self.rd[r] = {}
        self.ops.append(op)
        return i

    def pe(self, fn, reads=(), writes=()):
        return self.add("pe", fn, reads, writes)

    def act(self, fn, reads=(), writes=()):
        return self.add("act", fn, reads, writes)

    def dve(self, fn, reads=(), writes=()):
        return self.add("dve", fn, reads, writes)

    def dma(self, q, key, out, in_, reads=(), writes=()):
        return self.add(q, lambda e: e.dma_start(out=out, in_=in_), reads, writes, dkey=key)

    def barrier(self, bar_tiles):
        n = self.nbar
        self.nbar += 1
        engs = ("pe", "act", "dve", "sp")
        def tiny(e):
            if e == "dve":
                t = bar_tiles[e]
                return lambda en, t=t: en.memset(t, 0.0)
            if e == "act":
                t, t2 = bar_tiles[e], bar_tiles["act_src"]
                return lambda en, t=t, t2=t2: en.copy(out=t, in_=t2)
            if e == "pe":
                pt_, o32 = bar_tiles["pe_ps"], bar_tiles["ones32"]
                return lambda en: en.matmul(pt_, lhsT=o32, rhs=o32, start=True, stop=True)
            return lambda en: en.nop()
        pkey = bar_tiles["pe_key"]
        extra = {"pe": ([], [pkey]), "act": (["bart_src"], ["bart_act"]), "dve": ([], ["bart_dve"]), "sp": ([], [])}
        for e in engs:
            self.add(e, tiny(e), reads=extra[e][0], writes=["bar%d_%s" % (n, e)] + extra[e][1])
        for e in engs:
            self.add(e, tiny(e), reads=["bar%d_%s" % (n, x) for x in engs] + extra[e][0], writes=["bar%d_%s_b" % (n, e)] + extra[e][1])

    def emit(self):
        nc = self.nc
        ops = self.ops
        for op in ops:
            nd = set()
            for d in op.deps:
                p = ops[d]
                if p.dkey is None and op.dkey is None and p.eng == op.eng and p.eng == "pe":
                    continue
                nd.add(d)
                p.has_dep = True
            op.deps = nd
        cnt = {e: 0 for e in ALL_ENG}
        for op in ops:
            if op.dkey is None and op.has_dep:
                cnt[op.eng] += 1
                op.sig = cnt[op.eng]
        sems = {e: self.stack.enter_context(nc.semaphore("s_" + e)) for e in ALL_ENG}
        for k in self.dma_cum:
            self.dma_sems[k] = self.stack.enter_context(nc.semaphore("d_" + str(k)))
        known = {e: {} for e in ALL_ENG}
        for op in ops:
            need = {}
            for d in op.deps:
                p = ops[d]
                if p.dkey is not None:
                    sk, v = ("d", p.dkey), p.dval
                else:
                    sk, v = ("c", p.eng), p.sig
                if v > need.get(sk, 0):
                    need[sk] = v
            kn = known[op.eng]
            waits = []
            for sk, v in need.items():
                if kn.get(sk, 0) >= v:
                    continue
                kn[sk] = v
                waits.append((sk, v))
            op.waits = waits
        self.n_waits = sum(len(o.waits) for o in ops)

        def emit_engine(ename, eobj):
            for op in ops:
                if op.eng != ename:
                    continue
                for sk, v in op.waits:
                    s = self.dma_sems[sk[1]] if sk[0] == "d" else sems[sk[1]]
                    eobj.wait_ge(s, v)
                ins = op.fn(eobj)
                if op.dkey is not None:
                    ins.then_inc(self.dma_sems[op.dkey], 16)
                elif op.sig:
                    ins.then_inc(sems[op.eng], 1)
            if ename == "sp":
                for k, v in self.dma_cum.items():
                    eobj.wait_ge(self.dma_sems[k], v)

        with nc.Block() as block:
            @block.tensor
            def _(e):
                emit_engine("pe", e)

            @block.scalar
            def _(e):
                emit_engine("act", e)

            @block.vector
            def _(e):
                emit_engine("dve", e)

            @block.gpsimd
            def _(e):
                emit_engine("pool", e)

            @block.sync
            def _(e):
                emit_engine("sp", e)


class Reg:
    def __init__(self, t, nwords):
        self.t, self.n, self.off = t, nwords, 0

    def reset(self):
        self.off = 0

    def _take(self, w):
        assert self.off + w <= self.n, ("region overflow", self.off, w, self.n)
        ap = self.t[:, self.off:self.off + w]
        self.off += w
        return ap

    @staticmethod
    def _shape(ap, shape):
        if len(shape) == 1:
            return ap
        if len(shape) == 2:
            return ap.rearrange("p (a b) -> p a b", a=shape[0])
        if len(shape) == 3:
            return ap.rearrange("p (a b c) -> p a b c", a=shape[0], b=shape[1])
        raise ValueError(shape)

    def f32(self, *shape):
        n = int(np.prod(shape))
        return self._shape(self._take(n), shape)

    def bf16(self, *shape):
        n = int(np.prod(shape))
        w = (n + 1) // 2
        return self._shape(self._take(w).bitcast(BF16)[:, 0:n], shape)


def build(do_sample=True):
    nc = bass.Bass("TRN2", target_bir_lowering=False)

    def din(name, shape):
        return nc.dram_tensor(name, list(shape), F32, kind="ExternalInput").ap()

    def dout(name, shape):
        return nc.dram_tensor(name, list(shape), F32, kind="ExternalOutput").ap()

    xo = din("xo", [NT, D]); xp = din("xp", [NT, D]); xs_in = din("xs", [NS, D])
    mask_in = din("mask", [128, 1]); mem = din("mem", [256, D])
    st_h = din("st_h", [NS, 1024]); st_conv = din("st_conv", [NS, 3072])
    st_C = din("st_C", [NS, 4, 128, 256]); st_n = din("st_n", [NS, 512]); st_m = din("st_m", [NS, 4])
    ck = din("ck", [NS, 256, D]); cv = din("cv", [NS, 256, D])
    g_mix = din("g_mix", [D]); w_in = din("w_in", [D, IN_W]); conv_w = din("conv_w", [4, 1024])
    conv_b = din("conv_b", [1024]); w_rg_a = din("w_rg_a", [8, 128, 128]); b_rg_a = din("b_rg_a", [1024])
    w_rg_x = din("w_rg_x", [8, 128, 128]); b_rg_x = din("b_rg_x", [1024]); rg_lambda = din("rg_lambda", [1024])
    b_ml_i = din("b_ml_i", [4]); b_ml_f = din("b_ml_f", [4]); g_rg_out = din("g_rg_out", [1024])
    g_ml_out = din("g_ml_out", [1024]); w_out = din("w_out", [D, D]); g_xa = din("g_xa", [D]); g_mem = din("g_mem", [D])
    w_xa_q = din("w_xa_q", [D, D]); w_xa_k = din("w_xa_k", [D, D]); w_xa_v = din("w_xa_v", [D, D]); w_xa_o = din("w_xa_o", [D, D])
    g_ffn = din("g_ffn", [D]); w_g = din("w_ffn_gate", [D, DFF]); w_u = din("w_ffn_up", [D, DFF]); w_d = din("w_ffn_down", [DFF, D])
    g_final = din("g_final", [D])
    ident_in = din("ident", [128, 128]); causal_in = din("causal", [128, 128]); eyebc_in = din("eyebc", [128, 256])

    yo = dout("yo", [NT, D]); ys = dout("ys", [NS, D])
    o_ph = dout("o_ph", [128, 8]); o_pconv = dout("o_pconv", [128, 24]); o_pC = dout("o_pC", [128, 4 * 257]); o_pm = dout("o_pm", [4, 1])
    o_mk = dout("o_mk", [256, D]); o_mv = dout("o_mv", [256, D])
    o_sh = dout("o_sh", [NS, 1024]); o_sconv = dout("o_sconv", [NS, 3072]); o_sC = dout("o_sC", [NS, 4, 128, 256])
    o_sn = dout("o_sn", [NS, 512]); o_sm = dout("o_sm", [NS, 4])

    st = ExitStack()
    with st:
        P = Prog(nc, st)
        SB = lambda name, shape, dt=F32: st.enter_context(nc.sbuf_tensor("sb_" + name, list(shape), dt))
        A = SB("A", [128, 16, NT], BF16)
        B = SB("B", [128, 16, NT], BF16)
        RXt = SB("RX", [128, 16384], F32)
        RINGt = SB("RING", [128, 3, 8192], BF16)
        NSCR = 5132
        SCRt = SB("SCR", [128, NSCR], F32)
        ident32 = SB("ident32", [128, 128]); identb = SB("identb", [128, 128], BF16)
        causal = SB("causal", [128, 128]); eyebc = SB("eyebc", [128, 256])
        ones32 = SB("ones32", [128, 128]); onesb = SB("onesb", [128, 128], BF16)
        prm = SB("prm", [128, 72]); cl = SB("cl", [128, 8]); gfm = SB("gfm", [128, 64])
        maskt = SB("maskt", [128, 1])
        wgt = SB("wgt", [128, 8, 2, 128], BF16)
        wif = SB("wif", [128, 16, 8], BF16)
        bart = SB("bart", [128, 4])
        ph_sb = SB("ph_sb", [128, 8]); pconv_sb = SB("pconv_sb", [128, 8, 3]); pm_sb = SB("pm_sb", [4, 1])
        small = SB("small", [128, 64])
        PS = [st.enter_context(nc.psum_tensor("ps%d" % i, [128, 512], F32)) for i in range(8)]
        bar_tiles = {"act": bart[:, 0:1], "dve": bart[:, 1:2], "act_src": bart[:, 2:3]}
        P.dve(lambda e: e.memset(bart[:], 0.0), writes=["bart_act", "bart_dve", "bart_src", "bart3"])
        bar_tiles["pe_ps"] = PS[7][0:1, 511:512]
        bar_tiles["ones32"] = ones32[0:1, 0:1]
        bar_tiles["pe_key"] = "ps7"

        RX = Reg(RXt, 16384)
        NSMP = 776
        SMPp = Reg(SCRt[:, 0:NSMP], NSMP)
        SCR = Reg(SCRt[:, NSMP:NSCR], NSCR - NSMP)
        xrs = SMPp.f32(16, 16)
        Xs = SMPp.bf16(16, 16)
        Ys = SMPp.bf16(16, 16)
        zsT = SMPp.f32(8, 16); zgT = SMPp.f32(8, 16)
        lif_s = SMPp.f32(8)
        xrs2 = xrs.rearrange("p a b -> p (a b)")
        srs = small[:, 32:48]

        def norm_fm(gc0, sq3):
            sq2 = sq3.rearrange("p a b -> p (a b)")
            P.dve(lambda e: e.tensor_tensor(out=sq2, in0=xrs2, in1=xrs2, op=ALU.mult), reads=["xrs"], writes=["sq3"])
            pt, pk = newps()
            for kc in range(16):
                P.pe(lambda e, kc=kc, pt=pt: e.matmul(pt[:, 0:16], lhsT=ones32[:], rhs=sq3[:, kc, :], start=(kc == 0), stop=(kc == 15)), reads=["ones32", "sq3"], writes=[pk])
            P.act(lambda e, pt=pt: e.activation(out=srs, in_=pt[:, 0:16], func=AF.Sqrt, bias=EPS, scale=1.0 / D), reads=[pk], writes=["srs"])
            P.dve(lambda e: e.reciprocal(out=srs, in_=srs), reads=["srs"], writes=["srs"])
            P.dve(lambda e: e.tensor_tensor(out=sq3, in0=xrs, in1=srs.unsqueeze(1).to_broadcast([128, 16, 16]), op=ALU.mult), reads=["xrs", "srs", "sq3"], writes=["sq3"])
            P.dve(lambda e: e.tensor_tensor(out=Xs, in0=sq3, in1=gfm[:, gc0:gc0 + 16].unsqueeze(2).to_broadcast([128, 16, 16]), op=ALU.mult), reads=["sq3", "gfm"], writes=["Xs"])
        xres = RXt[:, :].rearrange("p (t n) -> p t n", t=8)

        ps_state = {"pool": list(range(8)), "i": 0}

        def newps():
            pool = ps_state["pool"]
            b = pool[ps_state["i"] % len(pool)]
            ps_state["i"] += 1
            return PS[b], "ps%d" % b

        ring_state = {"n": 0, "slots": [0, 1, 2]}

        def ring_unit(parts):
            sl_ = ring_state["slots"]
            s = sl_[ring_state["n"] % len(sl_)]
            ring_state["n"] += 1
            slot = RINGt[:, s, :]
            key = "ring%d" % s
            for dst_fn, src in parts:
                P.dma("pool", key, dst_fn(slot), src, writes=[key])
            return slot, key

        P.dma("sp", "c_ident", ident32[:], ident_in[:, :], writes=["ident32"])
        P.dma("sp", "c_causal", causal[:], causal_in[:, :], writes=["causal"])
        P.dma("sp", "c_eyebc", eyebc[:], eyebc_in[:, :], writes=["eyebc"])
        P.dma("sp", "c_mask", maskt[:], mask_in[:, :], writes=["mask"])
        P.dve(lambda e: e.tensor_copy(out=identb[:], in_=ident32[:]), reads=["ident32"], writes=["identb"])
        P.dve(lambda e: e.memset(ones32[:], 1.0), writes=["ones32"])
        P.dve(lambda e: e.memset(onesb[:], 1.0), writes=["onesb"])
        P.dma("pool", "c_wga", wgt[:, :, 0, :], w_rg_a.rearrange("c k d -> k c d"), writes=["wgt"])
        P.dma("pool", "c_wgx", wgt[:, :, 1, :], w_rg_x.rearrange("c k d -> k c d"), writes=["wgt"])
        P.dma("pool", "c_wif", wif[:], w_in[:, OFF_I:OFF_I + 8].rearrange("(k p) n -> p k n", p=128), writes=["wif"])
        SCR.reset()
        prow = RXt[:, 16128:16256]
        P.dma("sp", "c_pr0", prow[0:32, :], conv_w.rearrange("j (c p) -> (j c) p", p=128), writes=["prow"])
        for i, v in enumerate([conv_b, b_rg_a, b_rg_x, rg_lambda, g_rg_out]):
            P.dma("sp", "c_pr%d" % (i + 1), prow[32 + 8 * i:40 + 8 * i, :], v.rearrange("(c p) -> c p", p=128), writes=["prow"])
        pst, psk = newps()
        P.pe(lambda e: e.transpose(out=pst[:, 0:72], in_=prow[0:72, :], identity=ident32[0:72, 0:72]), reads=["prow", "ident32"], writes=[psk])
        P.dve(lambda e: e.tensor_copy(out=prm[:], in_=pst[:, 0:72]), reads=[psk], writes=["prm"])
        grow = RXt[:, 16256:16384]
        for i, v in enumerate([g_mix, g_xa, g_ffn, g_final]):
            P.dma("sp", "c_gr%d" % i, grow[16 * i:16 * i + 16, :], v.rearrange("(c p) -> c p", p=128), writes=["grow"])
        pst2, psk2 = newps()
        P.pe(lambda e: e.transpose(out=pst2[:, 0:64], in_=grow[0:64, :], identity=ident32[0:64, 0:64]), reads=["grow", "ident32"], writes=[psk2])
        P.dve(lambda e: e.tensor_copy(out=gfm[:], in_=pst2[:, 0:64]), reads=[psk2], writes=["gfm"])
        P.act(lambda e: e.activation(out=cl[:], in_=prm[:, 56:64], func=AF.Exp, scale=-1.0), reads=["prm"], writes=["cl"])
        P.act(lambda e: e.activation(out=cl[:], in_=cl[:], func=AF.Ln, bias=1.0, scale=1.0), reads=["cl"], writes=["cl"])
        P.dve(lambda e: e.tensor_scalar(out=cl[:], in0=cl[:], scalar1=-8.0, scalar2=None, op0=ALU.mult), reads=["cl"], writes=["cl"])

        def rstd_from_ss(ss, n, key):
            P.act(lambda e: e.activation(out=ss, in_=ss, func=AF.Sqrt, bias=EPS, scale=1.0 / n), reads=[key], writes=[key])
            P.dve(lambda e: e.reciprocal(out=ss, in_=ss), reads=[key], writes=[key])

        def norm_tile(src, srckey, gbc, gkey, dst3, dstkey, junk, xn_tm, ssap, sskey, rows=128):
            P.act(lambda e: e.activation(out=junk[0:rows, :], in_=src, func=AF.Square, accum_out=ssap[0:rows, :]), reads=[srckey], writes=["xn_tm", sskey])
            rstd_from_ss(ssap[0:rows, :], D, sskey)
            P.dve(lambda e: e.scalar_tensor_tensor(out=xn_tm[0:rows, :], in0=src, scalar=ssap[0:rows, :], in1=gbc[0:rows, :], op0=ALU.mult, op1=ALU.mult),
                  reads=[srckey, sskey, gkey], writes=["xn_tm"])
            for hf in range(2):
                pt, pk = newps()
                ptb = pt[:].bitcast(BF16)
                for k in range(8):
                    kk = hf * 8 + k
                    P.pe(lambda e, k=k, kk=kk, ptb=ptb: e.transpose(out=ptb[:, k * 128:k * 128 + rows], in_=xn_tm[0:rows, kk * 128:(kk + 1) * 128], identity=identb[0:rows, 0:rows]),
                         reads=["xn_tm", "identb"], writes=[pk])
                srcv = ptb[:, 0:1024].rearrange("p (k t) -> p k t", k=8)[:, :, 0:rows]
                if hf == 0:
                    P.act(lambda e, srcv=srcv: e.copy(out=dst3[:, 0:8, :], in_=srcv), reads=[pk], writes=[dstkey])
                else:
                    P.dve(lambda e, srcv=srcv: e.tensor_copy(out=dst3[:, 8:16, :], in_=srcv), reads=[pk], writes=[dstkey])

        def load_gbc(gvec, gbc, key):
            P.dma("sp", "gbc", gbc, gvec.partition_broadcast(128), writes=[key])

        try:
            RX.reset()
            PRE = RX.bf16(16, NT)
            SCR.reset()
            gbc = SCR.f32(2048)
            xn_tm = SCR.bf16(2048)
            junk = xn_tm
            xstage = [RX.f32(2048), RX.f32(2048)]
            load_gbc(g_mix, gbc, "gbc")
            for i in range(16):
                src_d = xp if i < 8 else xo
                t = i % 8
                stg = xstage[i % 2]
                skey = "xstage%d" % (i % 2)
                P.dma("sp", skey, stg, src_d[t * 128:(t + 1) * 128, :], writes=[skey])
                dstbuf, dkey = (PRE, "PRE") if i < 8 else (A, "A")
                norm_tile(stg, skey, gbc, "gbc", dstbuf[:, :, t * 128:(t + 1) * 128], dkey, junk, xn_tm, small[:, 0:1], "ss0")

            if do_sample:
                xs_tm = xstage[0][0:16, :]
                P.dma("sp", "xstage0", xs_tm, xs_in[:, :], writes=["xstage0"])
                pt, pk = newps()
                for kc in range(16):
                    P.pe(lambda e, kc=kc, pt=pt: e.transpose(out=pt[:, kc * 16:(kc + 1) * 16], in_=xs_tm[:, kc * 128:(kc + 1) * 128], identity=ident32[0:16, 0:16]), reads=["xstage0", "ident32"], writes=[pk])
                P.dve(lambda e, pt=pt: e.tensor_copy(out=xrs2, in_=pt[:, 0:256]), reads=[pk], writes=["xrs"])
                sq3_a = SCR.f32(16, 16)
                norm_fm(0, sq3_a)
            _stop("A")
            P.barrier(bar_tiles)
            rgw = Reg(RXt[:, 8192:16384], 8192)
            zbs = [rgw.f32(1027), rgw.f32(1027)]; xr = rgw.f32(1024); xrb = rgw.bf16(1024)
            ssacc = gbc[:, 0:1024]
            rr = rgw.f32(1024); ig = rgw.f32(1024); tmp = rgw.f32(1024); hh = rgw.f32(1024)
            av = rr
            hl = small[:, 1:2]
            ps_state["pool"] = list(range(8)); ps_state["i"] = 0
            w2048 = w_in[:, 0:2048].rearrange("(k p) (g n) -> p k g n", p=128, g=2)
            glb = SCR.bf16(1024)
            hgsq = SCR.bf16(1024)
            glbs = [glb, gbc[:, 1024:1536].bitcast(BF16)]
            units = {}

            def rg_unit(u):
                if u not in units:
                    slot, rk = ring_unit([
                        (lambda s: s.rearrange("p (k g n) -> p k g n", k=16, g=2)[:, :, 0, :], w2048[:, :, 0, u * 256:(u + 1) * 256]),
                        (lambda s: s.rearrange("p (k g n) -> p k g n", k=16, g=2)[:, :, 1, :], w2048[:, :, 1, u * 256:(u + 1) * 256])])
                    units[u] = (slot.rearrange("p (k g n) -> p k g n", k=16, g=2), rk)
                return units[u]

            def H1a(c, pas):
                u, j = c // 2, c % 2
                wu, rk = rg_unit(u)
                X, xkey = (PRE, "PRE") if pas == 0 else (A, "A")
                zb = zbs[pas]; zk = "zb%d" % pas
                if pas == 0:
                    P.dve(lambda e, zb=zb: e.memset(zb[:, 0:3], 0.0), writes=[zk])
                else:
                    P.dve(lambda e, zb=zb: e.tensor_scalar(out=zb[:, 0:3], in0=zbs[0][:, 1024:1027], scalar1=maskt[:, 0:1], scalar2=None, op0=ALU.mult),
                          reads=["zb0", "mask"], writes=[zk])
                for tg in range(2):
                    pt, pk = newps()
                    for kc in range(16):
                        P.pe(lambda e, kc=kc, pt=pt, X=X, tg=tg, j=j, wu=wu: e.matmul(pt[:], lhsT=wu[:, kc, 0, j * 128:(j + 1) * 128], rhs=X[:, kc, tg * 512:(tg + 1) * 512], start=(kc == 0), stop=(kc == 15)),
                             reads=[rk, xkey], writes=[pk])
                    P.act(lambda e, pt=pt, tg=tg, zb=zb: e.copy(out=zb[:, 3 + tg * 512:3 + (tg + 1) * 512], in_=pt[:]), reads=[pk], writes=[zk])
                if pas == 1:
                    for tg in range(2):
                        pt, pk = newps()
                        for kc in range(16):
                            P.pe(lambda e, kc=kc, pt=pt, tg=tg, j=j, wu=wu: e.matmul(pt[:], lhsT=wu[:, kc, 1, j * 128:(j + 1) * 128], rhs=A[:, kc, tg * 512:(tg + 1) * 512], start=(kc == 0), stop=(kc == 15)),
                                 reads=[rk, "A"], writes=[pk])
                        P.act(lambda e, pt=pt, tg=tg, c=c: e.activation(out=glbs[c % 2][:, tg * 512:(tg + 1) * 512], in_=pt[:], func=AF.Gelu), reads=[pk], writes=["glb%d" % (c % 2)])
                    if do_sample:
                        pt, pk = newps()
                        for g_ in range(2):
                            for kc in range(16):
                                P.pe(lambda e, kc=kc, pt=pt, g_=g_, j=j, wu=wu: e.matmul(pt[:, g_ * 16:(g_ + 1) * 16], lhsT=wu[:, kc, g_, j * 128:(j + 1) * 128], rhs=Xs[:, kc, :], start=(kc == 0), stop=(kc == 15)),
                                     reads=[rk, "Xs"], writes=[pk])
                        P.dve(lambda e, pt=pt, c=c: e.tensor_copy(out=zsT[:, c, :], in_=pt[:, 0:16]), reads=[pk], writes=["zsT"])
                        P.dve(lambda e, pt=pt, c=c: e.tensor_copy(out=zgT[:, c, :], in_=pt[:, 16:32]), reads=[pk], writes=["zgT"])

            def H1b(c, pas):
                zb = zbs[pas]; zk = "zb%d" % pas
                P.dve(lambda e, c=c, zb=zb: e.tensor_scalar(out=xr, in0=zb[:, 3:1027], scalar1=prm[:, 24 + c:25 + c], scalar2=prm[:, 32 + c:33 + c], op0=ALU.mult, op1=ALU.add),
                      reads=[zk, "prm"], writes=["xr"])
                for jj in range(3):
                    P.dve(lambda e, c=c, jj=jj, zb=zb: e.scalar_tensor_tensor(out=xr, in0=zb[:, jj:jj + 1024], scalar=prm[:, jj * 8 + c:jj * 8 + c + 1], in1=xr, op0=ALU.mult, op1=ALU.add),
                          reads=[zk, "prm", "xr"], writes=["xr"])
                P.act(lambda e: e.copy(out=xrb, in_=xr), reads=["xr"], writes=["xrb"])
                if pas == 1:
                    P.dve(lambda e, c=c, zb=zb: e.tensor_copy(out=pconv_sb[:, c, :], in_=zb[:, 1024:1027]), reads=[zk], writes=["pconv_sb"])
                gps = []
                for gi in range(2):
                    for tg in range(2):
                        pt, pk = newps()
                        P.pe(lambda e, pt=pt, gi=gi, c=c, tg=tg: e.matmul(pt[:], lhsT=wgt[:, c, gi, :], rhs=xrb[:, tg * 512:(tg + 1) * 512], start=True, stop=True),
                             reads=["wgt", "xrb"], writes=[pk])
                        gps.append((pt, pk))
                return gps

            def H2a(c, pas, gps):
                i_ = 0
                for gi, (dst, dk_, bo) in enumerate([(rr, "rr", 40), (ig, "ig", 48)]):
                    for tg in range(2):
                        pt, pk = gps[i_]; i_ += 1
                        P.act(lambda e, pt=pt, dst=dst, tg=tg, bo=bo, c=c: e.activation(out=dst[:, tg * 512:(tg + 1) * 512], in_=pt[:], func=AF.Sigmoid, bias=prm[:, bo + c:bo + c + 1], scale=1.0),
                              reads=[pk, "prm"], writes=[dk_])
                P.dve(lambda e: e.tensor_tensor(out=ig, in0=ig, in1=xr, op=ALU.mult), reads=["ig", "xr"], writes=["ig"])

            def H2b(c, pas):
                P.act(lambda e, c=c: e.activation(out=av, in_=rr, func=AF.Exp, scale=cl[:, c:c + 1]), reads=["rr", "cl"], writes=["rr"])
                P.act(lambda e: e.activation(out=tmp, in_=av, func=AF.Square), reads=["rr"], writes=["tmp"])
                P.act(lambda e: e.activation(out=tmp, in_=tmp, func=AF.Sqrt, bias=1.0, scale=-1.0), reads=["tmp"], writes=["tmp"])
                P.dve(lambda e: e.tensor_tensor(out=ig, in0=ig, in1=tmp, op=ALU.mult), reads=["ig", "tmp"], writes=["ig"])
                if pas == 0:
                    P.dve(lambda e: e.tensor_tensor_scan(out=hh, data0=av, data1=ig, initial=0.0, op0=ALU.mult, op1=ALU.add), reads=["rr", "ig"], writes=["hh"])
                    P.dve(lambda e: e.tensor_scalar(out=hl, in0=hh[:, 1023:1024], scalar1=maskt[:, 0:1], scalar2=None, op0=ALU.mult), reads=["hh", "mask"], writes=["hl"])
                else:
                    P.dve(lambda e: e.tensor_tensor_scan(out=hh, data0=av, data1=ig, initial=hl, op0=ALU.mult, op1=ALU.add), reads=["rr", "ig", "hl"], writes=["hh"])
                    P.dve(lambda e, c=c: e.tensor_copy(out=ph_sb[:, c:c + 1], in_=hh[:, 1023:1024]), reads=["hh"], writes=["ph_sb"])
                    P.dve(lambda e, c=c: e.tensor_tensor(out=hh, in0=hh, in1=glbs[c % 2], op=ALU.mult), reads=["hh", "glb%d" % (c % 2)], writes=["hh"])
                    P.act(lambda e, c=c: e.copy(out=B[:, c, :], in_=hh), reads=["hh"], writes=["B"])
                    P.act(lambda e: e.activation(out=hgsq, in_=hh, func=AF.Square), reads=["hh"], writes=["hgsq"])
                    for tg in range(2):
                        pt, pk = newps()
                        P.pe(lambda e, tg=tg, pt=pt: e.matmul(pt[:], lhsT=onesb[:], rhs=hgsq[:, tg * 512:(tg + 1) * 512], start=True, stop=True),
                             reads=["onesb", "hgsq"], writes=[pk])
                        if c == 0:
                            P.dve(lambda e, tg=tg, pt=pt: e.tensor_copy(out=ssacc[:, tg * 512:(tg + 1) * 512], in_=pt[:]), reads=[pk], writes=["ssacc"])
                        else:
                            P.dve(lambda e, tg=tg, pt=pt: e.tensor_tensor(out=ssacc[:, tg * 512:(tg + 1) * 512], in0=ssacc[:, tg * 512:(tg + 1) * 512], in1=pt[:], op=ALU.add), reads=[pk, "ssacc"], writes=["ssacc"])

            steps = [(c, pas) for c in range(8) for pas in range(2)]
            H1a(*steps[0])
            for k, (c, pas) in enumerate(steps):
                if k + 1 < len(steps):
                    H1a(*steps[k + 1])
                gps = H1b(c, pas)
                if k > 0:
                    H2b(*steps[k - 1])
                H2a(c, pas, gps)
            H2b(*steps[-1])
            _stop("B3")
            P.act(lambda e: e.activation(out=tmp, in_=ssacc, func=AF.Sqrt, bias=EPS, scale=1.0 / 1024), reads=["ssacc"], writes=["tmp"])
            P.dve(lambda e: e.reciprocal(out=tmp, in_=tmp), reads=["tmp"], writes=["tmp"])
            for c in range(8):
                P.dve(lambda e, c=c: e.scalar_tensor_tensor(out=B[:, c, :], in0=B[:, c, :], scalar=prm[:, 64 + c:65 + c], in1=tmp, op0=ALU.mult, op1=ALU.mult),
                      reads=["B", "prm", "tmp"], writes=["B"])
            P.dma("sp", "o_ph", o_ph[:, :], ph_sb[:], reads=["ph_sb"])
            P.dma("sp", "o_pconv", o_pconv[:, :], pconv_sb[:].rearrange("p c r -> p (c r)"), reads=["pconv_sb"])
            ps_state["pool"] = list(range(8)); ps_state["i"] = 0

            _stop("B")
            P.barrier(bar_tiles)
            mw = Reg(RXt[:, 8192:16384], 8192)
            gli = mw.f32(2048); glf = mw.f32(2048); gB = mw.f32(2048)
            SCR.reset()
            gml_bc = SCR.f32(1024)
            P.dma("sp", "gml", gml_bc, g_ml_out.partition_broadcast(128), writes=["gml_bc"])
            bif = small[0:4, 2:4]
            P.dma("sp", "c_bi", bif[:, 0:1], b_ml_i.rearrange("(p o) -> p o", o=1), writes=["bif"])
            P.dma("sp", "c_bf", bif[:, 1:2], b_ml_f.rearrange("(p o) -> p o", o=1), writes=["bif"])
            P.dve(lambda e: e.tensor_scalar(out=bif[:, 1:2], in0=bif[:, 1:2], scalar1=-1.0, scalar2=None, op0=ALU.mult), reads=["bif"], writes=["bif"])
            for pas in range(2):
                X, xkey = (PRE, "PRE") if pas == 0 else (A, "A")
                for tg in range(2):
                    col = pas * 1024 + tg * 512
                    for gi in range(2):
                        pt, pk = newps()
                        for kc in range(16):
                            P.pe(lambda e, kc=kc, pt=pt, X=X, tg=tg, gi=gi: e.matmul(pt[0:4, :], lhsT=wif[:, kc, gi * 4:gi * 4 + 4], rhs=X[:, kc, tg * 512:(tg + 1) * 512], start=(kc == 0), stop=(kc == 15)),
                                 reads=["wif", xkey], writes=[pk])
                        if gi == 0:
                            P.act(lambda e, pt=pt, col=col: e.activation(out=gli[0:4, col:col + 512], in_=pt[0:4, :], func=AF.Identity, bias=bif[:, 0:1], scale=1.0), reads=[pk, "bif"], writes=["gli"])
                        else:
                            P.act(lambda e, pt=pt, col=col: e.activation(out=glf[0:4, col:col + 512], in_=pt[0:4, :], func=AF.Exp, bias=bif[:, 1:2], scale=-1.0), reads=[pk, "bif"], writes=["glf"])
            P.act(lambda e: e.activation(out=glf[0:4, :], in_=glf[0:4, :], func=AF.Ln, bias=1.0, scale=1.0), reads=["glf"], writes=["glf"])
            P.dve(lambda e: e.tensor_scalar(out=glf[0:4, :], in0=glf[0:4, :], scalar1=-1.0, scalar2=None, op0=ALU.mult), reads=["glf"], writes=["glf"])
            _stop("C0")
            gsm = SB("gsm", [4, 192])
            Gpe = gsm[:, 0:16]; Gend = gsm[:, 16:32]; dec = gsm[:, 32:48]; m0own = gsm[:, 48:49]; decd = gsm[:, 64:128]
            gones = mw.f32(1024)
            P.dve(lambda e: e.memset(gones[0:4, :], 1.0), writes=["gones"])
            for pas in range(2):
                sl = slice(pas * 1024, (pas + 1) * 1024)
                P.dve(lambda e, sl=sl: e.tensor_tensor_scan(out=gB[0:4, sl], data0=gones[0:4, :], data1=glf[0:4, sl], initial=0.0, op0=ALU.mult, op1=ALU.add),
                      reads=["gones", "glf"], writes=["gB"])
                P.dve(lambda e, sl=sl: e.tensor_tensor(out=gli[0:4, sl], in0=gli[0:4, sl], in1=gB[0:4, sl], op=ALU.subtract), reads=["gli", "gB"], writes=["gli"])
                if pas == 0:
                    P.dve(lambda e, sl=sl: e.tensor_tensor_scan(out=glf[0:4, sl], data0=gli[0:4, sl], data1=gli[0:4, sl], initial=0.0, op0=ALU.max, op1=ALU.max),
                          reads=["gli", "glf"], writes=["glf"])
                    P.dve(lambda e: e.memset(Gpe[:, 0:1], 0.0), writes=["Gpe"])
                else:
                    P.dve(lambda e, sl=sl: e.tensor_tensor_scan(out=glf[0:4, sl], data0=gli[0:4, sl], data1=gli[0:4, sl], initial=m0own, op0=ALU.max, op1=ALU.max),
                          reads=["gli", "glf", "gsm_m0"], writes=["glf"])
                    P.dve(lambda e: e.tensor_copy(out=Gpe[:, 8:9], in_=m0own), reads=["gsm_m0"], writes=["Gpe"])
                Gv = glf[0:4, sl].rearrange("p (c t) -> p c t", t=128)
                P.dve(lambda e, Gv=Gv, pas=pas: e.tensor_copy(out=Gend[:, pas * 8:pas * 8 + 8], in_=Gv[:, :, 127]), reads=["glf"], writes=["Gend"])
                P.dve(lambda e, pas=pas: e.tensor_copy(out=Gpe[:, pas * 8 + 1:pas * 8 + 8], in_=Gend[:, pas * 8:pas * 8 + 7]), reads=["Gend"], writes=["Gpe"])
                if pas == 0:
                    P.dve(lambda e: e.tensor_tensor(out=m0own, in0=gB[0:4, 1023:1024], in1=glf[0:4, 1023:1024], op=ALU.add), reads=["gB", "glf"], writes=["gsm_m0"])
                    P.dve(lambda e: e.tensor_scalar(out=m0own, in0=m0own, scalar1=maskt[0:4, 0:1], scalar2=None, op0=ALU.mult), reads=["gsm_m0", "mask"], writes=["gsm_m0"])
                else:
                    P.dve(lambda e: e.tensor_tensor(out=pm_sb[:], in0=gB[0:4, 2047:2048], in1=glf[0:4, 2047:2048], op=ALU.add), reads=["gB", "glf"], writes=["pm_sb"])
                    P.dma("sp", "o_pm", o_pm[:, :], pm_sb[:], reads=["pm_sb"])
            if do_sample:
                pt, pk = newps()
                for kc in range(16):
                    P.pe(lambda e, kc=kc, pt=pt: e.matmul(pt[0:16, 0:8], lhsT=Xs[:, kc, :], rhs=wif[:, kc, :], start=(kc == 0), stop=(kc == 15)), reads=["Xs", "wif"], writes=[pk])
                P.dve(lambda e, pt=pt: e.tensor_copy(out=lif_s[0:16, :], in_=pt[0:16, 0:8]), reads=[pk], writes=["lif_s"])
            P.dve(lambda e: e.tensor_tensor(out=dec, in0=Gpe, in1=Gend, op=ALU.subtract), reads=["Gpe", "Gend"], writes=["dec"])
            P.act(lambda e: e.activation(out=dec, in_=dec, func=AF.Exp), reads=["dec"], writes=["dec"])
            Gpe_b = Gpe.unsqueeze(2).to_broadcast([4, 16, 128])
            uv = gli[0:4, :].rearrange("p (c t) -> p c t", t=128)
            Bv = gB[0:4, :].rearrange("p (c t) -> p c t", t=128)
            P.dve(lambda e: e.tensor_tensor(out=uv, in0=uv, in1=Gpe_b, op=ALU.subtract), reads=["gli", "Gpe"], writes=["gli"])
            P.act(lambda e: e.activation(out=gli[0:4, :], in_=gli[0:4, :], func=AF.Exp), reads=["gli"], writes=["gli"])
            P.dve(lambda e: e.tensor_tensor(out=Bv, in0=Bv, in1=Gpe_b, op=ALU.add), reads=["gB", "Gpe"], writes=["gB"])
            P.act(lambda e: e.activation(out=gB[0:4, :], in_=gB[0:4, :], func=AF.Exp, scale=-1.0), reads=["gB"], writes=["gB"])
            ekT = SB("ekT", [128, 16, 4]); clT = SB("clT", [128, 8, 4]); decbc = SB("decbc", [128, 16, 4])
            pt, pk = newps()
            for tt in range(16):
                P.pe(lambda e, tt=tt, pt=pt: e.transpose(out=pt[:, tt * 4:tt * 4 + 4], in_=gli[0:4, tt * 128:(tt + 1) * 128], identity=ident32[0:4, 0:4]), reads=["gli", "ident32"], writes=[pk])
            for c in range(8):
                P.pe(lambda e, c=c, pt=pt: e.transpose(out=pt[:, 64 + c * 4:64 + c * 4 + 4], in_=gB[0:4, 1024 + c * 128:1024 + (c + 1) * 128], identity=ident32[0:4, 0:4]), reads=["gB", "ident32"], writes=[pk])
            P.dve(lambda e, pt=pt: e.tensor_copy(out=ekT[:].rearrange("p a b -> p (a b)"), in_=pt[:, 0:64]), reads=[pk], writes=["ekT"])
            P.dve(lambda e, pt=pt: e.tensor_copy(out=clT[:].rearrange("p a b -> p (a b)"), in_=pt[:, 64:96]), reads=[pk], writes=["clT"])
            ekTs = SB("ekTs", [128, 16, 4])
            P.dve(lambda e: e.tensor_scalar(out=ekTs[:], in0=ekT[:], scalar1=128 ** -0.5, scalar2=None, op0=ALU.mult), reads=["ekT"], writes=["ekTs"])
            eye4 = ident32[0:4, 0:4]
            P.dve(lambda e: e.tensor_tensor(out=decd.rearrange("p (a b) -> p a b", b=4), in0=dec.unsqueeze(2).to_broadcast([4, 16, 4]), in1=eye4.unsqueeze(1).to_broadcast([4, 16, 4]), op=ALU.mult),
                  reads=["dec", "ident32"], writes=["decd"])
            pt, pk = newps()
            P.pe(lambda e, pt=pt: e.matmul(pt[:, 0:64], lhsT=ones32[0:4, :], rhs=decd, start=True, stop=True), reads=["ones32", "decd"], writes=[pk])
            P.dve(lambda e, pt=pt: e.tensor_copy(out=decbc[:].rearrange("p a b -> p (a b)"), in_=pt[:, 0:64]), reads=[pk], writes=["decbc"])

            _stop("C1")
            P.barrier(bar_tiles)
            mw = Reg(RXt[:, 8192:16384], 8192)
            qT = mw.bf16(1024); kT = mw.bf16(1024)
            kw = mw.bf16(16, 128); vaug = mw.bf16(16, 264)[:, :, 0:257]; og = mw.bf16(8, 256)
            Cbf = mw.bf16(8, 264)[:, :, 0:257]
            pC_sb = mw.f32(4, 257)
            Rst = SCR.f32(257); C0own = SCR.f32(257)
            Ptil = SCR.bf16(128); t1 = SCR.f32(256); ymlb = SCR.bf16(256)
            den = small[:, 8:9]; ssm = small[:, 9:10]; tot = small[:, 10:11]
            P.dve(lambda e: e.memset(vaug[:, :, 256:257], 1.0), writes=["vaug"])
            DK_S = 128 ** -0.5
            uO = None
            for h in range(4):
                def fillA(s, h=h):
                    return s
                slotA, rkA = ring_unit([
                    (lambda s: s.rearrange("p (k n) -> p k n", k=16)[:, :, 0:128], w_in[:, OFF_Q + h * 128:OFF_Q + (h + 1) * 128].rearrange("(k p) n -> p k n", p=128)),
                    (lambda s: s.rearrange("p (k n) -> p k n", k=16)[:, :, 128:256], w_in[:, OFF_K + h * 128:OFF_K + (h + 1) * 128].rearrange("(k p) n -> p k n", p=128)),
                    (lambda s: s.rearrange("p (k n) -> p k n", k=16)[:, :, 256:512], w_in[:, OFF_V + h * 256:OFF_V + (h + 1) * 256].rearrange("(k p) n -> p k n", p=128)),
                ])
                uA = slotA.rearrange("p (k n) -> p k n", k=16)
                if h % 2 == 0:
                    slotO, rkO = ring_unit([(lambda s: s.rearrange("p (k n) -> p k n", k=16), w_in[:, OFF_O + h * 256:OFF_O + (h + 2) * 256].rearrange("(k p) n -> p k n", p=128))])
                    uO = slotO.rearrange("p (k n) -> p k n", k=16)
                for which, dstT, dkey_, sc_ in ((0, qT, "qT", 1.0), (1, kT, "kT", DK_S)):
                    for tg in range(2):
                        pt, pk = newps()
                        for kc in range(16):
                            P.pe(lambda e, kc=kc, pt=pt, tg=tg, which=which, uA=uA: e.matmul(pt[:], lhsT=uA[:, kc, which * 128:(which + 1) * 128], rhs=A[:, kc, tg * 512:(tg + 1) * 512], start=(kc == 0), stop=(kc == 15)),
                                 reads=[rkA, "A"], writes=[pk])
                        P.act(lambda e, pt=pt, tg=tg, dstT=dstT, sc_=sc_: e.activation(out=dstT[:, tg * 512:(tg + 1) * 512], in_=pt[:], func=AF.Identity, scale=sc_), reads=[pk], writes=[dkey_])
                _stop("C2a")
                for tt in range(16):
                    X, xkey = (PRE, "PRE") if tt < 8 else (A, "A")
                    t = tt % 8
                    pt, pk = newps()
                    for kc in range(16):
                        P.pe(lambda e, kc=kc, pt=pt, X=X, t=t, uA=uA: e.matmul(pt[:, 0:384], lhsT=X[:, kc, t * 128:(t + 1) * 128], rhs=uA[:, kc, 128:512], start=(kc == 0), stop=(kc == 15)),
                             reads=[rkA, xkey], writes=[pk])
                    P.dve(lambda e, pt=pt, tt=tt, h=h: e.tensor_scalar(out=kw[:, tt, :], in0=pt[:, 0:128], scalar1=ekTs[:, tt, h:h + 1], scalar2=None, op0=ALU.mult),
                          reads=[pk, "ekTs"], writes=["kw"])
                    P.act(lambda e, pt=pt, tt=tt: e.copy(out=vaug[:, tt, 0:256], in_=pt[:, 128:384]), reads=[pk], writes=["vaug"])
                _stop("C2b")
                for t in range(8):
                    pt, pk = newps()
                    for kc in range(16):
                        P.pe(lambda e, kc=kc, pt=pt, t=t, uO=uO, h=h: e.matmul(pt[:, 0:256], lhsT=A[:, kc, t * 128:(t + 1) * 128], rhs=uO[:, kc, (h % 2) * 256:(h % 2 + 1) * 256], start=(kc == 0), stop=(kc == 15)),
                             reads=[rkO, "A"], writes=[pk])
                    P.act(lambda e, pt=pt, t=t: e.activation(out=og[:, t, :], in_=pt[:, 0:256], func=AF.Sigmoid), reads=[pk], writes=["og"])
                def ml_out(c, h=h):
                    cs = slice(c * 128, (c + 1) * 128)
                    pt, pk = newps()
                    P.pe(lambda e, pt=pt, cs=cs: e.matmul(pt[:, 0:128], lhsT=kT[:, cs], rhs=qT[:, cs], start=True, stop=True), reads=["kT", "qT"], writes=[pk])
                    P.dve(lambda e, pt=pt, c=c, h=h: e.scalar_tensor_tensor(out=Ptil, in0=pt[:, 0:128], scalar=ekT[:, 8 + c, h:h + 1], in1=causal[:], op0=ALU.mult, op1=ALU.mult),
                          reads=[pk, "ekT", "causal"], writes=["Ptil"])
                    pa, pak = newps()
                    P.pe(lambda e, pa=pa, c=c: e.matmul(pa[:, 0:257], lhsT=Ptil, rhs=vaug[:, 8 + c, :], start=True, stop=False), reads=["Ptil", "vaug"], writes=[pak])
                    P.pe(lambda e, pa=pa, c=c, cs=cs: e.matmul(pa[:, 0:257], lhsT=qT[:, cs], rhs=Cbf[:, c, :], start=False, stop=True), reads=["qT", "Cbf"], writes=[pak])
                    P.act(lambda e, pa=pa: e.activation(out=den, in_=pa[:, 256:257], func=AF.Abs), reads=[pak], writes=["den"])
                    P.dve(lambda e, c=c, h=h: e.tensor_scalar(out=den, in0=den, scalar1=clT[:, c, h:h + 1], scalar2=None, op0=ALU.max), reads=["den", "clT"], writes=["den"])
                    P.dve(lambda e: e.reciprocal(out=den, in_=den), reads=["den"], writes=["den"])
                    P.act(lambda e, pa=pa: e.activation(out=t1, in_=pa[:, 0:256], func=AF.Square, scale=den, accum_out=ssm), reads=[pak, "den"], writes=["t1", "ssm"])
                    rstd_from_ss(ssm, 256, "ssm")
                    P.dve(lambda e: e.tensor_tensor(out=tot, in0=ssm, in1=den, op=ALU.mult), reads=["ssm", "den"], writes=["tot"])
                    P.dve(lambda e, pa=pa, h=h: e.scalar_tensor_tensor(out=t1, in0=pa[:, 0:256], scalar=tot, in1=gml_bc[:, h * 256:(h + 1) * 256], op0=ALU.mult, op1=ALU.mult),
                          reads=[pak, "tot", "gml_bc", "t1"], writes=["t1"])
                    P.dve(lambda e, c=c: e.tensor_tensor(out=ymlb, in0=t1, in1=og[:, c, :], op=ALU.mult), reads=["t1", "og"], writes=["ymlb"])
                    ptr, ptk = newps()
                    ptrb = ptr[:].bitcast(BF16)
                    for jj in range(2):
                        P.pe(lambda e, jj=jj, ptrb=ptrb: e.transpose(out=ptrb[:, jj * 128:(jj + 1) * 128], in_=ymlb[:, jj * 128:(jj + 1) * 128], identity=identb[:]), reads=["ymlb", "identb"], writes=[ptk])
                    P.act(lambda e, ptrb=ptrb, h=h, cs=cs: e.copy(out=B[:, 8 + 2 * h:10 + 2 * h, cs], in_=ptrb[:, 0:256].rearrange("p (j t) -> p j t", j=2)), reads=[ptk], writes=["B"])
                _stop("C2")
                for tt in range(16):
                    c = tt % 8
                    pt, pk = newps()
                    P.pe(lambda e, pt=pt, tt=tt: e.matmul(pt[:, 0:257], lhsT=kw[:, tt, :], rhs=vaug[:, tt, :], start=True, stop=True), reads=["kw", "vaug"], writes=[pk])
                    if tt == 0:
                        P.dve(lambda e, pt=pt: e.tensor_copy(out=Rst, in_=pt[:, 0:257]), reads=[pk], writes=["Rst"])
                    elif tt == 8:
                        P.dve(lambda e, h=h: e.tensor_scalar(out=C0own, in0=Rst, scalar1=decbc[:, 7, h:h + 1], scalar2=maskt[:, 0:1], op0=ALU.mult, op1=ALU.mult),
                              reads=["Rst", "decbc", "mask"], writes=["C0own"])
                        P.act(lambda e: e.copy(out=Cbf[:, 0, :], in_=C0own), reads=["C0own"], writes=["Cbf"])
                        P.dve(lambda e, pt=pt: e.tensor_tensor(out=Rst, in0=C0own, in1=pt[:, 0:257], op=ALU.add), reads=["C0own", pk, "Rst"], writes=["Rst"])
                    else:
                        if tt > 8:
                            P.act(lambda e, c=c, tt=tt, h=h: e.activation(out=Cbf[:, c, :], in_=Rst, func=AF.Identity, scale=decbc[:, tt - 1, h:h + 1]), reads=["Rst", "decbc"], writes=["Cbf"])
                        P.dve(lambda e, pt=pt, tt=tt, h=h: e.scalar_tensor_tensor(out=Rst, in0=Rst, scalar=decbc[:, tt - 1, h:h + 1], in1=pt[:, 0:257], op0=ALU.mult, op1=ALU.add),
                              reads=["Rst", "decbc", pk], writes=["Rst"])
                    if tt >= 8:
                        ml_out(tt - 8)
                P.dve(lambda e, h=h: e.tensor_scalar(out=pC_sb[:, h, :], in0=Rst, scalar1=decbc[:, 15, h:h + 1], scalar2=None, op0=ALU.mult), reads=["Rst", "decbc"], writes=["pC_sb"])
                _stop("C3")
            P.dma("sp", "o_pC", o_pC[:, :], pC_sb[:].rearrange("p h n -> p (h n)"), reads=["pC_sb"])

            if do_sample:
                P.barrier(bar_tiles)
                P.dve(lambda e: e.memset(bart[:, 3:4], 0.0), writes=["pC_sb", "bart3"])
                P.barrier(bar_tiles)
                SR = Reg(RXt, 16384)
                qkv_s = SR.f32(4, 512); og_s = SR.f32(1024)
                eye16 = ident32[0:16, 0:16]
                for h in range(4):
                    slotA, rkA = ring_unit([
                        (lambda s: s.rearrange("p (k n) -> p k n", k=16)[:, :, 0:128], w_in[:, OFF_Q + h * 128:OFF_Q + (h + 1) * 128].rearrange("(k p) n -> p k n", p=128)),
                        (lambda s: s.rearrange("p (k n) -> p k n", k=16)[:, :, 128:256], w_in[:, OFF_K + h * 128:OFF_K + (h + 1) * 128].rearrange("(k p) n -> p k n", p=128)),
                        (lambda s: s.rearrange("p (k n) -> p k n", k=16)[:, :, 256:512], w_in[:, OFF_V + h * 256:OFF_V + (h + 1) * 256].rearrange("(k p) n -> p k n", p=128)),
                    ])
                    uA = slotA.rearrange("p (k n) -> p k n", k=16)
                    pt, pk = newps()
                    for kc in range(16):
                        P.pe(lambda e, kc=kc, pt=pt, uA=uA: e.matmul(pt[0:16, :], lhsT=Xs[:, kc, :], rhs=uA[:, kc, :], start=(kc == 0), stop=(kc == 15)), reads=[rkA, "Xs"], writes=[pk])
                    P.act(lambda e, pt=pt, h=h: e.copy(out=qkv_s[0:16, h, :], in_=pt[0:16, :]), reads=[pk], writes=["qkv_s"])
                for hp in range(2):
                    slotO, rkO = ring_unit([(lambda s: s.rearrange("p (k n) -> p k n", k=16), w_in[:, OFF_O + hp * 512:OFF_O + (hp + 1) * 512].rearrange("(k p) n -> p k n", p=128))])
                    uO = slotO.rearrange("p (k n) -> p k n", k=16)
                    pt, pk = newps()
                    for kc in range(16):
                        P.pe(lambda e, kc=kc, pt=pt, uO=uO: e.matmul(pt[0:16, :], lhsT=Xs[:, kc, :], rhs=uO[:, kc, :], start=(kc == 0), stop=(kc == 15)), reads=[rkO, "Xs"], writes=[pk])
                    P.act(lambda e, pt=pt, hp=hp: e.activation(out=og_s[0:16, hp * 512:(hp + 1) * 512], in_=pt[0:16, :], func=AF.Sigmoid), reads=[pk], writes=["og_s"])
                sr_mark = SR.off
                cb_tm = SR.f32(3072); h0_tm = SR.f32(1024); fmw = SR.f32(512)
                xr3 = SR.f32(8, 16); tmp3 = SR.f32(8, 16); xrb3 = SR.bf16(8, 16); rg3 = SR.f32(16, 16)
                a3 = SR.f32(8, 16); m3 = SR.f32(8, 16); hn3 = SR.f32(8, 16); hg3 = SR.f32(8, 16)
                SR.off = 14336
                sh_tm = SR.f32(1024); sz_tm = SR.f32(1024)
                P.dma("sp", "cb_tm", cb_tm[0:16, :], st_conv[:, :], writes=["cb_tm"])
                P.dma("sp", "h0_tm", h0_tm[0:16, :], st_h[:, :], writes=["h0_tm"])
                P.dma("sp", "o_sconv01", o_sconv[:, 0:2048], cb_tm[0:16, 1024:3072], reads=["cb_tm"])
                pt, pk = newps()
                for blk in range(24):
                    P.pe(lambda e, blk=blk, pt=pt: e.transpose(out=pt[:, blk * 16:(blk + 1) * 16], in_=cb_tm[0:16, blk * 128:(blk + 1) * 128], identity=eye16), reads=["cb_tm", "ident32"], writes=[pk])
                for c in range(8):
                    P.pe(lambda e, c=c, pt=pt: e.transpose(out=pt[:, (24 + c) * 16:(25 + c) * 16], in_=h0_tm[0:16, c * 128:(c + 1) * 128], identity=eye16), reads=["h0_tm", "ident32"], writes=[pk])
                P.dve(lambda e, pt=pt: e.tensor_copy(out=fmw, in_=pt[:, :]), reads=[pk], writes=["fmw"])
                cbT = fmw[:, 0:384].rearrange("p (r c t) -> p r c t", r=3, c=8)
                h0T = fmw[:, 384:512].rearrange("p (c t) -> p c t", c=8)

                def pb(c0):
                    return prm[:, c0:c0 + 8].unsqueeze(2).to_broadcast([128, 8, 16])

                P.dve(lambda e: e.tensor_tensor(out=xr3, in0=zsT, in1=pb(24), op=ALU.mult), reads=["zsT", "prm"], writes=["xr3"])
                P.dve(lambda e: e.tensor_tensor(out=xr3, in0=xr3, in1=pb(32), op=ALU.add), reads=["xr3", "prm"], writes=["xr3"])
                for jj in range(3):
                    P.dve(lambda e, jj=jj: e.tensor_tensor(out=tmp3, in0=cbT[:, jj], in1=pb(jj * 8), op=ALU.mult), reads=["fmw", "prm"], writes=["tmp3"])
                    P.dve(lambda e: e.tensor_tensor(out=xr3, in0=xr3, in1=tmp3, op=ALU.add), reads=["xr3", "tmp3"], writes=["xr3"])
                P.act(lambda e: e.copy(out=xrb3, in_=xr3), reads=["xr3"], writes=["xrb3"])
                pt, pk = newps()
                for gi in range(2):
                    for c in range(8):
                        P.pe(lambda e, gi=gi, c=c, pt=pt: e.matmul(pt[:, (gi * 8 + c) * 16:(gi * 8 + c + 1) * 16], lhsT=wgt[:, c, gi, :], rhs=xrb3[:, c, :], start=True, stop=True), reads=["wgt", "xrb3"], writes=[pk])
                P.dve(lambda e, pt=pt: e.tensor_tensor(out=rg3, in0=pt[:, 0:256].rearrange("p (a b) -> p a b", a=16), in1=prm[:, 40:56].unsqueeze(2).to_broadcast([128, 16, 16]), op=ALU.add), reads=[pk, "prm"], writes=["rg3"])
                P.act(lambda e: e.activation(out=rg3, in_=rg3, func=AF.Sigmoid), reads=["rg3"], writes=["rg3"])
                r3 = rg3[:, 0:8, :]; ig3 = rg3[:, 8:16, :]
                P.dve(lambda e: e.tensor_tensor(out=a3, in0=r3, in1=cl[:, 0:8].unsqueeze(2).to_broadcast([128, 8, 16]), op=ALU.mult), reads=["rg3", "cl"], writes=["a3"])
                P.act(lambda e: e.activation(out=a3, in_=a3, func=AF.Exp), reads=["a3"], writes=["a3"])
                P.dve(lambda e: e.tensor_tensor(out=m3, in0=a3, in1=a3, op=ALU.mult), reads=["a3"], writes=["m3"])
                P.act(lambda e: e.activation(out=m3, in_=m3, func=AF.Sqrt, bias=1.0, scale=-1.0), reads=["m3"], writes=["m3"])
                P.dve(lambda e: e.tensor_tensor(out=ig3, in0=ig3, in1=xr3, op=ALU.mult), reads=["rg3", "xr3"], writes=["rg3"])
                P.dve(lambda e: e.tensor_tensor(out=ig3, in0=ig3, in1=m3, op=ALU.mult), reads=["rg3", "m3"], writes=["rg3"])
                P.dve(lambda e: e.tensor_tensor(out=hn3, in0=a3, in1=h0T, op=ALU.mult), reads=["a3", "fmw"], writes=["hn3"])
                P.dve(lambda e: e.tensor_tensor(out=hn3, in0=hn3, in1=ig3, op=ALU.add), reads=["hn3", "rg3"], writes=["hn3"])
                P.act(lambda e: e.activation(out=tmp3, in_=zgT, func=AF.Gelu), reads=["zgT", "tmp3"], writes=["tmp3"])
                P.dve(lambda e: e.tensor_tensor(out=hg3, in0=hn3, in1=tmp3, op=ALU.mult), reads=["hn3", "tmp3"], writes=["hg3"])
                P.dve(lambda e: e.tensor_tensor(out=m3, in0=hg3, in1=hg3, op=ALU.mult), reads=["hg3"], writes=["m3"])
                pt, pk = newps()
                for c in range(8):
                    P.pe(lambda e, c=c, pt=pt: e.matmul(pt[:, 0:16], lhsT=ones32[:], rhs=m3[:, c, :], start=(c == 0), stop=(c == 7)), reads=["ones32", "m3"], writes=[pk])
                P.act(lambda e, pt=pt: e.activation(out=srs, in_=pt[:, 0:16], func=AF.Sqrt, bias=EPS, scale=1.0 / 1024), reads=[pk], writes=["srs"])
                P.dve(lambda e: e.reciprocal(out=srs, in_=srs), reads=["srs"], writes=["srs"])
                P.dve(lambda e: e.tensor_tensor(out=hg3, in0=hg3, in1=srs.unsqueeze(1).to_broadcast([128, 8, 16]), op=ALU.mult), reads=["hg3", "srs"], writes=["hg3"])
                P.dve(lambda e: e.tensor_tensor(out=Ys[:, 0:8, :], in0=hg3, in1=pb(64), op=ALU.mult), reads=["hg3", "prm"], writes=["Ys"])
                for (src3, skey, dst, okey, odram) in ((hn3, "hn3", sh_tm, "sh_tm", o_sh[:, :]), (zsT, "zsT", sz_tm, "sz_tm", o_sconv[:, 2048:3072])):
                    for half in range(2):
                        pt, pk = newps()
                        for cc in range(4):
                            c = half * 4 + cc
                            P.pe(lambda e, cc=cc, c=c, pt=pt, src3=src3: e.transpose(out=pt[0:16, cc * 128:(cc + 1) * 128], in_=src3[:, c, :], identity=ident32[:]), reads=[skey, "ident32"], writes=[pk])
                        P.dve(lambda e, pt=pt, dst=dst, half=half: e.tensor_copy(out=dst[0:16, half * 512:(half + 1) * 512], in_=pt[0:16, :]), reads=[pk], writes=[okey])
                    P.dma("sp", okey, odram, dst[0:16, :], reads=[okey])
                _stop("S1")
                P.barrier(bar_tiles)
                P.dve(lambda e: e.memset(bart[:, 3:4], 0.0), writes=["cb_tm", "bart3"])
                P.barrier(bar_tiles)
                SR.off = sr_mark
                bi_bc = SR.f32(8); msm = SR.f32(64); n_in = SR.f32(512); prod = SR.f32(512); nn = SR.f32(512)
                kv = SR.f32(4 * 384); qTs = SR.f32(64); Qm = SR.f32(4 * 256); sdecd = SR.f32(64); decb = SR.f32(64)
                Cin = [SR.f32(1024), SR.f32(1024)]; Cout = [SR.f32(1024), SR.f32(1024)]
                rowb = SR.f32(384); num = SR.f32(1024); yml = SR.f32(1024); junk16 = SR.f32(256)
                assert SR.off <= 14336
                P.dma("sp", "bi_bc", bi_bc[0:16, 0:4], b_ml_i.partition_broadcast(16), writes=["bi_bc"])
                P.dma("sp", "bf_bc", bi_bc[0:16, 4:8], b_ml_f.partition_broadcast(16), writes=["bi_bc"])
                m_in = msm[0:16, 0:4]; li_ = msm[0:16, 4:8]; lf_ = msm[0:16, 8:12]; mnew = msm[0:16, 12:16]; Dg = msm[0:16, 16:20]
                scg = msm[0:16, 20:24]; qk_ = msm[0:16, 24:28]; qn_ = msm[0:16, 28:32]; den_ = msm[0:16, 32:36]; tmp4 = msm[0:16, 36:40]
                dqk = msm[0:16, 40:44]; tot4 = msm[0:16, 44:48]; ss4 = msm[0:16, 48:52]
                P.dma("sp", "m_in", m_in, st_m[:, :], writes=["msm"])
                P.dma("sp", "n_in", n_in[0:16, :], st_n[:, :], writes=["n_in"])
                MS = ["msm"]
                P.dve(lambda e: e.tensor_tensor(out=msm[0:16, 4:12], in0=lif_s[0:16, :], in1=bi_bc[0:16, :], op=ALU.add), reads=["lif_s", "bi_bc"], writes=MS)
                P.act(lambda e: e.activation(out=lf_, in_=lf_, func=AF.Exp, scale=-1.0), reads=MS, writes=MS)
                P.act(lambda e: e.activation(out=lf_, in_=lf_, func=AF.Ln, bias=1.0, scale=1.0), reads=MS, writes=MS)
                P.dve(lambda e: e.tensor_scalar(out=lf_, in0=lf_, scalar1=-1.0, scalar2=None, op0=ALU.mult), reads=MS, writes=MS)
                P.dve(lambda e: e.tensor_tensor(out=tmp4, in0=lf_, in1=m_in, op=ALU.add), reads=MS, writes=MS)
                P.dve(lambda e: e.tensor_tensor(out=mnew, in0=tmp4, in1=li_, op=ALU.max), reads=MS, writes=MS)
                P.dve(lambda e: e.tensor_tensor(out=Dg, in0=li_, in1=mnew, op=ALU.subtract), reads=MS, writes=MS)
                P.act(lambda e: e.activation(out=Dg, in_=Dg, func=AF.Exp), reads=MS, writes=MS)
                P.dve(lambda e: e.tensor_tensor(out=scg, in0=tmp4, in1=mnew, op=ALU.subtract), reads=MS, writes=MS)
                P.act(lambda e: e.activation(out=scg, in_=scg, func=AF.Exp), reads=MS, writes=MS)
                osm = SR.f32(4)
                P.dve(lambda e: e.tensor_copy(out=osm[0:16, :], in_=mnew), reads=MS, writes=["osm"])
                P.dma("sp", "o_sm", o_sm[:, :], osm[0:16, :], reads=["osm"])
                q3 = qkv_s[0:16, :, 0:128]; k3 = qkv_s[0:16, :, 128:256]; v3 = qkv_s[0:16, :, 256:512]
                n3 = n_in[0:16, :].rearrange("p (h d) -> p h d", h=4)
                prod3 = prod[0:16, :].rearrange("p (h d) -> p h d", h=4)
                nn3 = nn[0:16, :].rearrange("p (h d) -> p h d", h=4)
                kv3 = kv[0:16, :].rearrange("p (h d) -> p h d", h=4)
                P.dve(lambda e: e.tensor_scalar(out=k3, in0=k3, scalar1=DK_S, scalar2=None, op0=ALU.mult), reads=["qkv_s"], writes=["qkv_s"])
                P.dve(lambda e: e.tensor_tensor(out=prod3, in0=q3, in1=k3, op=ALU.mult), reads=["qkv_s"], writes=["prod"])
                P.dve(lambda e: e.reduce_sum(out=qk_, in_=prod3, axis=AX.X), reads=["prod"] + MS, writes=MS)
                P.dve(lambda e: e.tensor_tensor(out=prod3, in0=q3, in1=n3, op=ALU.mult), reads=["qkv_s", "n_in", "prod"], writes=["prod"])
                P.dve(lambda e: e.reduce_sum(out=qn_, in_=prod3, axis=AX.X), reads=["prod"] + MS, writes=MS)
                P.dve(lambda e: e.tensor_tensor(out=dqk, in0=Dg, in1=qk_, op=ALU.mult), reads=MS, writes=MS)
                P.dve(lambda e: e.tensor_tensor(out=den_, in0=scg, in1=qn_, op=ALU.mult), reads=MS, writes=MS)
                P.dve(lambda e: e.tensor_tensor(out=den_, in0=den_, in1=dqk, op=ALU.add), reads=MS, writes=MS)
                P.act(lambda e: e.activation(out=tmp4, in_=mnew, func=AF.Exp, scale=-1.0), reads=MS, writes=MS)
                P.act(lambda e: e.activation(out=den_, in_=den_, func=AF.Abs), reads=MS, writes=MS)
                P.dve(lambda e: e.tensor_tensor(out=den_, in0=den_, in1=tmp4, op=ALU.max), reads=MS, writes=MS)
                P.dve(lambda e: e.reciprocal(out=den_, in_=den_), reads=MS, writes=MS)
                P.dve(lambda e: e.tensor_tensor(out=nn3, in0=n3, in1=scg.unsqueeze(2).to_broadcast([16, 4, 128]), op=ALU.mult), reads=["n_in"] + MS, writes=["nn"])
                P.dve(lambda e: e.tensor_tensor(out=kv3[:, :, 0:128], in0=k3, in1=Dg.unsqueeze(2).to_broadcast([16, 4, 128]), op=ALU.mult), reads=["qkv_s"] + MS, writes=["kv"])
                P.dve(lambda e: e.tensor_copy(out=kv3[:, :, 128:384], in_=v3), reads=["qkv_s"], writes=["kv"])
                P.dve(lambda e: e.tensor_tensor(out=nn3, in0=nn3, in1=kv3[:, :, 0:128], op=ALU.add), reads=["nn", "kv"], writes=["nn"])
                P.dma("sp", "o_sn", o_sn[:, :], nn[0:16, :], reads=["nn"])
                pt, pk = newps()
                for h in range(4):
                    P.pe(lambda e, h=h, pt=pt: e.transpose(out=pt[:, h * 16:(h + 1) * 16], in_=qkv_s[0:16, h, 0:128], identity=eye16), reads=["qkv_s", "ident32"], writes=[pk])
                P.dve(lambda e, pt=pt: e.tensor_copy(out=qTs, in_=pt[:, 0:64]), reads=[pk], writes=["qTs"])
                qTs3 = qTs.rearrange("p (h s) -> p h s", h=4)
                Qm4 = Qm.rearrange("p (h s c) -> p h s c", h=4, s=16)
                eyebc3 = eyebc[:].rearrange("p (s c) -> p s c", s=16)
                for h in range(4):
                    P.dve(lambda e, h=h: e.tensor_tensor(out=Qm4[:, h], in0=qTs3[:, h, :].unsqueeze(2).to_broadcast([128, 16, 16]), in1=eyebc3, op=ALU.mult), reads=["qTs", "eyebc"], writes=["Qm"])
                ssdecd3 = sdecd[0:16, :].rearrange("p (s h) -> p s h", s=16)
                P.dve(lambda e: e.tensor_tensor(out=ssdecd3, in0=eye16.unsqueeze(2).to_broadcast([16, 16, 4]), in1=scg.unsqueeze(1).to_broadcast([16, 16, 4]), op=ALU.mult), reads=["ident32"] + MS, writes=["sdecd"])
                pt, pk = newps()
                P.pe(lambda e, pt=pt: e.matmul(pt[:, 0:64], lhsT=ones32[0:16, :], rhs=sdecd[0:16, :], start=True, stop=True), reads=["ones32", "sdecd"], writes=[pk])
                P.dve(lambda e, pt=pt: e.tensor_copy(out=decb, in_=pt[:, 0:64]), reads=[pk], writes=["decb"])
                ps_state["pool"] = [0, 1, 2, 3]; ps_state["i"] = 0
                for s_ in range(16):
                    ci = Cin[s_ % 2].rearrange("p (h v) -> p h v", h=4); cik = "Cin%d" % (s_ % 2)
                    co = Cout[s_ % 2].rearrange("p (h v) -> p h v", h=4); cok = "Cout%d" % (s_ % 2)
                    P.dma("sp", cik, ci, st_C[s_].rearrange("h d v -> d h v"), writes=[cik])
                    kwm = prod[0:16, :].rearrange("p (h d) -> p h d", h=4)
                    P.dve(lambda e, s_=s_, kwm=kwm: e.tensor_scalar(out=kwm, in0=kv3[:, :, 0:128], scalar1=ident32[0:16, s_:s_ + 1], scalar2=None, op0=ALU.mult), reads=["kv", "ident32"], writes=["prod"])
                    for h in range(4):
                        P.pe(lambda e, h=h, s_=s_, ci=ci: e.matmul(PS[4 + h][0:16, 0:256], lhsT=Qm4[:, h, s_, :], rhs=ci[:, h, :], start=(s_ == 0), stop=(s_ == 15)), reads=["Qm", cik], writes=["ps%d" % (4 + h)])
                        pt2, pk2 = newps()
                        P.pe(lambda e, pt2=pt2, h=h, kwm=kwm: e.matmul(pt2[:, 0:256], lhsT=kwm[:, h, :], rhs=kv3[:, h, 128:384], start=True, stop=True), reads=["prod", "kv"], writes=[pk2])
                        P.dve(lambda e, h=h, s_=s_, ci=ci, co=co, pt2=pt2: e.scalar_tensor_tensor(out=co[:, h, :], in0=ci[:, h, :], scalar=decb[:, s_ * 4 + h:s_ * 4 + h + 1], in1=pt2[:, 0:256], op0=ALU.mult, op1=ALU.add),
                              reads=[cik, "decb", pk2], writes=[cok])
                    P.dma("sp", cok, o_sC[s_].rearrange("h d v -> d h v"), co, reads=[cok])
                num3 = num[0:16, :].rearrange("p (h v) -> p h v", h=4)
                for h in range(4):
                    P.dve(lambda e, h=h: e.tensor_scalar(out=num3[:, h, :], in0=PS[4 + h][0:16, 0:256], scalar1=scg[:, h:h + 1], scalar2=None, op0=ALU.mult), reads=["ps%d" % (4 + h)] + MS, writes=["num"])
                    P.dve(lambda e, h=h: e.scalar_tensor_tensor(out=num3[:, h, :], in0=v3[:, h, :], scalar=dqk[:, h:h + 1], in1=num3[:, h, :], op0=ALU.mult, op1=ALU.add), reads=["qkv_s", "num"] + MS, writes=["num"])
                    P.act(lambda e, h=h: e.activation(out=junk16[0:16, :], in_=num3[:, h, :], func=AF.Square, scale=den_[:, h:h + 1], accum_out=ss4[:, h:h + 1]), reads=["num"] + MS, writes=["junk16"] + MS)
                P.act(lambda e: e.activation(out=ss4, in_=ss4, func=AF.Sqrt, bias=EPS, scale=1.0 / 256), reads=MS, writes=MS)
                P.dve(lambda e: e.reciprocal(out=ss4, in_=ss4), reads=MS, writes=MS)
                P.dve(lambda e: e.tensor_tensor(out=tot4, in0=ss4, in1=den_, op=ALU.mult), reads=MS, writes=MS)
                for h in range(4):
                    P.dve(lambda e, h=h: e.scalar_tensor_tensor(out=yml[0:16, h * 256:(h + 1) * 256], in0=num3[:, h, :], scalar=tot4[:, h:h + 1], in1=gml_bc[0:16, h * 256:(h + 1) * 256], op0=ALU.mult, op1=ALU.mult),
                          reads=["num", "gml_bc"] + MS, writes=["yml"])
                P.dve(lambda e: e.tensor_tensor(out=yml[0:16, :], in0=yml[0:16, :], in1=og_s[0:16, :], op=ALU.mult), reads=["yml", "og_s"], writes=["yml"])
                ps_state["pool"] = list(range(8)); ps_state["i"] = 0
                pt, pk = newps()
                for c in range(8):
                    P.pe(lambda e, c=c, pt=pt: e.transpose(out=pt[:, c * 16:(c + 1) * 16], in_=yml[0:16, c * 128:(c + 1) * 128], identity=eye16), reads=["yml", "ident32"], writes=[pk])
                P.dve(lambda e, pt=pt: e.tensor_copy(out=Ys[:, 8:16, :], in_=pt[:, 0:128].rearrange("p (c t) -> p c t", c=8)), reads=[pk], writes=["Ys"])
                P.barrier(bar_tiles)
                P.dve(lambda e: e.memset(bart[:, 3:4], 0.0), writes=["sh_tm", "sz_tm", "nn", "osm", "Cout0", "Cout1", "bart3"])
            _stop("S")

            _stop("C")
            P.barrier(bar_tiles)
            for t in range(8):
                P.dma("sp", "xres%d" % t, xres[:, t, :], xo[t * 128:(t + 1) * 128, :], writes=["xres%d" % t, "pC_sb"])

            def proj_residual(wmat, Xk, xkey):
                for ng in range(4):
                    slot, rk = ring_unit([(lambda s: s.rearrange("p (k n) -> p k n", k=16), wmat[:, ng * 512:(ng + 1) * 512].rearrange("(k p) n -> p k n", p=128))])
                    wv = slot.rearrange("p (k n) -> p k n", k=16)
                    for t in range(8):
                        pt, pk = newps()
                        for kc in range(16):
                            P.pe(lambda e, kc=kc, pt=pt, t=t, wv=wv: e.matmul(pt[:], lhsT=Xk[:, kc, t * 128:(t + 1) * 128], rhs=wv[:, kc, :], start=(kc == 0), stop=(kc == 15)),
                                 reads=[rk, xkey], writes=[pk])
                        P.dve(lambda e, pt=pt, t=t, ng=ng: e.tensor_tensor(out=xres[:, t, ng * 512:(ng + 1) * 512], in0=xres[:, t, ng * 512:(ng + 1) * 512], in1=pt[:], op=ALU.add),
                              reads=[pk, "xres%d" % t], writes=["xres%d" % t])
                    if do_sample and _on("D"):
                        pt, pk = newps()
                        for j in range(4):
                            for kc in range(16):
                                P.pe(lambda e, kc=kc, pt=pt, j=j, wv=wv: e.matmul(pt[:, j * 16:(j + 1) * 16], lhsT=wv[:, kc, j * 128:(j + 1) * 128], rhs=Ys[:, kc, :], start=(kc == 0), stop=(kc == 15)),
                                     reads=[rk, "Ys"], writes=[pk])
                        P.dve(lambda e, pt=pt, ng=ng: e.tensor_tensor(out=xrs[:, ng * 4:(ng + 1) * 4, :], in0=xrs[:, ng * 4:(ng + 1) * 4, :], in1=pt[:, 0:64].rearrange("p (a b) -> p a b", a=4), op=ALU.add),
                              reads=[pk, "xrs"], writes=["xrs"])

            proj_residual(w_out, B, "B")

            _stop("D")
            SCR.reset()
            gbc = SCR.f32(2048)
            xn_tm = SCR.bf16(2048)
            junk = xn_tm
            _w32 = wgt[:].rearrange("p a b c -> p (a b c)").bitcast(F32)
            stg_o = [_w32[:, 0:512], _w32[:, 512:1024]]
            load_gbc(g_mem, gbc, "gbc")
            memnT = A[:, :, 0:256]
            mstage = B[:, 0:4, :].rearrange("p a b -> p (a b)").bitcast(F32)
            for mt in range(2):
                P.dma("sp", "mstage", mstage, mem[mt * 128:(mt + 1) * 128, :], reads=[], writes=["B"])
                norm_tile(mstage, "B", gbc, "gbc", memnT[:, :, mt * 128:(mt + 1) * 128], "A", junk, xn_tm, small[:, 0:1], "ss0")
            ring_state["slots"] = [0, 1]; ring_state["n"] = 0
            mkT = RINGt[:, 2, 0:4096].rearrange("p (a b) -> p a b", a=16)
            mvb = RINGt[:, 2, 4096:8192].rearrange("p (a b) -> p a b", a=2)
            ostage_i = [0]

            def out_stage(pt, pk, dst_dram):
                i = ostage_i[0] % 2
                ostage_i[0] += 1
                sg = stg_o[i]
                key = "stg_o%d" % i
                P.act(lambda e, pt=pt, sg=sg: e.copy(out=sg, in_=pt[:]), reads=[pk], writes=[key])
                P.dma("sp", key, dst_dram, sg, reads=[key])

            for which, wmat in ((0, w_xa_k), (1, w_xa_v)):
                for ng in range(4):
                    slot, rk = ring_unit([(lambda s: s.rearrange("p (k n) -> p k n", k=16), wmat[:, ng * 512:(ng + 1) * 512].rearrange("(k p) n -> p k n", p=128))])
                    wv = slot.rearrange("p (k n) -> p k n", k=16)
                    if which == 0:
                        for dc in range(4):
                            pt, pk = newps()
                            for kc in range(16):
                                P.pe(lambda e, kc=kc, pt=pt, dc=dc, wv=wv: e.matmul(pt[:, 0:256], lhsT=wv[:, kc, dc * 128:(dc + 1) * 128], rhs=memnT[:, kc, :], start=(kc == 0), stop=(kc == 15)),
                                     reads=[rk, "A"], writes=[pk])
                            P.act(lambda e, pt=pt, ng=ng, dc=dc: e.copy(out=mkT[:, ng * 4 + dc, :], in_=pt[:, 0:256]), reads=[pk], writes=["ring2"])
                    for mt in range(2):
                        pt, pk = newps()
                        for kc in range(16):
                            P.pe(lambda e, kc=kc, pt=pt, mt=mt, wv=wv: e.matmul(pt[:], lhsT=memnT[:, kc, mt * 128:(mt + 1) * 128], rhs=wv[:, kc, :], start=(kc == 0), stop=(kc == 15)),
                                 reads=[rk, "A"], writes=[pk])
                        if which == 1:
                            P.dve(lambda e, pt=pt, mt=mt, ng=ng: e.tensor_copy(out=mvb[:, mt, ng * 512:(ng + 1) * 512], in_=pt[:]), reads=[pk], writes=["ring2"])
                        out_stage(pt, pk, (o_mk if which == 0 else o_mv)[mt * 128:(mt + 1) * 128, ng * 512:(ng + 1) * 512])
            load_gbc(g_xa, gbc, "gbc")
            for t in range(8):
                norm_tile(xres[:, t, :], "xres%d" % t, gbc, "gbc", A[:, :, t * 128:(t + 1) * 128], "A", junk, xn_tm, small[:, 0:1], "ss0")
            XS = 512 ** -0.5
            if do_sample:
                sq3_e = SCR.f32(16, 16)
                xq_s = SCR.bf16(2048)
                norm_fm(16, sq3_e)
            for ng in range(4):
                slot, rk = ring_unit([(lambda s: s.rearrange("p (k n) -> p k n", k=16), w_xa_q[:, ng * 512:(ng + 1) * 512].rearrange("(k p) n -> p k n", p=128))])
                wv = slot.rearrange("p (k n) -> p k n", k=16)
                for dc in range(4):
                    for tg in range(2):
                        pt, pk = newps()
                        for kc in range(16):
                            P.pe(lambda e, kc=kc, pt=pt, dc=dc, tg=tg, wv=wv: e.matmul(pt[:], lhsT=wv[:, kc, dc * 128:(dc + 1) * 128], rhs=A[:, kc, tg * 512:(tg + 1) * 512], start=(kc == 0), stop=(kc == 15)),
                                 reads=[rk, "A"], writes=[pk])
                        if tg == 0:
                            P.act(lambda e, pt=pt, ng=ng, dc=dc, tg=tg: e.activation(out=B[:, ng * 4 + dc, tg * 512:(tg + 1) * 512], in_=pt[:], func=AF.Identity, scale=XS), reads=[pk], writes=["B"])
                        else:
                            P.dve(lambda e, pt=pt, ng=ng, dc=dc, tg=tg: e.tensor_scalar(out=B[:, ng * 4 + dc, tg * 512:(tg + 1) * 512], in0=pt[:], scalar1=XS, scalar2=None, op0=ALU.mult), reads=[pk], writes=["B"])
                if do_sample and _on("Q"):
                    pt, pk = newps()
                    for kc in range(16):
                        P.pe(lambda e, kc=kc, pt=pt, wv=wv: e.matmul(pt[0:16, :], lhsT=Xs[:, kc, :], rhs=wv[:, kc, :], start=(kc == 0), stop=(kc == 15)), reads=[rk, "Xs"], writes=[pk])
                    P.act(lambda e, pt=pt, ng=ng: e.activation(out=xq_s[0:16, ng * 512:(ng + 1) * 512], in_=pt[0:16, :], func=AF.Identity, scale=XS), reads=[pk], writes=["xq_s"])
            _stop("E1")
            pfp = stg_o[0][:, 0:256]; pbf = stg_o[0][:, 256:384].bitcast(BF16)
            pT = stg_o[1].bitcast(BF16).rearrange("p (a b) -> p a b", a=2)
            mx = small[:, 12:13]; rs = small[:, 13:14]
            for tg in range(2):
                for hd in range(4):
                    for t4 in range(4):
                        t = tg * 4 + t4
                        ts_ = slice(t * 128, (t + 1) * 128)
                        pt, pk = newps()
                        for dc in range(4):
                            P.pe(lambda e, pt=pt, dc=dc, hd=hd, ts_=ts_: e.matmul(pt[:, 0:256], lhsT=B[:, hd * 4 + dc, ts_], rhs=mkT[:, hd * 4 + dc, :], start=(dc == 0), stop=(dc == 3)),
                                 reads=["B", "ring2"], writes=[pk])
                        P.dve(lambda e, pt=pt: e.reduce_max(out=mx, in_=pt[:, 0:256], axis=AX.X), reads=[pk], writes=["mx"])
                        P.dve(lambda e: e.tensor_scalar(out=mx, in0=mx, scalar1=-1.0, scalar2=None, op0=ALU.mult), reads=["mx"], writes=["mx"])
                        P.act(lambda e, pt=pt: e.activation(out=pfp, in_=pt[:, 0:256], func=AF.Exp, bias=mx, scale=1.0, accum_out=rs), reads=[pk, "mx"], writes=["stg_o0", "rs"])
                        P.dve(lambda e: e.reciprocal(out=rs, in_=rs), reads=["rs"], writes=["rs"])
                        P.dve(lambda e: e.tensor_scalar(out=pbf, in0=pfp, scalar1=rs, scalar2=None, op0=ALU.mult), reads=["stg_o0", "rs"], writes=["stg_o0"])
                        ptr, ptk = newps()
                        ptrb = ptr[:].bitcast(BF16)
                        for mt in range(2):
                            P.pe(lambda e, mt=mt, ptrb=ptrb: e.transpose(out=ptrb[:, mt * 128:(mt + 1) * 128], in_=pbf[:, mt * 128:(mt + 1) * 128], identity=identb[:]), reads=["stg_o0", "identb"], writes=[ptk])
                        P.act(lambda e, ptrb=ptrb, t4=t4: e.copy(out=pT[:, :, t4 * 128:(t4 + 1) * 128], in_=ptrb[:, 0:256].rearrange("p (m t) -> p m t", m=2)), reads=[ptk], writes=["stg_o1"])
                    for dc in range(4):
                        pt, pk = newps()
                        for mt in range(2):
                            P.pe(lambda e, pt=pt, mt=mt, hd=hd, dc=dc: e.matmul(pt[:], lhsT=mvb[:, mt, hd * 512 + dc * 128:hd * 512 + (dc + 1) * 128], rhs=pT[:, mt, :], start=(mt == 0), stop=(mt == 1)),
                                 reads=["ring2", "stg_o1"], writes=[pk])
                        if dc % 2 == 0:
                            P.act(lambda e, pt=pt, hd=hd, dc=dc, tg=tg: e.copy(out=A[:, hd * 4 + dc, tg * 512:(tg + 1) * 512], in_=pt[:]), reads=[pk], writes=["A"])
                        else:
                            P.dve(lambda e, pt=pt, hd=hd, dc=dc, tg=tg: e.tensor_copy(out=A[:, hd * 4 + dc, tg * 512:(tg + 1) * 512], in_=pt[:]), reads=[pk], writes=["A"])
            if do_sample and _on("T"):
                P.barrier(bar_tiles)
                Bf = B[:].rearrange("p a b -> p (a b)")
                Kb = [Bf[:, 0:4096].bitcast(F32), Bf[:, 4096:8192].bitcast(F32)]
                Vb = [Bf[:, 8192:10240], Bf[:, 10240:12288]]
                xm = Bf[:, 12288:14336]
                sc_all = Bf[:, 14336:14592].bitcast(F32)
                junkK = Bf[:, 14592:15616].bitcast(F32)
                scT = Bf[:, 15616:16128].bitcast(F32)
                pTs = Bf[:, 16128:16384].bitcast(F32)
                Pm = A[:, 0:2, :].rearrange("p a b -> p (a b)")
                for s_ in range(16):
                    P.dve(lambda e, s_=s_: e.tensor_scalar(out=xm[0:16, :], in0=xq_s[0:16, :], scalar1=ident32[0:16, s_:s_ + 1], scalar2=None, op0=ALU.mult), reads=["xq_s", "ident32"], writes=["xm"])
                    qb = []
                    for h in range(4):
                        pt, pk = newps()
                        P.pe(lambda e, pt=pt, h=h: e.matmul(pt[:], lhsT=onesb[0:16, :], rhs=xm[0:16, h * 512:(h + 1) * 512], start=True, stop=True), reads=["onesb", "xm"], writes=[pk])
                        qb.append((pt, pk))
                    for mt in range(2):
                        i_ = (s_ * 2 + mt) % 2
                        P.dma("sp", "Kb%d" % i_, Kb[i_], ck[s_, mt * 128:(mt + 1) * 128, :], writes=["Kb%d" % i_])
                        for h in range(4):
                            col = mt * 64 + s_ * 4 + h
                            P.dve(lambda e, i_=i_, h=h, col=col, pt=qb[h][0]: e.scalar_tensor_tensor(out=junkK, in0=Kb[i_][:, h * 512:(h + 1) * 512], scalar=1.0, in1=pt[:], op0=ALU.mult, op1=ALU.mult, accum_out=sc_all[:, col:col + 1]),
                                  reads=["Kb%d" % i_, qb[h][1]], writes=["junkK", "sc_all"])
                pt, pk = newps()
                for mt in range(2):
                    P.pe(lambda e, mt=mt, pt=pt: e.transpose(out=pt[0:64, mt * 128:(mt + 1) * 128], in_=sc_all[:, mt * 64:(mt + 1) * 64], identity=ident32[:]), reads=["sc_all", "ident32"], writes=[pk])
                mxs = small[0:64, 20:21]; rss = small[0:64, 21:22]
                P.dve(lambda e, pt=pt: e.reduce_max(out=mxs, in_=pt[0:64, 0:256], axis=AX.X), reads=[pk], writes=["mxs"])
                P.dve(lambda e: e.tensor_scalar(out=mxs, in0=mxs, scalar1=-1.0, scalar2=None, op0=ALU.mult), reads=["mxs"], writes=["mxs"])
                P.act(lambda e, pt=pt: e.activation(out=scT[0:64, :], in_=pt[0:64, 0:256], func=AF.Exp, bias=mxs, scale=1.0, accum_out=rss), reads=[pk, "mxs"], writes=["scT", "rss"])
                P.dve(lambda e: e.reciprocal(out=rss, in_=rss), reads=["rss"], writes=["rss"])
                P.dve(lambda e: e.tensor_scalar(out=scT[0:64, :], in0=scT[0:64, :], scalar1=rss, scalar2=None, op0=ALU.mult), reads=["scT", "rss"], writes=["scT"])
                pt, pk = newps()
                for mt in range(2):
                    P.pe(lambda e, mt=mt, pt=pt: e.transpose(out=pt[:, mt * 64:(mt + 1) * 64], in_=scT[0:64, mt * 128:(mt + 1) * 128], identity=ident32[0:64, 0:64]), reads=["scT", "ident32"], writes=[pk])
                P.dve(lambda e, pt=pt: e.tensor_copy(out=pTs, in_=pt[:, 0:128]), reads=[pk], writes=["pTs"])
                Pm5 = Kb[0].bitcast(BF16)[:, 0:2048].rearrange("p (mt h s c) -> p mt h s c", mt=2, h=4, s=16)
                pTs4 = pTs.rearrange("p (mt s h) -> p mt s h", mt=2, s=16)
                eyebc3b = eyebc[:].rearrange("p (s c) -> p s c", s=16)
                for mt in range(2):
                    for h in range(4):
                        P.dve(lambda e, mt=mt, h=h: e.tensor_tensor(out=Pm5[:, mt, h], in0=pTs4[:, mt, :, h].unsqueeze(2).to_broadcast([128, 16, 16]), in1=eyebc3b, op=ALU.mult),
                              reads=["pTs", "eyebc", "Kb0"], writes=["Kb0"])
                ps_state["pool"] = [0, 1, 2, 3]; ps_state["i"] = 0
                for s_ in range(16):
                    for mt in range(2):
                        i_ = (s_ * 2 + mt) % 2
                        P.dma("pool", "Vb%d" % i_, Vb[i_], cv[s_, mt * 128:(mt + 1) * 128, :], writes=["Vb%d" % i_] + (["B"] if s_ == 0 else []))
                        for h in range(4):
                            P.pe(lambda e, s_=s_, mt=mt, h=h, i_=i_: e.matmul(PS[4 + h][0:16, :], lhsT=Pm5[:, mt, h, s_, :], rhs=Vb[i_][:, h * 512:(h + 1) * 512], start=(s_ == 0 and mt == 0), stop=(s_ == 15 and mt == 1)),
                                 reads=["Kb0", "Vb%d" % i_], writes=["ps%d" % (4 + h)])
                os_tm = Kb[1].bitcast(BF16)[:, 0:2048]
                for h in range(4):
                    P.act(lambda e, h=h: e.copy(out=os_tm[0:16, h * 512:(h + 1) * 512], in_=PS[4 + h][0:16, :]), reads=["ps%d" % (4 + h)], writes=["Kb1"])
                ps_state["pool"] = list(range(8)); ps_state["i"] = 0
                pt, pk = newps()
                ptb = pt[:].bitcast(BF16)
                for kc in range(16):
                    P.pe(lambda e, kc=kc, ptb=ptb: e.transpose(out=ptb[:, kc * 16:(kc + 1) * 16], in_=os_tm[0:16, kc * 128:(kc + 1) * 128], identity=identb[0:16, 0:16]), reads=["Kb1", "identb"], writes=[pk])
                P.dve(lambda e, ptb=ptb: e.tensor_copy(out=Ys[:].rearrange("p a b -> p (a b)"), in_=ptb[:, 0:256]), reads=[pk], writes=["Ys"])
                P.barrier(bar_tiles)
            proj_residual(w_xa_o, A, "A")

            _stop("E")
            ring_state["slots"] = [0, 1, 2]; ring_state["n"] = 2
            load_gbc(g_ffn, gbc, "gbc")
            for t in range(8):
                norm_tile(xres[:, t, :], "xres%d" % t, gbc, "gbc", B[:, :, t * 128:(t + 1) * 128], "B", junk, xn_tm, small[:, 0:1], "ss0")
            if do_sample:
                SCR.reset()
                gbc = SCR.f32(2048); xn_tm = SCR.bf16(2048); junk = xn_tm
                sq3_f = SCR.f32(16, 16)
                sgs = SCR.f32(64); hs = SCR.bf16(4, 16)
                norm_fm(32, sq3_f)
            hT = [A[:, 0:4, :], A[:, 4:8, :]]
            sg = A[:, 8:10, :].rearrange("p a b -> p (a b)").bitcast(F32)
            for grp in range(11):
                cs0 = grp * 512
                slg, rkg = ring_unit([(lambda s: s.rearrange("p (k n) -> p k n", k=16), w_g[:, cs0:cs0 + 512].rearrange("(k p) n -> p k n", p=128))])
                slu, rku = ring_unit([(lambda s: s.rearrange("p (k n) -> p k n", k=16), w_u[:, cs0:cs0 + 512].rearrange("(k p) n -> p k n", p=128))])
                sld, rkd = ring_unit([(lambda s: s.rearrange("p (c n) -> p c n", c=4), w_d[cs0:cs0 + 512, :].rearrange("(c p) n -> p c n", p=128))])
                wgv = slg.rearrange("p (k n) -> p k n", k=16); wuv = slu.rearrange("p (k n) -> p k n", k=16); wdv = sld.rearrange("p (c n) -> p c n", c=4)
                hb = hT[grp % 2]; hk = "hT%d" % (grp % 2)
                for dc in range(4):
                    for tg in range(2):
                        pg, pgk = newps()
                        for kc in range(16):
                            P.pe(lambda e, kc=kc, pg=pg, dc=dc, tg=tg, wgv=wgv: e.matmul(pg[:], lhsT=wgv[:, kc, dc * 128:(dc + 1) * 128], rhs=B[:, kc, tg * 512:(tg + 1) * 512], start=(kc == 0), stop=(kc == 15)),
                                 reads=[rkg, "B"], writes=[pgk])
                        pu, puk = newps()
                        for kc in range(16):
                            P.pe(lambda e, kc=kc, pu=pu, dc=dc, tg=tg, wuv=wuv: e.matmul(pu[:], lhsT=wuv[:, kc, dc * 128:(dc + 1) * 128], rhs=B[:, kc, tg * 512:(tg + 1) * 512], start=(kc == 0), stop=(kc == 15)),
                                 reads=[rku, "B"], writes=[puk])
                        sgi = (dc * 2 + tg) % 2
                        sgv = sg[:, sgi * 512:(sgi + 1) * 512]
                        P.act(lambda e, pg=pg, sgv=sgv: e.activation(out=sgv, in_=pg[:], func=AF.Silu), reads=[pgk], writes=["sg%d" % sgi] + (["A"] if grp == 0 and dc == 0 else []))
                        P.dve(lambda e, pu=pu, sgv=sgv, hb=hb, dc=dc, tg=tg: e.tensor_tensor(out=hb[:, dc, tg * 512:(tg + 1) * 512], in0=sgv, in1=pu[:], op=ALU.mult), reads=["sg%d" % sgi, puk], writes=[hk] + (["A"] if grp < 2 and dc == 0 and tg == 0 else []))
                if do_sample and _on("F"):
                    pt, pk = newps()
                    for gu, wv_, rk_ in ((0, wgv, rkg), (1, wuv, rku)):
                        for dc in range(4):
                            for kc in range(16):
                                P.pe(lambda e, kc=kc, pt=pt, gu=gu, dc=dc, wv_=wv_: e.matmul(pt[:, (gu * 4 + dc) * 16:(gu * 4 + dc + 1) * 16], lhsT=wv_[:, kc, dc * 128:(dc + 1) * 128], rhs=Xs[:, kc, :], start=(kc == 0), stop=(kc == 15)),
                                     reads=[rk_, "Xs"], writes=[pk])
                    P.act(lambda e, pt=pt: e.activation(out=sgs, in_=pt[:, 0:64], func=AF.Silu), reads=[pk], writes=["sgs"])
                    P.dve(lambda e, pt=pt: e.tensor_tensor(out=hs[:].rearrange("p a b -> p (a b)"), in0=sgs, in1=pt[:, 64:128], op=ALU.mult), reads=["sgs", pk], writes=["hs"])
                    pt2, pk2 = newps()
                    for f in range(16):
                        for dc in range(4):
                            P.pe(lambda e, pt2=pt2, f=f, dc=dc, wdv=wdv: e.matmul(pt2[:, f * 16:(f + 1) * 16], lhsT=wdv[:, dc, f * 128:(f + 1) * 128], rhs=hs[:, dc, :], start=(dc == 0), stop=(dc == 3)),
                                 reads=[rkd, "hs"], writes=[pk2])
                    P.dve(lambda e, pt2=pt2: e.tensor_tensor(out=xrs2, in0=xrs2, in1=pt2[:, 0:256], op=ALU.add), reads=[pk2, "xrs"], writes=["xrs"])
                for t in range(8):
                    for ng in range(4):
                        pt, pk = newps()
                        for dc in range(4):
                            P.pe(lambda e, pt=pt, dc=dc, t=t, ng=ng, hb=hb, wdv=wdv: e.matmul(pt[:], lhsT=hb[:, dc, t * 128:(t + 1) * 128], rhs=wdv[:, dc, ng * 512:(ng + 1) * 512], start=(dc == 0), stop=(dc == 3)),
                                 reads=[hk, rkd], writes=[pk])
                        P.dve(lambda e, pt=pt, t=t, ng=ng: e.tensor_tensor(out=xres[:, t, ng * 512:(ng + 1) * 512], in0=xres[:, t, ng * 512:(ng + 1) * 512], in1=pt[:], op=ALU.add),
                              reads=[pk, "xres%d" % t], writes=["xres%d" % t])

            _stop("F")
            load_gbc(g_final, gbc, "gbc")
            ystg = [B[:, 0:4, :].rearrange("p a b -> p (a b)").bitcast(F32), B[:, 4:8, :].rearrange("p a b -> p (a b)").bitcast(F32)]
            for t in range(8):
                ssx = small[:, 16 + (t % 2):17 + (t % 2)]
                sk = "ssx%d" % (t % 2)
                yb = ystg[t % 2]
                yk = "ystg%d" % (t % 2)
                P.act(lambda e, t=t, ssx=ssx: e.activation(out=junk, in_=xres[:, t, :], func=AF.Square, accum_out=ssx), reads=["xres%d" % t], writes=["xn_tm", sk])
                rstd_from_ss(ssx, D, sk)
                P.dve(lambda e, t=t, ssx=ssx, yb=yb: e.scalar_tensor_tensor(out=yb, in0=xres[:, t, :], scalar=ssx, in1=gbc, op0=ALU.mult, op1=ALU.mult),
                      reads=["xres%d" % t, sk, "gbc"], writes=[yk, "B"])
                P.dma("sp", yk, yo[t * 128:(t + 1) * 128, :], yb, reads=[yk])


            if do_sample and _on("G"):
                sq3_g = SCR.f32(16, 16)
                yfm = SCR.f32(16, 16)
                norm_fm(48, sq3_g)
                P.dve(lambda e: e.tensor_tensor(out=yfm, in0=sq3_g, in1=gfm[:, 48:64].unsqueeze(2).to_broadcast([128, 16, 16]), op=ALU.mult), reads=["sq3", "gfm", "Xs"], writes=["yfm"])
                ys_tm = B[:, 8:12, :].rearrange("p a b -> p (a b)").bitcast(F32)
                for q4 in range(4):
                    pt, pk = newps()
                    for cc in range(4):
                        kc = q4 * 4 + cc
                        P.pe(lambda e, pt=pt, cc=cc, kc=kc: e.transpose(out=pt[0:16, cc * 128:(cc + 1) * 128], in_=yfm[:, kc, :], identity=ident32[:]), reads=["yfm", "ident32"], writes=[pk])
                    P.act(lambda e, pt=pt, q4=q4: e.copy(out=ys_tm[0:16, q4 * 512:(q4 + 1) * 512], in_=pt[0:16, :]), reads=[pk], writes=["ys_tm", "B"])
                P.dma("sp", "ys_tm", ys[:, :], ys_tm[0:16, :], reads=["ys_tm"])
        except _Stop:
            pass
        P.emit()
        print("ops", len(P.ops), "waits", P.n_waits, flush=True)
    return nc


_CONST = {}


def _consts():
    if not _CONST:
        _CONST["ident"] = np.eye(128, dtype=np.float32)
        _CONST["causal"] = np.triu(np.ones((128, 128), dtype=np.float32))
        _CONST["eyebc"] = np.ascontiguousarray(np.broadcast_to(np.eye(16, dtype=np.float32).reshape(1, 256), (128, 256)))
    return _CONST


def make_in_maps(inp):
    f = lambda a: np.ascontiguousarray(np.asarray(a, dtype=np.float32))
    cst = _consts()
    shared = {}
    for k in ["g_mix", "w_in", "conv_w", "conv_b", "w_rg_a", "b_rg_a", "w_rg_x", "b_rg_x", "rg_lambda", "b_ml_i", "b_ml_f",
              "g_rg_out", "g_ml_out", "w_out", "g_xa", "g_mem", "w_xa_q", "w_xa_k", "w_xa_v", "w_xa_o", "g_ffn",
              "w_ffn_gate", "w_ffn_up", "w_ffn_down"]:
        shared[k] = f(np.asarray(inp[k])[0])
    shared["g_final"] = f(inp["g_final"])
    shared.update(cst)
    xpr = np.asarray(inp["x_prompt"]); xsm = np.asarray(inp["x_sample"]); memp = np.asarray(inp["mem_prompt"])
    maps = []
    for c in range(8):
        j, h = c // 2, c % 2
        m = dict(shared)
        m["xo"] = f(xpr[j, h * NT:(h + 1) * NT])
        m["xp"] = f(xpr[j, 0:NT])
        m["xs"] = f(xsm[c * NS:(c + 1) * NS, 0])
        m["mask"] = np.full((128, 1), float(h), dtype=np.float32)
        m["mem"] = f(memp[j])
        sl = slice(c * NS, (c + 1) * NS)
        m["st_h"] = f(np.asarray(inp["state_rg_h"])[0, sl])
        m["st_conv"] = f(np.asarray(inp["state_rg_conv"])[0, sl].reshape(NS, 3072))
        m["st_C"] = f(np.asarray(inp["state_ml_C"])[0, sl])
        m["st_n"] = f(np.asarray(inp["state_ml_n"])[0, sl].reshape(NS, 512))
        m["st_m"] = f(np.asarray(inp["state_ml_m"])[0, sl])
        m["ck"] = f(np.asarray(inp["cache_mem_k"])[0, sl].reshape(NS, 256, D))
        m["cv"] = f(np.asarray(inp["cache_mem_v"])[0, sl].reshape(NS, 256, D))
        maps.append(m)
    return maps


_NC = {}


def kernel(**inputs):
    if "nc" not in _NC:
        _NC["nc"] = build(do_sample=os.environ.get("KNOSAMPLE") is None)
    nc = _NC["nc"]
    maps = make_in_maps(inputs)
    res = run_bass_kernel_spmd(nc, maps, core_ids=list(range(8)))
    R = res.results
    y_prompt = np.zeros((4, 2048, D), np.float32)
    for c in range(8):
        j, h = c // 2, c % 2
        y_prompt[j, h * NT:(h + 1) * NT] = R[c]["yo"]
    y_sample = np.concatenate([R[c]["ys"] for c in range(8)], axis=0).reshape(128, 1, D)
    odd = [R[2 * j + 1] for j in range(4)]
    p_rg_h = np.stack([r["o_ph"].T.reshape(1024) for r in odd])[None]
    p_rg_conv = np.stack([r["o_pconv"].reshape(128, 8, 3).transpose(2, 1, 0).reshape(3, 1024) for r in odd])[None]
    pC = np.stack([r["o_pC"].reshape(128, 4, 257).transpose(1, 0, 2) for r in odd])
    p_ml_C = np.ascontiguousarray(pC[..., 0:256])[None]
    p_ml_n = np.ascontiguousarray(pC[..., 256])[None]
    p_ml_m = np.stack([r["o_pm"].reshape(4) for r in odd])[None]
    p_mem_k = np.stack([R[2 * j]["o_mk"].reshape(256, 4, 512) for j in range(4)])[None]
    p_mem_v = np.stack([R[2 * j]["o_mv"].reshape(256, 4, 512) for j in range(4)])[None]
    cat = lambda k, shp: np.concatenate([R[c][k] for c in range(8)], axis=0).reshape(shp)
    s_rg_h = cat("o_sh", (1, 128, 1024))
    s_rg_conv = cat("o_sconv", (1, 128, 3, 1024))
    s_ml_C = cat("o_sC", (1, 128, 4, 128, 256))
    s_ml_n = cat("o_sn", (1, 128, 4, 128))
    s_ml_m = cat("o_sm", (1, 128, 4))
    return (y_prompt, y_sample, p_rg_h, p_rg_conv, p_ml_C, p_ml_n, p_ml_m, p_mem_k, p_mem_v,
            s_rg_h, s_rg_conv, s_ml_C, s_ml_n, s_ml_m)
```
